# Optimizing a Trainium2 kernel written in Bass

```python
import math
import jax, jax.numpy as jnp
from jax import lax
import numpy as np

D_MODEL = 1024
BATCH = 1
SEQ = 16384
DEPTH = 2
DEC_BATCH = 16
DEC_SEQ = 4096
PAST_LEN = 128

GRID_W = 64
HEAD_DIM = 64
NA_HEADS = 8
DIFF_HEADS = 4
NA_WIDTH = NA_HEADS * HEAD_DIM
DIFF_QK_WIDTH = DIFF_HEADS * 2 * HEAD_DIM
DIFF_V_DIM = 2 * HEAD_DIM
DIFF_V_WIDTH = DIFF_HEADS * DIFF_V_DIM
IN_WIDTH = 3 * NA_WIDTH + 2 * DIFF_QK_WIDTH + DIFF_V_WIDTH + 2 * D_MODEL
D_FF = 256 * ((8 * D_MODEL // 3 + 255) // 256)
WIN_H = 8
WIN_W = 16
COL_QBLOCK = 16
COL_KBLOCK = 32
ROPE_THETA = 500000.0
ROPE_DIM = HEAD_DIM // 4
Q_BLOCK = 128
NORM_EPS = 1e-6
SUBLN_EPS = 1e-5
NEG_INF = -1e30
FFN_RES = 0.5

kernel_name = "hybrid_natten_diffattn_macaron_encoder"

F32 = jnp.float32


def rms_norm(x, g, eps=NORM_EPS):
    xf = x.astype(F32)
    y = xf * lax.rsqrt(jnp.mean(xf * xf, axis=-1, keepdims=True) + eps)
    return (y * g.astype(F32)).astype(x.dtype)


def swiglu_ffn(x, wi, wo):
    a, b = jnp.split(x @ wi, 2, axis=-1)
    return (jax.nn.silu(a) * b) @ wo


def partial_rope(x, positions):
    half = ROPE_DIM // 2
    inv_freq = jnp.power(ROPE_THETA, -jnp.arange(half, dtype=F32) * 2.0 / ROPE_DIM)
    ang = positions[:, None] * inv_freq[None, :]
    cos = jnp.cos(ang)[None, :, None, None, :].astype(x.dtype)
    sin = jnp.sin(ang)[None, :, None, None, :].astype(x.dtype)
    x1 = x[..., :half]
    x2 = x[..., half:ROPE_DIM]
    rest = x[..., ROPE_DIM:]
    return jnp.concatenate([x1 * cos - x2 * sin, x2 * cos + x1 * sin, rest], axis=-1)


def neighbourhood_attention(q, k, v, rpb):
    B, L, H, DH = q.shape
    rows = L // GRID_W
    kh = min(WIN_H, rows)
    n_cb = GRID_W // COL_QBLOCK
    qcol = np.arange(GRID_W).reshape(n_cb, COL_QBLOCK)
    kc0 = np.clip(np.arange(n_cb) * COL_QBLOCK - WIN_W // 2, 0, GRID_W - COL_KBLOCK)
    kcol = kc0[:, None] + np.arange(COL_KBLOCK)
    wstart = np.clip(qcol - WIN_W // 2, 0, GRID_W - WIN_W)
    col_mask = (kcol[:, None, :] >= wstart[:, :, None]) & (kcol[:, None, :] < wstart[:, :, None] + WIN_W)
    dc = np.clip(kcol[:, None, :] - qcol[:, :, None] + WIN_W - 1, 0, 2 * WIN_W - 2)
    bias_c = jnp.where(col_mask, rpb[:, :, dc].astype(F32), NEG_INF)
    scale = DH ** -0.5
    q_rows = q.reshape(B, rows, n_cb, COL_QBLOCK, H, DH).transpose(1, 0, 2, 3, 4, 5)
    kg = k.reshape(B, rows, GRID_W, H, DH)
    vg = v.reshape(B, rows, GRID_W, H, DH)

    def row_block(args):
        r, q_row = args
        s0 = jnp.clip(r - kh // 2, 0, rows - kh)
        k_rows = lax.dynamic_slice_in_dim(kg, s0, kh, axis=1)
        v_rows = lax.dynamic_slice_in_dim(vg, s0, kh, axis=1)
        k_blk = k_rows[:, :, kcol]
        v_blk = v_rows[:, :, kcol]
        dr = s0 + jnp.arange(kh) - r + WIN_H - 1
        bias = bias_c[:, dr].transpose(0, 2, 3, 1, 4)
        s = jnp.einsum('bjqhd,bijkhd->bhjqik', q_row, k_blk).astype(F32) * scale + bias
        p = jax.nn.softmax(s.reshape(s.shape[:4] + (kh * COL_KBLOCK,)), axis=-1).reshape(s.shape)
        o = jnp.einsum('bhjqik,bijkhd->bjqhd', p.astype(v.dtype), v_blk)
        return o.reshape(B, GRID_W, H, DH)

    out = lax.map(row_block, (jnp.arange(rows), q_rows))
    return out.transpose(1, 0, 2, 3, 4).reshape(B, L, H * DH)


def differential_attention(q, k, v, lam, lam_init, subln_g):
    B, L, H, _, DH = q.shape
    nb = L // Q_BLOCK
    scale = DH ** -0.5
    q_blocks = q.reshape(B, nb, Q_BLOCK, H, 2, DH).transpose(1, 0, 2, 3, 4, 5)

    def q_block(q_blk):
        s = jnp.einsum('bqhmd,bkhmd->bhmqk', q_blk, k).astype(F32) * scale
        p = jax.nn.softmax(s, axis=-1)
        a = p[:, :, 0] - lam * p[:, :, 1]
        return jnp.einsum('bhqk,bkhe->bqhe', a.astype(v.dtype), v)

    o = lax.map(q_block, q_blocks)
    o = o.transpose(1, 0, 2, 3, 4).reshape(B, L, H, 2 * DH)
    o = rms_norm(o, subln_g, SUBLN_EPS) * (1.0 - lam_init)
    return o.reshape(B, L, H * 2 * DH)


def encoder_layer(x, layer_idx, ffn1_norm, ffn1_wi, ffn1_wo, mix_norm, w_in, qa_norm, ka_norm, rpb,
                  qb_norm, kb_norm, lam_q1, lam_k1, lam_q2, lam_k2, subln, w_a_out, w_b_out, w_o,
                  ffn2_norm, ffn2_wi, ffn2_wo):
    B, L, _ = x.shape
    x = x + FFN_RES * swiglu_ffn(rms_norm(x, ffn1_norm), ffn1_wi, ffn1_wo)
    u = rms_norm(x, mix_norm)
    proj = u @ w_in
    splits = np.cumsum([NA_WIDTH] * 3 + [DIFF_QK_WIDTH] * 2 + [DIFF_V_WIDTH, D_MODEL]).tolist()
    qa, ka, va, qb, kb, vb, ga, gb = jnp.split(proj, splits, axis=-1)
    qa = rms_norm(qa.reshape(B, L, NA_HEADS, HEAD_DIM), qa_norm)
    ka = rms_norm(ka.reshape(B, L, NA_HEADS, HEAD_DIM), ka_norm)
    va = va.reshape(B, L, NA_HEADS, HEAD_DIM)
    ya = neighbourhood_attention(qa, ka, va, rpb) @ w_a_out
    pos = jnp.arange(L, dtype=F32)
    qb = partial_rope(rms_norm(qb.reshape(B, L, DIFF_HEADS, 2, HEAD_DIM), qb_norm), pos)
    kb = partial_rope(rms_norm(kb.reshape(B, L, DIFF_HEADS, 2, HEAD_DIM), kb_norm), pos)
    vb = vb.reshape(B, L, DIFF_HEADS, DIFF_V_DIM)
    lam_init = 0.8 - 0.6 * math.exp(-0.3 * layer_idx)
    lam = (jnp.exp(jnp.sum(lam_q1.astype(F32) * lam_k1.astype(F32)))
           - jnp.exp(jnp.sum(lam_q2.astype(F32) * lam_k2.astype(F32))) + lam_init)
    yb = differential_attention(qb, kb, vb, lam, lam_init, subln) @ w_b_out
    merged = jax.nn.sigmoid(ga) * ya + jax.nn.sigmoid(gb) * yb
    x = x + merged @ w_o
    x = x + FFN_RES * swiglu_ffn(rms_norm(x, ffn2_norm), ffn2_wi, ffn2_wo)
    return x


def setup_inputs(seed: int = 0) -> dict:
    key = jax.random.key(seed)
    ks = jax.random.split(key, 24)

    def nrm(k, shape, scale):
        return jax.random.normal(k, shape, F32) * scale

    def gain(k, shape):
        return 1.0 + 0.05 * jax.random.normal(k, shape, F32)

    return {
        "x_prompt": nrm(ks[0], (BATCH, SEQ, D_MODEL), 1.0),
        "x_sample": nrm(ks[1], (DEC_BATCH, DEC_SEQ, D_MODEL), 1.0),
        "ffn1_norm": gain(ks[2], (DEPTH, D_MODEL)),
        "ffn1_wi": nrm(ks[3], (DEPTH, D_MODEL, 2 * D_FF), D_MODEL ** -0.5),
        "ffn1_wo": nrm(ks[4], (DEPTH, D_FF, D_MODEL), D_FF ** -0.5),
        "mix_norm": gain(ks[5], (DEPTH, D_MODEL)),
        "w_in": nrm(ks[6], (DEPTH, D_MODEL, IN_WIDTH), D_MODEL ** -0.5),
        "qa_norm": gain(ks[7], (DEPTH, HEAD_DIM)),
        "ka_norm": gain(ks[8], (DEPTH, HEAD_DIM)),
        "rpb": nrm(ks[9], (DEPTH, NA_HEADS, 2 * WIN_H - 1, 2 * WIN_W - 1), 0.1),
        "qb_norm": gain(ks[10], (DEPTH, HEAD_DIM)),
        "kb_norm": gain(ks[11], (DEPTH, HEAD_DIM)),
        "lam_q1": nrm(ks[12], (DEPTH, HEAD_DIM), 0.1),
        "lam_k1": nrm(ks[13], (DEPTH, HEAD_DIM), 0.1),
        "lam_q2": nrm(ks[14], (DEPTH, HEAD_DIM), 0.1),
        "lam_k2": nrm(ks[15], (DEPTH, HEAD_DIM), 0.1),
        "subln": gain(ks[16], (DEPTH, DIFF_V_DIM)),
        "w_a_out": nrm(ks[17], (DEPTH, NA_WIDTH, D_MODEL), NA_WIDTH ** -0.5),
        "w_b_out": nrm(ks[18], (DEPTH, DIFF_V_WIDTH, D_MODEL), DIFF_V_WIDTH ** -0.5),
        "w_o": nrm(ks[19], (DEPTH, D_MODEL, D_MODEL), D_MODEL ** -0.5),
        "ffn2_norm": gain(ks[20], (DEPTH, D_MODEL)),
        "ffn2_wi": nrm(ks[21], (DEPTH, D_MODEL, 2 * D_FF), D_MODEL ** -0.5),
        "ffn2_wo": nrm(ks[22], (DEPTH, D_FF, D_MODEL), D_FF ** -0.5),
    }


def reference(x_prompt, x_sample, ffn1_norm, ffn1_wi, ffn1_wo, mix_norm, w_in, qa_norm, ka_norm, rpb,
              qb_norm, kb_norm, lam_q1, lam_k1, lam_q2, lam_k2, subln, w_a_out, w_b_out, w_o,
              ffn2_norm, ffn2_wi, ffn2_wo):
    y_prompt = x_prompt
    y_sample = x_sample
    for l in range(DEPTH):
        layer_args = (ffn1_norm[l], ffn1_wi[l], ffn1_wo[l], mix_norm[l], w_in[l], qa_norm[l], ka_norm[l],
                      rpb[l], qb_norm[l], kb_norm[l], lam_q1[l], lam_k1[l], lam_q2[l], lam_k2[l], subln[l],
                      w_a_out[l], w_b_out[l], w_o[l], ffn2_norm[l], ffn2_wi[l], ffn2_wo[l])
        y_prompt = encoder_layer(y_prompt, l, *layer_args)
        y_sample = encoder_layer(y_sample, l, *layer_args)
    return (y_prompt, y_sample)
```

```python
import contextlib
import math
import numpy as np
import concourse.bass as bass
import concourse.mybir as mybir
from concourse.bass_utils import run_bass_kernel_spmd

F32 = mybir.dt.float32
BF16 = mybir.dt.bfloat16
AF = mybir.ActivationFunctionType
ALU = mybir.AluOpType
AX = mybir.AxisListType

D = 1024
DFF = 2816
NJ = DFF // 128
GRID_W = 64
NEG_INF = -1e30
NORM_EPS = 1e-6
SUBLN_EPS = 1e-5
ROPE_THETA = 500000.0

ENGS = ("pe", "act", "dve", "pool", "sp")
N_DMA_SEMS = 24
SEM_LIMIT = 30000
GR = 512
SB_BASE = 16640
SB_END = 229300


class Op:
    __slots__ = ("eng", "fn", "deps", "sig", "is_dma", "needs_sig", "waits")

    def __init__(self, eng, fn, is_dma):
        self.eng = eng
        self.fn = fn
        self.deps = []
        self.sig = None
        self.is_dma = is_dma
        self.needs_sig = is_dma
        self.waits = None


class Sched:
    def __init__(self, same_engine_sync=("act", "dve", "pool")):
        self.ops = []
        self.last_w = {}
        self.readers = {}
        self.same_sync = set(same_engine_sync)
        self.dma_rr = 0
        self.dma_last = [None] * N_DMA_SEMS

    def add(self, eng, fn, reads=(), writes=(), is_dma=False):
        op = Op(eng, fn, is_dma)
        deps = []
        lw = self.last_w
        rdrs = self.readers
        for r in reads:
            w = lw.get(r)
            if w is not None:
                deps.append(w)
        for r in writes:
            w = lw.get(r)
            if w is not None:
                deps.append(w)
            rl = rdrs.get(r)
            if rl:
                deps.extend(rl)
        for r in reads:
            rl = rdrs.get(r)
            if rl is None:
                rdrs[r] = [op]
            else:
                rl.append(op)
        for r in writes:
            lw[r] = op
            rdrs[r] = []
        if is_dma:
            slot = self.dma_rr % N_DMA_SEMS
            self.dma_rr += 1
            prev = self.dma_last[slot]
            if prev is not None:
                deps.append(prev)
            self.dma_last[slot] = op
            op.sig = slot
        seen = set()
        for d in deps:
            if d is op or id(d) in seen:
                continue
            seen.add(id(d))
            if d.eng == op.eng and not d.is_dma and not op.is_dma and d.eng not in self.same_sync:
                continue
            d.needs_sig = True
            op.deps.append(d)
        self.ops.append(op)
        return op

    def finalize(self, nc, st):
        cnt = {e: 0 for e in ENGS}
        for op in self.ops:
            if not op.is_dma and op.needs_sig:
                cnt[op.eng] += 1
        self.esems = {}
        for e in ENGS:
            n = (cnt[e] + SEM_LIMIT - 1) // SEM_LIMIT
            self.esems[e] = [st.enter_context(nc.semaphore("s_%s%d" % (e, i))) for i in range(max(n, 1))]
        self.dsems = [st.enter_context(nc.semaphore("s_dma%d" % i)) for i in range(N_DMA_SEMS)]
        cnt = {e: 0 for e in ENGS}
        dcnt = [0] * N_DMA_SEMS
        for op in self.ops:
            if op.is_dma:
                slot = op.sig
                dcnt[slot] += 16
                op.sig = (self.dsems[slot], dcnt[slot], 16)
            elif op.needs_sig:
                c = cnt[op.eng]
                cnt[op.eng] = c + 1
                op.sig = (self.esems[op.eng][c // SEM_LIMIT], c % SEM_LIMIT + 1, 1)
        waited = {e: {} for e in ENGS}
        self.per_eng = {e: [] for e in ENGS}
        for op in self.ops:
            w = {}
            for d in op.deps:
                s, v, _ = d.sig
                k = s.num
                cur = w.get(k)
                if cur is None or v > cur[1]:
                    w[k] = (s, v)
            ws = []
            wd = waited[op.eng]
            for k, (s, v) in w.items():
                if wd.get(k, 0) >= v:
                    continue
                wd[k] = v
                ws.append((s, v))
            op.waits = ws
            self.per_eng[op.eng].append(op)
        self.final_dma = dcnt

    def run_engine(self, ename, e, final_wait=False):
        for op in self.per_eng[ename]:
            for s, v in op.waits:
                e.wait_ge(s, v)
            ins = op.fn(e)
            if op.needs_sig:
                s, v, inc = op.sig
                ins.then_inc(s, inc)
        if final_wait:
            for i, v in enumerate(self.final_dma):
                if v > 0:
                    e.wait_ge(self.dsems[i], v)


class Buf:
    def __init__(self, nc, name, shape, dtype, off):
        self.esz = 4 if dtype == F32 else 2
        n = 1
        for s in shape[1:]:
            n *= s
        self.nbytes = n * self.esz
        self.off = off
        self.shape = shape
        assert off >= SB_BASE and off + self.nbytes <= SB_END, (name, off, self.nbytes)
        self.t = nc.alloc_sbuf_tensor_at(name, list(shape), dtype, offset=off)
        self.inner = (n // shape[1]) * self.esz if len(shape) > 2 else self.nbytes
        self._all = self.k(0, self.nbytes)

    def k(self, lo=0, hi=None):
        if hi is None:
            hi = self.nbytes
        return list(range((self.off + lo) // GR, (self.off + hi - 1) // GR + 1))

    def all(self):
        return self._all

    def ck(self, c, n=1):
        return self.k(c * self.inner, (c + n) * self.inner)


def _bias_blocks(rpb_l):
    cq = np.arange(64)
    ck = np.arange(64)
    wstart = np.clip(cq - 8, 0, 48)
    colmask = (ck[:, None] >= wstart[None, :]) & (ck[:, None] < wstart[None, :] + 16)
    dc = np.clip(ck[:, None] - cq[None, :] + 15, 0, 30)
    Bm = np.where(colmask[None, None], rpb_l[:, :, dc], np.float32(NEG_INF)).astype(np.float32)
    return Bm


def _compact_tables(rpb_l):
    Bm = _bias_blocks(rpb_l)
    neg = np.full((8, 64, 64), NEG_INF, np.float32)

    def blk(d, lo, hi):
        if lo <= d <= hi:
            return Bm[:, d + 7]
        return neg

    tI = np.empty((8, 128, 23 * 64), np.float32)
    for s in range(23):
        d0 = 10 - s
        tI[:, 0:64, s * 64:(s + 1) * 64] = blk(d0, -4, 3)
        tI[:, 64:128, s * 64:(s + 1) * 64] = blk(d0 + 1, -4, 3)
    tF = np.empty((8, 128, 16 * 64), np.float32)
    for s in range(16):
        d0 = 7 - s
        tF[:, 0:64, s * 64:(s + 1) * 64] = blk(d0, -7, 7)
        tF[:, 64:128, s * 64:(s + 1) * 64] = blk(d0 + 1, -7, 7)
    return tI, tF


def _rope_tables(L):
    half = 8
    inv_freq = np.power(np.float32(ROPE_THETA), -np.arange(half, dtype=np.float32) * np.float32(2.0) / np.float32(16)).astype(np.float32)
    pos = np.arange(L, dtype=np.float32)
    ang = (pos[:, None] * inv_freq[None, :]).astype(np.float32)
    cos = np.cos(ang).astype(np.float32).T
    sin = np.sin(ang).astype(np.float32).T
    cosF = np.ones((128, L), np.float32)
    sinF = np.zeros((128, L), np.float32)
    for b in (0, 64):
        cosF[b:b + 8] = cos
        cosF[b + 8:b + 16] = cos
        sinF[b:b + 8] = -sin
        sinF[b + 8:b + 16] = sin
    rotT = np.zeros((128, 128), np.float32)
    for b in (0, 64):
        for d in range(8):
            rotT[b + d + 8, b + d] = 1.0
            rotT[b + d, b + d + 8] = 1.0
    return cosF, sinF, rotT


class Prog:
    def __init__(self, seqs, n_layers, lmax):
        self.seqs = seqs
        self.NL = n_layers
        self.LMAX = lmax
        self.nc = bass.Bass("TRN2", target_bir_lowering=False)
        self.S = Sched()
        self.sb_ptr = SB_BASE
        self.bank_rr = 0
        self.build()

    def sb(self, name, shape, dtype, off=None):
        esz = 4 if dtype == F32 else 2
        n = 1
        for s in shape[1:]:
            n *= s
        nbytes = n * esz
        if off is None:
            off = self.sb_ptr
            self.sb_ptr = (off + nbytes + GR - 1) // GR * GR
        return Buf(self.nc, name, shape, dtype, off)

    def bank(self):
        i = self.bank_rr % 8
        self.bank_rr += 1
        return self.ps[i], ("p", i)

    def dma(self, q, out, in_, reads, writes):
        self.S.add(q, lambda e: e.dma_start(out=out, in_=in_), reads, writes, is_dma=True)

    def mm(self, out, lhsT, rhs, start, stop, reads, writes):
        self.S.add("pe", lambda e: e.matmul(out, lhsT=lhsT, rhs=rhs, start=start, stop=stop), reads, writes)

    def act(self, out, in_, func, reads, writes, scale=1.0, bias=None):
        if bias is None:
            self.S.add("act", lambda e: e.activation(out=out, in_=in_, func=func, scale=scale), reads, writes)
        else:
            self.S.add("act", lambda e: e.activation(out=out, in_=in_, func=func, scale=scale, bias=bias), reads, writes)

    def tt(self, eng, out, in0, in1, op, reads, writes):
        self.S.add(eng, lambda e: e.tensor_tensor(out=out, in0=in0, in1=in1, op=op), reads, writes)

    def stt(self, eng, out, in0, scalar, in1, op0, op1, reads, writes):
        self.S.add(eng, lambda e: e.scalar_tensor_tensor(out=out, in0=in0, scalar=scalar, in1=in1, op0=op0, op1=op1), reads, writes)

    def ts(self, eng, out, in0, s1, op0, reads, writes, s2=None, op1=None):
        if op1 is None:
            self.S.add(eng, lambda e: e.tensor_scalar(out=out, in0=in0, scalar1=s1, scalar2=None, op0=op0), reads, writes)
        else:
            self.S.add(eng, lambda e: e.tensor_scalar(out=out, in0=in0, scalar1=s1, scalar2=s2, op0=op0, op1=op1), reads, writes)

    def cp(self, eng, out, in_, reads, writes):
        if eng == "act":
            self.S.add("act", lambda e: e.activation(out=out, in_=in_, func=AF.Copy), reads, writes)
        else:
            self.S.add(eng, lambda e: e.tensor_copy(out=out, in_=in_), reads, writes)

    def memset(self, eng, ap, val, reads, writes):
        self.S.add(eng, lambda e: e.memset(ap, val), reads, writes)

    def recip(self, out, in_, reads, writes):
        self.S.add("dve", lambda e: e.reciprocal(out=out, in_=in_), reads, writes)

    def build(self):
        nc = self.nc
        NL = self.NL
        LMAX = self.LMAX
        NCH = LMAX // 128
        self.din = {}
        self.dout = {}
        rows = {}
        self.has_seg = any(m == "seg" for (_, _, _, _, m) in self.seqs)
        for (iname, oname, L, row0, mode) in self.seqs:
            rows[iname] = max(rows.get(iname, 0), row0 + L)
        for iname, r in rows.items():
            self.din[iname] = nc.dram_tensor(iname, [r, D], F32, kind="ExternalInput").ap()
        for (iname, oname, L, row0, mode) in self.seqs:
            if oname not in self.dout:
                orows = 2048 if mode == "seg" else rows[iname]
                self.dout[oname] = nc.dram_tensor(oname, [orows, D], F32, kind="ExternalOutput").ap()
        ein = lambda name, shape: nc.dram_tensor(name, shape, F32, kind="ExternalInput").ap()
        self.w_ffn_wi = [ein("ffn1_wi", [NL, D, 2 * DFF]), ein("ffn2_wi", [NL, D, 2 * DFF])]
        self.w_ffn_wo = [ein("ffn1_wo", [NL, DFF, D]), ein("ffn2_wo", [NL, DFF, D])]
        self.w_in = ein("w_in", [NL, D, 5120])
        self.w_a = ein("w_a_out", [NL, 512, D])
        self.w_b = ein("w_b_out", [NL, 512, D])
        self.w_o = ein("w_o", [NL, D, D])
        self.gcols_d = ein("gcols", [128, NL * 29])
        self.lamv_d = ein("lamv", [NL * 256])
        self.tabI_d = ein("tabI", [NL, 8, 128, 1472])
        self.tabF_d = ein("tabF", [NL, 8, 128, 1024])
        self.cos_d = ein("cosF", [128, LMAX])
        self.sin_d = ein("sinF", [128, LMAX])
        self.ident_d = ein("ident", [128, 128])
        self.rot_d = ein("rotT", [128, 128])
        if self.has_seg:
            self.ncand = LMAX // 2048
            self.wsel_d = ein("wsel", [128, self.ncand])
            self.segtab_d = ein("segtab", [2, 8, 128, 2048])
        sc = lambda name, shape, dt: nc.dram_tensor(name, shape, dt)
        self.wi_b = [[sc("wib%d%d" % (l, f), [NJ, 128, 8, 256], BF16) for f in range(2)] for l in range(NL)]
        self.wo_b = [[sc("wob%d%d" % (l, f), [8, 128, NJ, 128], BF16) for f in range(2)] for l in range(NL)]
        self.win_b = [sc("winb%d" % l, [20, 128, 8, 256], BF16) for l in range(NL)]
        self.wa_b = [sc("wab%d" % l, [4, 128, 4, 256], BF16) for l in range(NL)]
        self.wb_b = [sc("wbb%d" % l, [4, 128, 4, 256], BF16) for l in range(NL)]
        self.wout_b = [sc("woutb%d" % l, [4, 128, 8, 256], BF16) for l in range(NL)]
        self.eint_b = sc("eintb", [NL, 128, 8, 1472], BF16)
        self.efull_b = sc("efullb", [NL, 8, 128, 1024], BF16)
        self.qaT = sc("qaT", [4, 128, LMAX], BF16)
        self.kaT = sc("kaT", [4, 128, LMAX + 512], BF16)
        self.qbT = sc("qbT", [4, 128, LMAX], BF16)
        self.kbT = sc("kbT", [4, 128, LMAX], BF16)
        self.va = sc("va", [128, NCH + 4, 512], BF16)
        self.vb = sc("vb", [4, 128, NCH, 128], BF16)
        self.gA = sc("gA", [8, 128, LMAX], F32)
        self.gB = sc("gB", [8, 128, LMAX], F32)
        self.xmid = sc("xmid", [8, 128, LMAX], F32)
        self.x1 = sc("x1", [8, 128, LMAX], F32)
        if self.has_seg:
            self.segtab_b = sc("segtabb", [2, 8, 128, 2048], BF16)
            self.sel_qaT = sc("sel_qaT", [4, 128, 2048], BF16)
            self.sel_qbT = sc("sel_qbT", [4, 128, 2048], BF16)
            self.sel_kwin = sc("sel_kwin", [4, 4, 128, 1024], BF16)
            self.sel_vwin = sc("sel_vwin", [4, 128, 8, 512], BF16)
            self.sel_gA = sc("sel_gA", [8, 128, 2048], F32)
            self.sel_gB = sc("sel_gB", [8, 128, 2048], F32)
            self.sel_xmid = sc("sel_xmid", [8, 128, 2048], F32)

        self.ps = [nc.alloc_psum_tensor("ps%d" % i, [128, 512], F32) for i in range(8)]

        sb = self.sb
        self.ident = sb("ident", [128, 128], F32)
        self.ones_b = sb("ones_b", [128, 128], BF16)
        self.blk_b = sb("blk_b", [128, 128], BF16)
        self.half_b = [sb("half0_b", [128, 128], BF16), sb("half1_b", [128, 128], BF16)]
        self.rot_b = sb("rot_b", [128, 128], BF16)
        self.zero_b = sb("zero_b", [128, 1024], BF16)
        self.gcols = sb("gcols_sb", [128, NL * 29], F32)
        self.g32 = sb("g32", [128, NL * 24], F32)
        self.gd = sb("gd", [128, NL * 8], F32)
        self.eint = sb("eint", [128, 8, 1472], BF16)
        self.xT = sb("xT", [128, 8, 512], F32)
        self.gA_sb = sb("gA_sb", [128, 8, 512], F32)
        self.gB_sb = sb("gB_sb", [128, 8, 512], F32)
        self.wslot = [sb("wslot%d" % i, [128, 2816], BF16) for i in range(4)]
        self.ws_rr = 0
        self.cos_sb = sb("cos_sb", [128, 512], F32)
        self.sin_sb = sb("sin_sb", [128, 512], F32)
        self.AT = sb("AT", [128, 4, 512], BF16)
        self.BT = sb("BT", [128, 4, 512], BF16)
        base = self.sb_ptr
        self.uT = sb("uT", [128, 8, 512], BF16)
        self.hT = sb("hT", [128, NJ, 512], BF16)
        self.mT = sb("mT", [128, 8, 512], BF16, off=self.hT.off)
        self.xtok = sb("xtok", [128, 4, 1024], F32)
        self.sq = [sb("sq%d" % i, [128, 512], BF16) for i in range(2)]
        self.rstd = [sb("rstd%d" % i, [128, 512], F32) for i in range(2)]
        self.tmpf = [sb("tmpf%d" % i, [128, 512], F32) for i in range(4)]
        self.qf = [sb("qf%d" % i, [128, 512], F32) for i in range(2)]
        self.qnb = [sb("qnb%d" % i, [128, 512], BF16) for i in range(2)]
        self.stage = sb("stage", [128, 4, 512], BF16)
        self.vstA = sb("vstA", [128, 4, 512], BF16)
        self.vstB = sb("vstB", [128, 4, 4, 128], BF16)
        self.gst = [sb("gst%d" % i, [128, 2, 512], F32) for i in range(2)]
        endA = self.sb_ptr
        o = self.hT.off
        self.qfD = [Buf(nc, "qfD%d" % i, [128, 512], F32, o + i * 2048) for i in range(4)]
        o += 4 * 2048
        self.qnD = [Buf(nc, "qnD%d" % i, [128, 512], F32, o + i * 2048) for i in range(3)]
        o += 3 * 2048
        self.rstdD = [Buf(nc, "rstdD%d" % i, [128, 512], F32, o + i * 2048) for i in range(3)]
        o += 3 * 2048
        self.t1D = [Buf(nc, "t1D%d" % i, [128, 512], F32, o + i * 2048) for i in range(2)]
        o += 2 * 2048
        self.sqD = [Buf(nc, "sqD%d" % i, [128, 512], BF16, o + i * 1024) for i in range(3)]
        o += 3 * 1024
        self.qnbD = [Buf(nc, "qnbD%d" % i, [128, 512], BF16, o + i * 1024) for i in range(3)]
        o += 3 * 1024
        self.stageD = [Buf(nc, "stageD%d" % i, [128, 4, 512], BF16, o + i * 4096) for i in range(2)]
        o += 2 * 4096
        assert o <= self.xtok.off + self.xtok.nbytes, (o, self.xtok.off + self.xtok.nbytes)
        self.jq = 0
        self.sb_ptr = base
        self.qb_sb = sb("qb_sb", [128, 4, 512], BF16)
        self.qa_sb = sb("qa_sb", [128, 4, 512], BF16)
        self.kst = [sb("kst%d" % i, [128, 2048], BF16) for i in range(2)]
        self.vst = [sb("vst%d" % i, [128, 16, 128], BF16) for i in range(2)]
        self.P = [sb("P%d" % i, [128, 512], BF16) for i in range(4)]
        self.kwin = sb("kwin", [128, 4, 1024], BF16)
        self.vwin = sb("vwin", [128, 8, 512], BF16)
        self.vaug = sb("vaug", [128, 8, 8, 128], BF16)
        self.efull = [sb("efull%d" % i, [128, 2048], BF16) for i in range(2)]
        self.ef = [sb("ef%d" % i, [128, 512], F32) for i in range(4)]
        endB = self.sb_ptr
        self.sb_ptr = max(endA, endB)
        self.stg_f = [Buf(nc, "stgf%d" % i, [128, 2816], F32, base + i * 11264) for i in range(2)]
        self.stg_b = [Buf(nc, "stgb%d" % i, [128, 2816], BF16, base + 22528 + i * 5632) for i in range(2)]
        self.sb_used = self.sb_ptr
        if self.has_seg:
            self.wsel = Buf(nc, "wsel_sb", [128, self.ncand], F32, self.sb_ptr)
            self.sb_ptr += GR
            self.selF = [Buf(nc, "selF%d" % i, [128, 4096], F32, base + i * 16384) for i in range(3)]
            self.selH = [Buf(nc, "selH%d" % i, [128, 4096], BF16, base + i * 16384) for i in range(3)]

        self.prologue()
        for si, (iname, oname, L, row0, mode) in enumerate(self.seqs):
            for l in range(NL):
                self.seq_layer(si, iname, oname, L, row0, l, mode)
        self.emit()

    def prologue(self):
        NL = self.NL
        dma = self.dma
        dma("sp", self.ident.t[:], self.ident_d, [], self.ident.all())
        st0 = self.stg_f[0]
        dma("sp", st0.t[:, 0:128], self.rot_d, [], st0.k(0, 512))
        self.cp("dve", self.rot_b.t[:], st0.t[:, 0:128], st0.k(0, 512), self.rot_b.all())
        self.memset("pool", self.ones_b.t[:], 1.0, [], self.ones_b.all())
        self.memset("pool", self.zero_b.t[:], 0.0, [], self.zero_b.all())
        self.memset("pool", self.blk_b.t[:], 0.0, [], self.blk_b.all())
        self.memset("pool", self.blk_b.t[0:64, 0:64], 1.0, [], self.blk_b.all())
        self.memset("pool", self.blk_b.t[64:128, 64:128], 1.0, [], self.blk_b.all())
        for i in range(2):
            self.memset("pool", self.half_b[i].t[:], 0.0, [], self.half_b[i].all())
            self.memset("pool", self.half_b[i].t[:, i * 64:(i + 1) * 64], 1.0, [], self.half_b[i].all())
        dma("sp", self.gcols.t[:], self.gcols_d, [], self.gcols.all())
        lamb = self.stg_f[1]
        dma("sp", lamb.t[:, 0:NL * 256], self.lamv_d.partition_broadcast(128), [], lamb.k(0, NL * 1024))
        for l in range(NL):
            lam_init = 0.8 - 0.6 * math.exp(-0.3 * l)
            g = self.gcols.t
            gk = self.gcols.all()
            self.ts("dve", self.g32.t[:, l * 24:(l + 1) * 24], g[:, l * 29:l * 29 + 24], 32.0, ALU.mult, gk, self.g32.all())
            gd = self.gd.t
            gdk = self.gd.all()
            o = l * 8
            self.cp("dve", gd[:, o + 0:o + 1], g[:, l * 29 + 24:l * 29 + 25], gk, gdk)
            self.ts("dve", gd[:, o + 1:o + 2], g[:, l * 29 + 25:l * 29 + 26], 8.0, ALU.mult, gk, gdk)
            self.cp("dve", gd[:, o + 2:o + 3], g[:, l * 29 + 26:l * 29 + 27], gk, gdk)
            self.ts("dve", gd[:, o + 3:o + 4], g[:, l * 29 + 27:l * 29 + 28], 8.0, ALU.mult, gk, gdk)
            self.ts("dve", gd[:, o + 4:o + 5], g[:, l * 29 + 28:l * 29 + 29],
                    float(math.sqrt(128.0) * (1.0 - lam_init)), ALU.mult, gk, gdk)
            t = self.tmpf[0]
            lb = lamb.t
            lbk = lamb.k(0, NL * 1024)
            b0 = l * 256
            self.tt("dve", t.t[:, 0:64], lb[:, b0:b0 + 64], lb[:, b0 + 64:b0 + 128], ALU.mult, lbk, t.all())
            self.tt("dve", t.t[:, 64:128], lb[:, b0 + 128:b0 + 192], lb[:, b0 + 192:b0 + 256], ALU.mult, lbk, t.all())
            t2 = self.tmpf[1]
            self.S.add("dve", lambda e, t=t, t2=t2: e.reduce_sum(out=t2.t[:, 0:1], in_=t.t[:, 0:64], axis=AX.X), t.all(), t2.all())
            self.S.add("dve", lambda e, t=t, t2=t2: e.reduce_sum(out=t2.t[:, 1:2], in_=t.t[:, 64:128], axis=AX.X), t.all(), t2.all())
            self.act(t2.t[:, 2:4], t2.t[:, 0:2], AF.Exp, t2.all(), t2.all())
            self.tt("dve", t2.t[:, 4:5], t2.t[:, 3:4], t2.t[:, 2:3], ALU.subtract, t2.all(), t2.all())
            self.ts("dve", gd[:, o + 5:o + 6], t2.t[:, 4:5], float(-lam_init), ALU.add, t2.all(), gdk)
        self.cv_rr = 0
        if self.has_seg:
            dma("sp", self.wsel.t[:], self.wsel_d, [], self.wsel.all())
            for k in range(2):
                for h in range(8):
                    i = self.cv_rr % 2
                    self.cv_rr += 1
                    sf, sbb = self.stg_f[i], self.stg_b[i]
                    dma("sp", sf.t[:, 0:2048], self.segtab_d[k, h], [], sf.k(0, 8192))
                    self.act(sbb.t[:, 0:2048], sf.t[:, 0:2048], AF.Exp, sf.k(0, 8192), sbb.k(0, 4096))
                    dma("pool", self.segtab_b[k, h], sbb.t[:, 0:2048], sbb.k(0, 4096), [("segtabb", k, h)])
        for l in range(NL):
            for f in range(2):
                W = self.w_ffn_wi[f]
                for j in range(NJ):
                    srcs = [(W[l, :, j * 128:(j + 1) * 128].rearrange("(kc p) n -> p kc n", p=128), 0, 128),
                            (W[l, :, DFF + j * 128:DFF + (j + 1) * 128].rearrange("(kc p) n -> p kc n", p=128), 128, 128)]
                    self.convert(srcs, 8, 256, self.wi_b[l][f][j], ("wib", l, f, j))
                W = self.w_ffn_wo[f]
                for n in range(8):
                    srcs = [(W[l, :, n * 128:(n + 1) * 128].rearrange("(kc p) n -> p kc n", p=128), 0, 128)]
                    self.convert(srcs, NJ, 128, self.wo_b[l][f][n], ("wob", l, f, n))
            for g in range(20):
                srcs = [(self.w_in[l, :, g * 256:(g + 1) * 256].rearrange("(kc p) n -> p kc n", p=128), 0, 256)]
                self.convert(srcs, 8, 256, self.win_b[l][g], ("winb", l, g))
            for g in range(4):
                srcs = [(self.w_a[l, :, g * 256:(g + 1) * 256].rearrange("(kc p) n -> p kc n", p=128), 0, 256)]
                self.convert(srcs, 4, 256, self.wa_b[l][g], ("wab", l, g))
                srcs = [(self.w_b[l, :, g * 256:(g + 1) * 256].rearrange("(kc p) n -> p kc n", p=128), 0, 256)]
                self.convert(srcs, 4, 256, self.wb_b[l][g], ("wbb", l, g))
                srcs = [(self.w_o[l, :, g * 256:(g + 1) * 256].rearrange("(kc p) n -> p kc n", p=128), 0, 256)]
                self.convert(srcs, 8, 256, self.wout_b[l][g], ("woutb", l, g))
            for h in range(8):
                i = self.cv_rr % 2
                self.cv_rr += 1
                sf, sbb = self.stg_f[i], self.stg_b[i]
                dma("sp", sf.t[:, 0:1472], self.tabI_d[l, h], [], sf.k(0, 1472 * 4))
                self.act(sbb.t[:, 0:1472], sf.t[:, 0:1472], AF.Exp, sf.k(0, 1472 * 4), sbb.k(0, 1472 * 2))
                dma("pool", self.eint_b[l, :, h, :], sbb.t[:, 0:1472], sbb.k(0, 1472 * 2), [("eintb", l)])
                i = self.cv_rr % 2
                self.cv_rr += 1
                sf, sbb = self.stg_f[i], self.stg_b[i]
                dma("sp", sf.t[:, 0:1024], self.tabF_d[l, h], [], sf.k(0, 4096))
                self.act(sbb.t[:, 0:1024], sf.t[:, 0:1024], AF.Exp, sf.k(0, 4096), sbb.k(0, 2048))
                dma("pool", self.efull_b[l, h], sbb.t[:, 0:1024], sbb.k(0, 2048), [("efullb", l, h)])

    def convert(self, srcs, kc, n, dst, key):
        i = self.cv_rr % 2
        self.cv_rr += 1
        sf, sbb = self.stg_f[i], self.stg_b[i]
        tot = kc * n
        fv = sf.t[:, 0:tot].rearrange("p (k n) -> p k n", n=n)
        for (src, c0, w) in srcs:
            self.dma("sp", fv[:, :, c0:c0 + w], src, [], sf.k(0, tot * 4))
        eng = ("dve", "pool", "act")[self.cv_rr % 3]
        self.cp(eng, sbb.t[:, 0:tot], sf.t[:, 0:tot], sf.k(0, tot * 4), sbb.k(0, tot * 2))
        self.dma("pool", dst.rearrange("p k n -> p (k n)"), sbb.t[:, 0:tot], sbb.k(0, tot * 2), [key])

    def wload(self, src_ap, kc, n, key):
        ws = self.wslot[self.ws_rr % 4]
        self.ws_rr += 1
        tot = kc * n
        self.dma("sp", ws.t[:, 0:tot], src_ap.rearrange("p k n -> p (k n)"), [key], ws.k(0, tot * 2))
        return ws.t[:, 0:tot].rearrange("p (k n) -> p k n", n=n), ws.k(0, tot * 2)

    def rmsnorm(self, l, which):
        xT, uT = self.xT, self.uT
        pb, pk = self.bank()
        for c in range(8):
            sq = self.sq[c % 2]
            self.tt("dve" if c % 2 == 0 else "pool", sq.t[:], xT.t[:, c, :], xT.t[:, c, :], ALU.mult, xT.ck(c), sq.all())
            self.mm(pb[:], self.ones_b.t[:], sq.t[:], c == 0, c == 7, sq.all() + self.ones_b.all(), [pk])
        r = self.rstd[0]
        self.act(r.t[:], pb[:], AF.Sqrt, [pk], r.all(), scale=1.0, bias=float(1024 * NORM_EPS))
        self.recip(r.t[:], r.t[:], r.all(), r.all())
        gc = l * 24 + which * 8
        for c in range(8):
            self.stt("dve", uT.t[:, c, :], xT.t[:, c, :], self.g32.t[:, gc + c:gc + c + 1], r.t[:], ALU.mult, ALU.mult,
                     xT.ck(c) + self.g32.all() + r.all(), uT.ck(c))

    def ffn(self, l, f):
        xT, uT, hT = self.xT, self.uT, self.hT
        for j in range(NJ):
            W, wk = self.wload(self.wi_b[l][f][j], 8, 256, ("wib", l, f, j))
            pg, pgk = self.bank()
            pu, puk = self.bank()
            for kc in range(8):
                self.mm(pg[:], W[:, kc, 0:128], uT.t[:, kc, :], kc == 0, kc == 7, wk + uT.ck(kc), [pgk])
            for kc in range(8):
                self.mm(pu[:], W[:, kc, 128:256], uT.t[:, kc, :], kc == 0, kc == 7, wk + uT.ck(kc), [puk])
            t = self.tmpf[j % 2]
            t2 = self.tmpf[2 + j % 2]
            self.act(t.t[:], pg[:], AF.Silu, [pgk], t.all())
            self.cp("act", t2.t[:], pu[:], [puk], t2.all())
            self.tt("pool", hT.t[:, j, :], t.t[:], t2.t[:], ALU.mult, t.all() + t2.all(), hT.ck(j))
        for n in range(8):
            W, wk = self.wload(self.wo_b[l][f][n], NJ, 128, ("wob", l, f, n))
            pb, pk = self.bank()
            for j in range(NJ):
                self.mm(pb[:], W[:, j, :], hT.t[:, j, :], j == 0, j == NJ - 1, wk + hT.ck(j), [pk])
            self.stt("dve", xT.t[:, n, :], pb[:], 0.5, xT.t[:, n, :], ALU.mult, ALU.add, [pk] + xT.ck(n), xT.ck(n))

    def seq_layer(self, si, iname, oname, L, row0, l, mode="full"):
        NT = L // 512
        dma = self.dma
        dma("sp", self.eint.t[:], self.eint_b[l], [("eintb", l)], self.eint.all())
        NC = L // 128
        for pr in range(4):
            dma("pool", self.kaT[pr, :, 0:256], self.zero_b.t[:, 0:256], self.zero_b.all(), [("kaT", -1)])
            dma("pool", self.kaT[pr, :, 256 + L:512 + L], self.zero_b.t[:, 0:256], self.zero_b.all(), [("kaT", NT)])
        dma("pool", self.va[:, 0:2, :], self.zero_b.t[:].rearrange("p (c n) -> p c n", n=512), self.zero_b.all(), [("va", -1)])
        dma("pool", self.va[:, NC + 2:NC + 4, :], self.zero_b.t[:].rearrange("p (c n) -> p c n", n=512), self.zero_b.all(), [("va", NT)])
        for t in range(NT):
            self.phase_a(iname, L, row0, l, t)
        if mode == "seg" and l == self.NL - 1:
            self.select_segment(L)
            for slot in range(4):
                self.phase_bc(oname, L, 0, l, slot, slot=slot)
        else:
            for t in range(NT):
                self.phase_bc(oname, L, row0, l, t)

    def select_segment(self, L):
        NT = L // 512
        ncand = L // 2048
        rr = 0
        for slot in range(4):
            def win_keys(name, t):
                return [(name, tt_) for tt_ in (t - 1, t, t + 1) if -1 <= tt_ <= NT]
            kinds = [
                ("qaT", True, 4, 512, lambda t: self.qaT[:, :, t * 512:(t + 1) * 512].rearrange("c p n -> p c n"),
                 lambda t: [("qaT", t)], self.sel_qaT[:, :, slot * 512:(slot + 1) * 512].rearrange("c p n -> p c n")),
                ("qbT", True, 4, 512, lambda t: self.qbT[:, :, t * 512:(t + 1) * 512].rearrange("c p n -> p c n"),
                 lambda t: [("qbT", t)], self.sel_qbT[:, :, slot * 512:(slot + 1) * 512].rearrange("c p n -> p c n")),
                ("kwin", True, 4, 1024, lambda t: self.kaT[:, :, 512 * t:512 * t + 1024].rearrange("c p n -> p c n"),
                 lambda t: win_keys("kaT", t), self.sel_kwin[slot].rearrange("c p n -> p c n")),
                ("vwin", True, 8, 512, lambda t: self.va[:, 4 * t:4 * t + 8, :],
                 lambda t: win_keys("va", t), self.sel_vwin[slot]),
                ("gA", False, 8, 512, lambda t: self.gA[:, :, t * 512:(t + 1) * 512].rearrange("c p n -> p c n"),
                 lambda t: [("ga", t)], self.sel_gA[:, :, slot * 512:(slot + 1) * 512].rearrange("c p n -> p c n")),
                ("gB", False, 8, 512, lambda t: self.gB[:, :, t * 512:(t + 1) * 512].rearrange("c p n -> p c n"),
                 lambda t: [("gb", t)], self.sel_gB[:, :, slot * 512:(slot + 1) * 512].rearrange("c p n -> p c n")),
                ("xmid", False, 8, 512, lambda t: self.xmid[:, :, t * 512:(t + 1) * 512].rearrange("c p n -> p c n"),
                 lambda t: [("xmid", t)], self.sel_xmid[:, :, slot * 512:(slot + 1) * 512].rearrange("c p n -> p c n")),
            ]
            for (name, half, nch, nn, srcf, keyf, dst) in kinds:
                bufs = self.selH if half else self.selF
                esz = 2 if half else 4
                tot = nch * nn
                acc = bufs[2]
                for cnd in range(ncand):
                    t = cnd * 4 + slot
                    stg = bufs[rr % 2]
                    rr += 1
                    self.dma("sp", stg.t[:, 0:tot].rearrange("p (c n) -> p c n", n=nn), srcf(t), keyf(t), stg.k(0, tot * esz))
                    eng = "pool" if cnd == 0 else "dve"
                    wcol = self.wsel.t[:, cnd:cnd + 1]
                    if cnd == 0:
                        self.ts(eng, acc.t[:, 0:tot], stg.t[:, 0:tot], wcol, ALU.mult,
                                stg.k(0, tot * esz) + self.wsel.all(), acc.k(0, tot * esz))
                    else:
                        self.stt(eng, acc.t[:, 0:tot], stg.t[:, 0:tot], wcol, acc.t[:, 0:tot], ALU.mult, ALU.add,
                                 stg.k(0, tot * esz) + self.wsel.all() + acc.k(0, tot * esz), acc.k(0, tot * esz))
                self.dma("pool", dst, acc.t[:, 0:tot].rearrange("p (c n) -> p c n", n=nn), acc.k(0, tot * esz), [("sel", name, slot)])

    def phase_a(self, iname, L, row0, l, t):
        dma = self.dma
        xT, uT = self.xT, self.uT
        tok = slice(t * 512, (t + 1) * 512)
        if l == 0:
            xin = self.din[iname]
            for b in range(4):
                r0 = row0 + t * 512 + b * 128
                dma("sp", self.xtok.t[:, b, :], xin[r0:r0 + 128, :], [], self.xtok.ck(b))
            for c in range(8):
                pb, pk = self.bank()
                for b in range(4):
                    self.S.add("pe", lambda e, pb=pb, b=b, c=c: e.transpose(out=pb[:, b * 128:(b + 1) * 128],
                                                                               in_=self.xtok.t[:, b, c * 128:(c + 1) * 128],
                                                                               identity=self.ident.t[:]),
                               self.xtok.ck(b) + self.ident.all(), [pk])
                self.cp("act" if c % 2 == 0 else "dve", xT.t[:, c, :], pb[:], [pk], xT.ck(c))
        else:
            dma("sp", xT.t[:], self.x1[:, :, tok].rearrange("c p n -> p c n"), [("x1", t)], xT.all())
        dma("sp", self.cos_sb.t[:], self.cos_d[:, tok], [], self.cos_sb.all())
        dma("sp", self.sin_sb.t[:], self.sin_d[:, tok], [], self.sin_sb.all())
        self.rmsnorm(l, 0)
        self.ffn(l, 0)
        dma("pool", self.xmid[:, :, tok].rearrange("c p n -> p c n"), xT.t[:], xT.all(), [("xmid", t)])
        self.rmsnorm(l, 1)
        gdo = l * 8
        LAG = 2
        pending = []
        tick = [0]

        def run_due(force=False):
            for job in pending:
                if job[1] and (force or job[0] <= tick[0]):
                    job[1].pop(0)()
                    job[0] = tick[0] + LAG
            pending[:] = [j for j in pending if j[1]]

        def do_tick():
            tick[0] += 1
            run_due()

        kq = 0
        for g in range(20):
            W, wk = self.wload(self.win_b[l][g], 8, 256, ("winb", l, g))
            kind = ("qa", "qa", "ka", "ka", "va", "va", "qb", "qb", "kb", "kb", "vb", "vb",
                    "ga", "ga", "ga", "ga", "gb", "gb", "gb", "gb")[g]
            if kind in ("va", "vb"):
                half = g % 2
                for bp in range(2):
                    pb, pk = self.bank()
                    for bb in range(2):
                        b = bp * 2 + bb
                        for kc in range(8):
                            self.mm(pb[:, bb * 256:(bb + 1) * 256], uT.t[:, kc, b * 128:(b + 1) * 128], W[:, kc, :],
                                    kc == 0, kc == 7, wk + uT.ck(kc), [pk])
                    for bb in range(2):
                        b = bp * 2 + bb
                        if kind == "va":
                            self.cp("act", self.vstA.t[:, b, half * 256:(half + 1) * 256], pb[:, bb * 256:(bb + 1) * 256],
                                    [pk], self.vstA.all())
                        else:
                            self.cp("act", self.vstB.t[:, 2 * half:2 * half + 2, b, :],
                                    pb[:, bb * 256:(bb + 1) * 256].rearrange("p (h e) -> p h e", e=128),
                                    [pk], self.vstB.all())
                    do_tick()
                if half == 1:
                    if kind == "va":
                        dma("pool", self.va[:, 2 + 4 * t:2 + 4 * t + 4, :], self.vstA.t[:], self.vstA.all(), [("va", t)])
                    else:
                        dma("pool", self.vb[:, :, 4 * t:4 * t + 4, :].rearrange("h p c e -> p h c e"), self.vstB.t[:],
                            self.vstB.all(), [("vb", t)])
                continue
            for cc in range(2):
                ch = g * 2 + cc
                pb, pk = self.bank()
                for kc in range(8):
                    self.mm(pb[:], W[:, kc, cc * 128:(cc + 1) * 128], uT.t[:, kc, :], kc == 0, kc == 7, wk + uT.ck(kc), [pk])
                if kind in ("ga", "gb"):
                    gi = ch - 24 if kind == "ga" else ch - 32
                    gs = self.gst[(gi // 2) % 2]
                    self.act(gs.t[:, gi % 2, :], pb[:], AF.Sigmoid, [pk], gs.ck(gi % 2))
                    if gi % 2 == 1:
                        dst = self.gA if kind == "ga" else self.gB
                        c0 = gi - 1
                        dma("pool", dst[c0:c0 + 2, :, tok].rearrange("c p n -> p c n"), gs.t[:], gs.all(), [(kind, t)])
                    do_tick()
                    continue
                ci = ch % 4 if kind in ("qa", "ka") else (ch - 12) % 4
                if ci == 0:
                    kq += 1
                col = {"qa": 0, "ka": 1, "qb": 2, "kb": 3}[kind]
                gcol = self.gd.t[:, gdo + col:gdo + col + 1]
                jq = self.jq
                self.jq += 1
                qf = self.qfD[jq % 4]
                sq = self.sqD[jq % 3]
                r = self.rstdD[jq % 3]
                qn = self.qnD[jq % 3]
                qb16 = self.qnbD[jq % 3]
                t1 = self.t1D[jq % 2]
                stg = self.stageD[kq % 2]
                self.cp("act", qf.t[:], pb[:], [pk], qf.all())
                self.tt("pool", sq.t[:], qf.t[:], qf.t[:], ALU.mult, qf.all(), sq.all())

                def store(kind=kind, stg=stg):
                    if kind == "ka":
                        dma("pool", self.kaT[:, :, 256 + t * 512:256 + (t + 1) * 512].rearrange("c p n -> p c n"),
                            stg.t[:], stg.all(), [("kaT", t)])
                    else:
                        dst = {"qa": self.qaT, "qb": self.qbT, "kb": self.kbT}[kind]
                        dma("pool", dst[:, :, tok].rearrange("c p n -> p c n"), stg.t[:], stg.all(),
                            [({"qa": "qaT", "qb": "qbT", "kb": "kbT"}[kind], t)])

                def s2(kind=kind, ci=ci, gcol=gcol, qf=qf, sq=sq, r=r, qn=qn, qb16=qb16, stg=stg, store=store):
                    p2, p2k = self.bank()
                    self.mm(p2[:], self.blk_b.t[:], sq.t[:], True, True, sq.all() + self.blk_b.all(), [p2k])
                    self.act(r.t[:], p2[:], AF.Sqrt, [p2k], r.all(), scale=1.0, bias=float(64 * NORM_EPS))
                    self.recip(r.t[:], r.t[:], r.all(), r.all())
                    if kind in ("qa", "ka"):
                        self.stt("dve", stg.t[:, ci, :], qf.t[:], gcol, r.t[:], ALU.mult, ALU.mult,
                                 qf.all() + self.gd.all() + r.all(), stg.ck(ci))
                        if ci == 3:
                            store()
                    else:
                        self.stt("dve", qn.t[:], qf.t[:], gcol, r.t[:], ALU.mult, ALU.mult,
                                 qf.all() + self.gd.all() + r.all(), qn.all())
                        self.cp("act", qb16.t[:], qn.t[:], qn.all(), qb16.all())

                def s3(ci=ci, qn=qn, qb16=qb16, t1=t1, stg=stg, store=store):
                    p3, p3k = self.bank()
                    self.mm(p3[:], self.rot_b.t[:], qb16.t[:], True, True, qb16.all() + self.rot_b.all(), [p3k])
                    self.tt("pool", t1.t[:], qn.t[:], self.cos_sb.t[:], ALU.mult, qn.all() + self.cos_sb.all(), t1.all())
                    self.tt("dve", qn.t[:], p3[:], self.sin_sb.t[:], ALU.mult, [p3k] + self.sin_sb.all() + qn.all(), qn.all())
                    self.tt("dve", stg.t[:, ci, :], t1.t[:], qn.t[:], ALU.add, t1.all() + qn.all(), stg.ck(ci))
                    if ci == 3:
                        store()

                pending.append([tick[0] + LAG, [s2] if kind in ("qa", "ka") else [s2, s3]])
                do_tick()
        while pending:
            tick[0] += LAG
            run_due(force=True)

    def phase_bc(self, oname, L, row0, l, t, slot=None):
        dma = self.dma
        NT = L // 512
        tok = slice(t * 512, (t + 1) * 512)
        gdo = l * 8
        if slot is None:
            s_qa = (self.qaT[:, :, tok].rearrange("c p n -> p c n"), [("qaT", t)])
            s_kw = (self.kaT[:, :, 512 * t:512 * t + 1024].rearrange("c p n -> p c n"),
                    [("kaT", tt_) for tt_ in (t - 1, t, t + 1) if -1 <= tt_ <= NT])
            s_vw = (self.va[:, 4 * t:4 * t + 8, :], [("va", tt_) for tt_ in (t - 1, t, t + 1) if -1 <= tt_ <= NT])
            s_qb = (self.qbT[:, :, tok].rearrange("c p n -> p c n"), [("qbT", t)])
            s_xm = (self.xmid[:, :, tok].rearrange("c p n -> p c n"), [("xmid", t)])
            s_ga = (self.gA[:, :, tok].rearrange("c p n -> p c n"), [("ga", t)])
            s_gb = (self.gB[:, :, tok].rearrange("c p n -> p c n"), [("gb", t)])
            edge = "top" if t == 0 else ("bot" if t == NT - 1 else None)
        else:
            s_qa = (self.sel_qaT[:, :, tok].rearrange("c p n -> p c n"), [("sel", "qaT", slot)])
            s_kw = (self.sel_kwin[slot].rearrange("c p n -> p c n"), [("sel", "kwin", slot)])
            s_vw = (self.sel_vwin[slot], [("sel", "vwin", slot)])
            s_qb = (self.sel_qbT[:, :, tok].rearrange("c p n -> p c n"), [("sel", "qbT", slot)])
            s_xm = (self.sel_xmid[:, :, tok].rearrange("c p n -> p c n"), [("sel", "xmid", slot)])
            s_ga = (self.sel_gA[:, :, tok].rearrange("c p n -> p c n"), [("sel", "gA", slot)])
            s_gb = (self.sel_gB[:, :, tok].rearrange("c p n -> p c n"), [("sel", "gB", slot)])
            edge = "segtop" if slot == 0 else ("segbot" if slot == 3 else None)
        dma("sp", self.qa_sb.t[:], s_qa[0], s_qa[1], self.qa_sb.all())
        dma("sp", self.kwin.t[:], s_kw[0], s_kw[1], self.kwin.all())
        dma("sp", self.vwin.t[:], s_vw[0], s_vw[1], self.vwin.all())
        if t == 0 and l == 0 and not getattr(self, "_vaug_init", False):
            self._vaug_init = True
        self.memset("pool", self.vaug.t[:], 0.0, [], self.vaug.all())
        for par in range(2):
            self.S.add("pool", lambda e, par=par: e.tensor_copy(
                out=self.vaug.t[:, :, :, par * 64:(par + 1) * 64].rearrange("p c (hp two) d -> p c hp two d", two=2)[:, :, :, par, :],
                in_=self.vwin.t[:].rearrange("p c (hp two d) -> p c hp two d", two=2, d=64)[:, :, :, par, :]),
                self.vwin.all(), self.vaug.all())
        pi = 0
        for pr in range(4):
            nb = [self.bank() for _ in range(8)]
            po, pok = nb[0]
            pl, plk = nb[1]
            efbs = {}
            for hh in (2 * pr, 2 * pr + 1):
                if edge in ("top", "bot"):
                    efb = self.efull[hh % 2]
                    dma("sp", efb.t[:, 0:1024], self.efull_b[l, hh], [("efullb", l, hh)], efb.all())
                    efbs[hh] = efb
                elif edge is not None:
                    efb = self.efull[hh % 2]
                    kk = 0 if edge == "segtop" else 1
                    dma("sp", efb.t[:], self.segtab_b[kk, hh], [("segtabb", kk, hh)], efb.all())
                    efbs[hh] = efb
            items = [(hh, c) for hh in (2 * pr, 2 * pr + 1) for c in range(8)]
            nI = len(items)
            NLAG = 3

            def na_qk(i):
                hh, c = items[i]
                par = hh % 2
                psb, psk = nb[2 + i % 6]
                self.mm(psb[:], self.kwin.t[par * 64:(par + 1) * 64, pr, c * 128:(c + 1) * 128],
                        self.qa_sb.t[par * 64:(par + 1) * 64, pr, :], True, True,
                        self.kwin.ck(pr) + self.qa_sb.ck(pr), [psk])

            for i in range(min(NLAG, nI)):
                na_qk(i)
            for i in range(nI):
                if i + NLAG < nI:
                    na_qk(i + NLAG)
                hh, c = items[i]
                par = hh % 2
                psb, psk = nb[2 + i % 6]
                efb = efbs.get(hh)
                if True:
                    P = self.P[pi % 4]
                    pi += 1
                    self.act(P.t[:], psb[:], AF.Exp, [psk], P.all())
                    ei = self.eint
                    s0 = (14 - 2 * c) * 64
                    if edge is None:
                        self.tt("dve", P.t[:], P.t[:], ei.t[:, hh, s0:s0 + 512], ALU.mult, P.all() + ei.ck(hh), P.all())
                    elif edge == "segtop":
                        self.tt("dve", P.t[:, 0:256], P.t[:, 0:256], efb.t[:, c * 256:(c + 1) * 256], ALU.mult,
                                P.all() + efb.all(), P.all())
                        self.tt("dve", P.t[:, 256:512], P.t[:, 256:512], ei.t[:, hh, s0 + 256:s0 + 512], ALU.mult,
                                P.all() + ei.ck(hh), P.all())
                    elif edge == "segbot":
                        self.tt("dve", P.t[:, 0:256], P.t[:, 0:256], ei.t[:, hh, s0:s0 + 256], ALU.mult,
                                P.all() + ei.ck(hh), P.all())
                        self.tt("dve", P.t[:, 256:512], P.t[:, 256:512], efb.t[:, c * 256:(c + 1) * 256], ALU.mult,
                                P.all() + efb.all(), P.all())
                    else:
                        valid = 2 <= c <= 5
                        f0 = (11 - 2 * c) * 64
                        if edge == "top":
                            if valid:
                                self.tt("dve", P.t[:, 0:256], P.t[:, 0:256], efb.t[:, f0:f0 + 256], ALU.mult,
                                        P.all() + efb.all(), P.all())
                            else:
                                self.memset("dve", P.t[:, 0:256], 0.0, P.all(), P.all())
                            self.tt("dve", P.t[:, 256:512], P.t[:, 256:512], ei.t[:, hh, s0 + 256:s0 + 512], ALU.mult,
                                    P.all() + ei.ck(hh), P.all())
                        else:
                            self.tt("dve", P.t[:, 0:256], P.t[:, 0:256], ei.t[:, hh, s0:s0 + 256], ALU.mult,
                                    P.all() + ei.ck(hh), P.all())
                            if valid:
                                self.tt("dve", P.t[:, 256:512], P.t[:, 256:512], efb.t[:, f0 + 256:f0 + 512], ALU.mult,
                                        P.all() + efb.all(), P.all())
                            else:
                                self.memset("dve", P.t[:, 256:512], 0.0, P.all(), P.all())
                    self.mm(po[:], self.vaug.t[:, c, hh, :], P.t[:], i == 0, i == nI - 1, self.vaug.all() + P.all(), [pok])
                    self.mm(pl[:], self.half_b[par].t[:], P.t[:], i == 0, i == nI - 1, self.half_b[par].all() + P.all(), [plk])
            rl = self.ef[pr % 2]
            self.recip(rl.t[:], pl[:], [plk], rl.all())
            self.tt("dve", self.AT.t[:, pr, :], po[:], rl.t[:], ALU.mult, [pok] + rl.all(), self.AT.ck(pr))
        dma("sp", self.qb_sb.t[:], s_qb[0], s_qb[1], self.qb_sb.all())
        NB = (L + 2047) // 2048
        KB = min(L, 2048)
        CPB = KB // 128
        si = 0
        for h in range(4):
            pO = [self.bank(), self.bank()]
            pL = [self.bank(), self.bank()]
            items = [(b, c) for b in range(NB) for c in range(CPB)]
            pend = None
            kv = {}
            nI = len(items)

            def load_block(b):
                ks = self.kst[(si + b) % 2]
                vs = self.vst[(si + b) % 2]
                kkeys = [("kbT", tt_) for tt_ in range(b * KB // 512, (b + 1) * KB // 512)]
                vkeys = [("vb", tt_) for tt_ in range(b * KB // 512, (b + 1) * KB // 512)]
                dma("sp", ks.t[:, 0:KB], self.kbT[h, :, b * KB:(b + 1) * KB], kkeys, ks.k(0, KB * 2))
                dma("sp", vs.t[:, 0:CPB, :], self.vb[h, :, b * CPB:(b + 1) * CPB, :], vkeys, vs.k(0, CPB * 256))
                return ks, vs

            def qk(idx):
                b, c = items[idx]
                if c == 0:
                    kv[b] = load_block(b)
                ks, vs = kv[b]
                out = []
                for m in range(2):
                    psb, psk = self.bank3()
                    self.mm(psb[:], ks.t[m * 64:(m + 1) * 64, c * 128:(c + 1) * 128],
                            self.qb_sb.t[m * 64:(m + 1) * 64, h, :], True, True,
                            ks.k(c * 256, (c + 1) * 256) + self.qb_sb.ck(h), [psk])
                    out.append((psb, psk))
                return out

            self._b3 = 0
            self._b3banks = [self.bank() for _ in range(4)]
            cur = qk(0)
            for idx in range(nI):
                nxt = qk(idx + 1) if idx + 1 < nI else None
                b, c = items[idx]
                ks, vs = kv[b]
                for m in range(2):
                    psb, psk = cur[m]
                    P = self.P[(2 * idx + m) % 4]
                    self.act(P.t[:], psb[:], AF.Exp, [psk], P.all())
                    self.mm(pO[m][0][:], vs.t[:, c, :], P.t[:], idx == 0, idx == nI - 1, vs.k(c * 256, (c + 1) * 256) + P.all(), [pO[m][1]])
                    self.mm(pL[m][0][:], self.ones_b.t[:], P.t[:], idx == 0, idx == nI - 1, self.ones_b.all() + P.all(), [pL[m][1]])
                cur = nxt
            si += NB
            r0, r1, t0, t1 = self.ef
            self.recip(r0.t[:], pL[0][0][:], [pL[0][1]], r0.all())
            self.recip(r1.t[:], pL[1][0][:], [pL[1][1]], r1.all())
            self.tt("dve", t0.t[:], pO[0][0][:], r0.t[:], ALU.mult, [pO[0][1]] + r0.all(), t0.all())
            self.stt("dve", t1.t[:], pO[1][0][:], self.gd.t[:, gdo + 5:gdo + 6], r1.t[:], ALU.mult, ALU.mult,
                     [pO[1][1]] + self.gd.all() + r1.all(), t1.all())
            self.tt("dve", t0.t[:], t0.t[:], t1.t[:], ALU.add, t0.all() + t1.all(), t0.all())
            sq = self.P[0]
            self.tt("pool", sq.t[:], t0.t[:], t0.t[:], ALU.mult, t0.all(), sq.all())
            p2, p2k = self.bank()
            self.mm(p2[:], self.ones_b.t[:], sq.t[:], True, True, sq.all() + self.ones_b.all(), [p2k])
            self.act(r0.t[:], p2[:], AF.Sqrt, [p2k], r0.all(), scale=1.0, bias=float(128 * SUBLN_EPS))
            self.recip(r0.t[:], r0.t[:], r0.all(), r0.all())
            self.stt("dve", self.BT.t[:, h, :], t0.t[:], self.gd.t[:, gdo + 4:gdo + 5], r0.t[:], ALU.mult, ALU.mult,
                     t0.all() + self.gd.all() + r0.all(), self.BT.ck(h))
        xT = self.xT
        dma("sp", xT.t[:], s_xm[0], s_xm[1], xT.all())
        dma("sp", self.gA_sb.t[:], s_ga[0], s_ga[1], self.gA_sb.all())
        dma("sp", self.gB_sb.t[:], s_gb[0], s_gb[1], self.gB_sb.all())
        for g in range(4):
            Wa, wak = self.wload(self.wa_b[l][g], 4, 256, ("wab", l, g))
            Wb, wbk = self.wload(self.wb_b[l][g], 4, 256, ("wbb", l, g))
            for cc in range(2):
                n = g * 2 + cc
                pa, pak = self.bank()
                pb, pbk = self.bank()
                for j in range(4):
                    self.mm(pa[:], Wa[:, j, cc * 128:(cc + 1) * 128], self.AT.t[:, j, :], j == 0, j == 3, wak + self.AT.ck(j), [pak])
                for j in range(4):
                    self.mm(pb[:], Wb[:, j, cc * 128:(cc + 1) * 128], self.BT.t[:, j, :], j == 0, j == 3, wbk + self.BT.ck(j), [pbk])
                t1 = self.tmpf[0]
                t2 = self.tmpf[1]
                self.tt("dve", t1.t[:], pa[:], self.gA_sb.t[:, n, :], ALU.mult, [pak] + self.gA_sb.ck(n), t1.all())
                self.tt("dve", t2.t[:], pb[:], self.gB_sb.t[:, n, :], ALU.mult, [pbk] + self.gB_sb.ck(n), t2.all())
                self.tt("pool", self.mT.t[:, n, :], t1.t[:], t2.t[:], ALU.add, t1.all() + t2.all(), self.mT.ck(n))
        for g in range(4):
            W, wk = self.wload(self.wout_b[l][g], 8, 256, ("woutb", l, g))
            for cc in range(2):
                n = g * 2 + cc
                pb, pk = self.bank()
                for kc in range(8):
                    self.mm(pb[:], W[:, kc, cc * 128:(cc + 1) * 128], self.mT.t[:, kc, :], kc == 0, kc == 7, wk + self.mT.ck(kc), [pk])
                self.tt("dve", xT.t[:, n, :], pb[:], xT.t[:, n, :], ALU.add, [pk] + xT.ck(n), xT.ck(n))
        self.rmsnorm(l, 2)
        self.ffn(l, 1)
        if l < self.NL - 1:
            dma("pool", self.x1[:, :, tok].rearrange("c p n -> p c n"), xT.t[:], xT.all(), [("x1", t)])
        else:
            yout = self.dout[oname]
            for b in range(4):
                for hc in range(2):
                    pb, pk = self.bank()
                    for cq in range(4):
                        c = hc * 4 + cq
                        self.S.add("pe", lambda e, pb=pb, b=b, c=c, cq=cq: e.transpose(
                            out=pb[:, cq * 128:(cq + 1) * 128], in_=xT.t[:, c, b * 128:(b + 1) * 128], identity=self.ident.t[:]),
                            xT.ck(c) + self.ident.all(), [pk])
                    self.cp("act" if hc == 0 else "dve", self.xtok.t[:, b, hc * 512:(hc + 1) * 512], pb[:], [pk], self.xtok.ck(b))
                r0 = row0 + t * 512 + b * 128
                dma("pool", yout[r0:r0 + 128, :], self.xtok.t[:, b, :], self.xtok.ck(b), [("y", oname, r0)])

    def bank3(self):
        b = self._b3banks[self._b3 % 4]
        self._b3 += 1
        return b

    def emit(self):
        nc = self.nc
        S = self.S
        with contextlib.ExitStack() as st:
            S.finalize(nc, st)
            block = st.enter_context(nc.Block())

            @block.sync
            def _(e):
                S.run_engine("sp", e, final_wait=True)

            @block.scalar
            def _(e):
                S.run_engine("act", e)

            @block.vector
            def _(e):
                S.run_engine("dve", e)

            @block.gpsimd
            def _(e):
                S.run_engine("pool", e)

            @block.tensor
            def _(e):
                S.run_engine("pe", e)


def host_tables(inputs, n_layers, lmax):
    gc = np.zeros((128, n_layers * 29), np.float32)
    lamv = np.zeros((n_layers * 256,), np.float32)
    tabI = np.zeros((n_layers, 8, 128, 1472), np.float32)
    tabF = np.zeros((n_layers, 8, 128, 1024), np.float32)
    for l in range(n_layers):
        o = l * 29
        gc[:, o + 0:o + 8] = np.asarray(inputs["ffn1_norm"][l], np.float32).reshape(8, 128).T
        gc[:, o + 8:o + 16] = np.asarray(inputs["mix_norm"][l], np.float32).reshape(8, 128).T
        gc[:, o + 16:o + 24] = np.asarray(inputs["ffn2_norm"][l], np.float32).reshape(8, 128).T
        gc[:, o + 24] = np.tile(np.asarray(inputs["qa_norm"][l], np.float32), 2)
        gc[:, o + 25] = np.tile(np.asarray(inputs["ka_norm"][l], np.float32), 2)
        gc[:, o + 26] = np.tile(np.asarray(inputs["qb_norm"][l], np.float32), 2)
        gc[:, o + 27] = np.tile(np.asarray(inputs["kb_norm"][l], np.float32), 2)
        gc[:, o + 28] = np.asarray(inputs["subln"][l], np.float32)
        for i, nm in enumerate(("lam_q1", "lam_k1", "lam_q2", "lam_k2")):
            lamv[l * 256 + i * 64:l * 256 + (i + 1) * 64] = np.asarray(inputs[nm][l], np.float32)
        tabI[l], tabF[l] = _compact_tables(np.asarray(inputs["rpb"][l], np.float32))
    cosF, sinF, rotT = _rope_tables(lmax)
    return dict(gcols=gc, lamv=lamv, tabI=tabI, tabF=tabF, cosF=cosF, sinF=sinF, rotT=rotT,
                ident=np.eye(128, dtype=np.float32))


def seg_tables(tI, tF, is_first, is_last):
    out = np.empty((2, 8, 128, 2048), np.float32)
    for c in range(8):
        a = (14 - 2 * c) * 64
        f = (11 - 2 * c) * 64
        if is_first:
            out[0, :, :, c * 256:(c + 1) * 256] = tF[:, :, f:f + 256] if 2 <= c <= 5 else np.float32(NEG_INF)
        else:
            out[0, :, :, c * 256:(c + 1) * 256] = tI[:, :, a:a + 256]
        if is_last:
            out[1, :, :, c * 256:(c + 1) * 256] = tF[:, :, f + 256:f + 512] if 2 <= c <= 5 else np.float32(NEG_INF)
        else:
            out[1, :, :, c * 256:(c + 1) * 256] = tI[:, :, a + 256:a + 512]
    return out


_PROG_CACHE = {}


def kernel(x_prompt, x_sample, ffn1_norm, ffn1_wi, ffn1_wo, mix_norm, w_in, qa_norm, ka_norm, rpb,
           qb_norm, kb_norm, lam_q1, lam_k1, lam_q2, lam_k2, subln, w_a_out, w_b_out, w_o,
           ffn2_norm, ffn2_wi, ffn2_wo):
    inputs = dict(locals())
    NL = 2
    LP = x_prompt.shape[1]
    LS = x_sample.shape[1]
    NCORES = 8
    per = x_sample.shape[0] // NCORES
    seqs = [("xp", "yp", LP, 0, "seg")] + [("xs", "ys", LS, i * LS, "full") for i in range(per)]
    assert LP // 2048 == NCORES
    key = (LP, LS, per)
    if key not in _PROG_CACHE:
        _PROG_CACHE[key] = Prog(seqs, NL, max(LP, LS))
    prog = _PROG_CACHE[key]
    tabs = host_tables(inputs, NL, max(LP, LS))
    shared = {k: np.ascontiguousarray(np.asarray(inputs[k], np.float32)) for k in
              ("ffn1_wi", "ffn2_wi", "ffn1_wo", "ffn2_wo", "w_in", "w_a_out", "w_b_out", "w_o")}
    shared.update(tabs)
    xp = np.ascontiguousarray(np.asarray(x_prompt, np.float32).reshape(LP, D))
    xs = np.asarray(x_sample, np.float32)
    in_maps = []
    for c in range(NCORES):
        m = dict(shared)
        m["xp"] = xp
        m["xs"] = np.ascontiguousarray(xs[c * per:(c + 1) * per].reshape(per * LS, D))
        ws = np.zeros((128, NCORES), np.float32)
        ws[:, c] = 1.0
        m["wsel"] = ws
        m["segtab"] = seg_tables(tabs["tabI"][NL - 1], tabs["tabF"][NL - 1], c == 0, c == NCORES - 1)
        in_maps.append(m)
    res = run_bass_kernel_spmd(prog.nc, in_maps, core_ids=list(range(NCORES)))
    yp = np.concatenate([np.asarray(res.results[c]["yp"], np.float32) for c in range(NCORES)], axis=0).reshape(1, LP, D)
    ys = np.concatenate([np.asarray(res.results[c]["ys"], np.float32).reshape(per, LS, D) for c in range(NCORES)], axis=0)
    return (yp, ys)
```

```python
import contextlib
import math
import numpy as np
import concourse.bass as bass
import concourse.mybir as mybir
from concourse.bass_utils import run_bass_kernel_spmd

F32 = mybir.dt.float32
BF16 = mybir.dt.bfloat16
AF = mybir.ActivationFunctionType
ALU = mybir.AluOpType
AX = mybir.AxisListType

D = 1024
DFF = 2816
NJ = DFF // 128
GRID_W = 64
NEG_INF = -1e30
NORM_EPS = 1e-6
SUBLN_EPS = 1e-5
ROPE_THETA = 500000.0

ENGS = ("pe", "act", "dve", "pool", "sp")
N_DMA_SEMS = 24
SEM_LIMIT = 30000
GR = 512
SB_BASE = 16640
SB_END = 229300


class Op:
    __slots__ = ("eng", "fn", "deps", "sig", "is_dma", "needs_sig", "waits")

    def __init__(self, eng, fn, is_dma):
        self.eng = eng
        self.fn = fn
        self.deps = []
        self.sig = None
        self.is_dma = is_dma
        self.needs_sig = is_dma
        self.waits = None


class Sched:
    def __init__(self, same_engine_sync=("act", "dve", "pool")):
        self.ops = []
        self.last_w = {}
        self.readers = {}
        self.same_sync = set(same_engine_sync)
        self.dma_rr = 0
        self.dma_last = [None] * N_DMA_SEMS

    def add(self, eng, fn, reads=(), writes=(), is_dma=False):
        op = Op(eng, fn, is_dma)
        deps = []
        lw = self.last_w
        rdrs = self.readers
        for r in reads:
            w = lw.get(r)
            if w is not None:
                deps.append(w)
        for r in writes:
            w = lw.get(r)
            if w is not None:
                deps.append(w)
            rl = rdrs.get(r)
            if rl:
                deps.extend(rl)
        for r in reads:
            rl = rdrs.get(r)
            if rl is None:
                rdrs[r] = [op]
            else:
                rl.append(op)
        for r in writes:
            lw[r] = op
            rdrs[r] = []
        if is_dma:
            slot = self.dma_rr % N_DMA_SEMS
            self.dma_rr += 1
            prev = self.dma_last[slot]
            if prev is not None:
                deps.append(prev)
            self.dma_last[slot] = op
            op.sig = slot
        seen = set()
        for d in deps:
            if d is op or id(d) in seen:
                continue
            seen.add(id(d))
            if d.eng == op.eng and not d.is_dma and not op.is_dma and d.eng not in self.same_sync:
                continue
            d.needs_sig = True
            op.deps.append(d)
        self.ops.append(op)
        return op

    def finalize(self, nc, st):
        cnt = {e: 0 for e in ENGS}
        for op in self.ops:
            if not op.is_dma and op.needs_sig:
                cnt[op.eng] += 1
        self.esems = {}
        for e in ENGS:
            n = (cnt[e] + SEM_LIMIT - 1) // SEM_LIMIT
            self.esems[e] = [st.enter_context(nc.semaphore("s_%s%d" % (e, i))) for i in range(max(n, 1))]
        self.dsems = [st.enter_context(nc.semaphore("s_dma%d" % i)) for i in range(N_DMA_SEMS)]
        cnt = {e: 0 for e in ENGS}
        dcnt = [0] * N_DMA_SEMS
        for op in self.ops:
            if op.is_dma:
                slot = op.sig
                dcnt[slot] += 16
                op.sig = (self.dsems[slot], dcnt[slot], 16)
            elif op.needs_sig:
                c = cnt[op.eng]
                cnt[op.eng] = c + 1
                op.sig = (self.esems[op.eng][c // SEM_LIMIT], c % SEM_LIMIT + 1, 1)
        waited = {e: {} for e in ENGS}
        self.per_eng = {e: [] for e in ENGS}
        for op in self.ops:
            w = {}
            for d in op.deps:
                s, v, _ = d.sig
                k = s.num
                cur = w.get(k)
                if cur is None or v > cur[1]:
                    w[k] = (s, v)
            ws = []
            wd = waited[op.eng]
            for k, (s, v) in w.items():
                if wd.get(k, 0) >= v:
                    continue
                wd[k] = v
                ws.append((s, v))
            op.waits = ws
            self.per_eng[op.eng].append(op)
        self.final_dma = dcnt

    def run_engine(self, ename, e, final_wait=False):
        for op in self.per_eng[ename]:
            for s, v in op.waits:
                e.wait_ge(s, v)
            ins = op.fn(e)
            if op.needs_sig:
                s, v, inc = op.sig
                ins.then_inc(s, inc)
        if final_wait:
            for i, v in enumerate(self.final_dma):
                if v > 0:
                    e.wait_ge(self.dsems[i], v)


class Buf:
    def __init__(self, nc, name, shape, dtype, off):
        self.esz = 4 if dtype == F32 else 2
        n = 1
        for s in shape[1:]:
            n *= s
        self.nbytes = n * self.esz
        self.off = off
        self.shape = shape
        assert off >= SB_BASE and off + self.nbytes <= SB_END, (name, off, self.nbytes)
        self.t = nc.alloc_sbuf_tensor_at(name, list(shape), dtype, offset=off)
        self.inner = (n // shape[1]) * self.esz if len(shape) > 2 else self.nbytes
        self._all = self.k(0, self.nbytes)

    def k(self, lo=0, hi=None):
        if hi is None:
            hi = self.nbytes
        return list(range((self.off + lo) // GR, (self.off + hi - 1) // GR + 1))

    def all(self):
        return self._all

    def ck(self, c, n=1):
        return self.k(c * self.inner, (c + n) * self.inner)


def _bias_blocks(rpb_l):
    cq = np.arange(64)
    ck = np.arange(64)
    wstart = np.clip(cq - 8, 0, 48)
    colmask = (ck[:, None] >= wstart[None, :]) & (ck[:, None] < wstart[None, :] + 16)
    dc = np.clip(ck[:, None] - cq[None, :] + 15, 0, 30)
    Bm = np.where(colmask[None, None], rpb_l[:, :, dc], np.float32(NEG_INF)).astype(np.float32)
    return Bm


def _compact_tables(rpb_l):
    Bm = _bias_blocks(rpb_l)
    neg = np.full((8, 64, 64), NEG_INF, np.float32)

    def blk(d, lo, hi):
        if lo <= d <= hi:
            return Bm[:, d + 7]
        return neg

    tI = np.empty((8, 128, 23 * 64), np.float32)
    for s in range(23):
        d0 = 10 - s
        tI[:, 0:64, s * 64:(s + 1) * 64] = blk(d0, -4, 3)
        tI[:, 64:128, s * 64:(s + 1) * 64] = blk(d0 + 1, -4, 3)
    tF = np.empty((8, 128, 16 * 64), np.float32)
    for s in range(16):
        d0 = 7 - s
        tF[:, 0:64, s * 64:(s + 1) * 64] = blk(d0, -7, 7)
        tF[:, 64:128, s * 64:(s + 1) * 64] = blk(d0 + 1, -7, 7)
    return tI, tF


def _rope_tables(L):
    half = 8
    inv_freq = np.power(np.float32(ROPE_THETA), -np.arange(half, dtype=np.float32) * np.float32(2.0) / np.float32(16)).astype(np.float32)
    pos = np.arange(L, dtype=np.float32)
    ang = (pos[:, None] * inv_freq[None, :]).astype(np.float32)
    cos = np.cos(ang).astype(np.float32).T
    sin = np.sin(ang).astype(np.float32).T
    cosF = np.ones((128, L), np.float32)
    sinF = np.zeros((128, L), np.float32)
    for b in (0, 64):
        cosF[b:b + 8] = cos
        cosF[b + 8:b + 16] = cos
        sinF[b:b + 8] = -sin
        sinF[b + 8:b + 16] = sin
    rotT = np.zeros((128, 128), np.float32)
    for b in (0, 64):
        for d in range(8):
            rotT[b + d + 8, b + d] = 1.0
            rotT[b + d, b + d + 8] = 1.0
    return cosF, sinF, rotT


class Prog:
    def __init__(self, seqs, n_layers, lmax):
        self.seqs = seqs
        self.NL = n_layers
        self.LMAX = lmax
        self.nc = bass.Bass("TRN2", target_bir_lowering=False)
        self.S = Sched()
        self.sb_ptr = SB_BASE
        self.bank_rr = 0
        self.build()

    def sb(self, name, shape, dtype, off=None):
        esz = 4 if dtype == F32 else 2
        n = 1
        for s in shape[1:]:
            n *= s
        nbytes = n * esz
        if off is None:
            off = self.sb_ptr
            self.sb_ptr = (off + nbytes + GR - 1) // GR * GR
        return Buf(self.nc, name, shape, dtype, off)

    def bank(self):
        i = self.bank_rr % 8
        self.bank_rr += 1
        return self.ps[i], ("p", i)

    def dma(self, q, out, in_, reads, writes):
        self.S.add(q, lambda e: e.dma_start(out=out, in_=in_), reads, writes, is_dma=True)

    def mm(self, out, lhsT, rhs, start, stop, reads, writes):
        self.S.add("pe", lambda e: e.matmul(out, lhsT=lhsT, rhs=rhs, start=start, stop=stop), reads, writes)

    def act(self, out, in_, func, reads, writes, scale=1.0, bias=None):
        if bias is None:
            self.S.add("act", lambda e: e.activation(out=out, in_=in_, func=func, scale=scale), reads, writes)
        else:
            self.S.add("act", lambda e: e.activation(out=out, in_=in_, func=func, scale=scale, bias=bias), reads, writes)

    def tt(self, eng, out, in0, in1, op, reads, writes):
        self.S.add(eng, lambda e: e.tensor_tensor(out=out, in0=in0, in1=in1, op=op), reads, writes)

    def stt(self, eng, out, in0, scalar, in1, op0, op1, reads, writes):
        self.S.add(eng, lambda e: e.scalar_tensor_tensor(out=out, in0=in0, scalar=scalar, in1=in1, op0=op0, op1=op1), reads, writes)

    def ts(self, eng, out, in0, s1, op0, reads, writes, s2=None, op1=None):
        if op1 is None:
            self.S.add(eng, lambda e: e.tensor_scalar(out=out, in0=in0, scalar1=s1, scalar2=None, op0=op0), reads, writes)
        else:
            self.S.add(eng, lambda e: e.tensor_scalar(out=out, in0=in0, scalar1=s1, scalar2=s2, op0=op0, op1=op1), reads, writes)

    def cp(self, eng, out, in_, reads, writes):
        if eng == "act":
            self.S.add("act", lambda e: e.activation(out=out, in_=in_, func=AF.Copy), reads, writes)
        else:
            self.S.add(eng, lambda e: e.tensor_copy(out=out, in_=in_), reads, writes)

    def memset(self, eng, ap, val, reads, writes):
        self.S.add(eng, lambda e: e.memset(ap, val), reads, writes)

    def rsqrt_act(self, out, in_, bias, reads, writes):
        self.S.add("act", lambda e: e.activation(out=out, in_=in_, func=AF.Ln, bias=bias, scale=1.0), reads, writes)
        self.S.add("act", lambda e: e.activation(out=out, in_=out, func=AF.Exp, scale=-0.5), writes, writes)

    def recip_act(self, out, in_, reads, writes):
        self.S.add("act", lambda e: e.activation(out=out, in_=in_, func=AF.Ln), reads, writes)
        self.S.add("act", lambda e: e.activation(out=out, in_=out, func=AF.Exp, scale=-1.0), writes, writes)

    def recip(self, out, in_, reads, writes):
        self.S.add("dve", lambda e: e.reciprocal(out=out, in_=in_), reads, writes)

    def build(self):
        nc = self.nc
        NL = self.NL
        LMAX = self.LMAX
        NCH = LMAX // 128
        self.din = {}
        self.dout = {}
        rows = {}
        self.has_seg = any(m == "seg" for (_, _, _, _, m) in self.seqs)
        for (iname, oname, L, row0, mode) in self.seqs:
            rows[iname] = max(rows.get(iname, 0), row0 + L)
        for iname, r in rows.items():
            self.din[iname] = nc.dram_tensor(iname, [r, D], F32, kind="ExternalInput").ap()
        for (iname, oname, L, row0, mode) in self.seqs:
            if oname not in self.dout:
                orows = 2048 if mode == "seg" else rows[iname]
                self.dout[oname] = nc.dram_tensor(oname, [orows, D], F32, kind="ExternalOutput").ap()
        ein = lambda name, shape: nc.dram_tensor(name, shape, F32, kind="ExternalInput").ap()
        self.w_ffn_wi = [ein("ffn1_wi", [NL, D, 2 * DFF]), ein("ffn2_wi", [NL, D, 2 * DFF])]
        self.w_ffn_wo = [ein("ffn1_wo", [NL, DFF, D]), ein("ffn2_wo", [NL, DFF, D])]
        self.w_in = ein("w_in", [NL, D, 5120])
        self.w_a = ein("w_a_out", [NL, 512, D])
        self.w_b = ein("w_b_out", [NL, 512, D])
        self.w_o = ein("w_o", [NL, D, D])
        self.gcols_d = ein("gcols", [128, NL * 29])
        self.lamv_d = ein("lamv", [NL * 256])
        self.tabI_d = ein("tabI", [NL, 8, 128, 1472])
        self.tabF_d = ein("tabF", [NL, 8, 128, 1024])
        self.cos_d = ein("cosF", [128, LMAX])
        self.sin_d = ein("sinF", [128, LMAX])
        self.ident_d = ein("ident", [128, 128])
        self.rot_d = ein("rotT", [128, 128])
        if self.has_seg:
            self.ncand = LMAX // 2048
            self.wsel_d = ein("wsel", [128, self.ncand])
            self.segtab_d = ein("segtab", [2, 8, 128, 2048])
        sc = lambda name, shape, dt: nc.dram_tensor(name, shape, dt)
        self.wi_b = [[sc("wib%d%d" % (l, f), [NJ, 128, 8, 256], BF16) for f in range(2)] for l in range(NL)]
        self.wo_b = [[sc("wob%d%d" % (l, f), [8, 128, NJ, 128], BF16) for f in range(2)] for l in range(NL)]
        self.win_b = [sc("winb%d" % l, [20, 128, 8, 256], BF16) for l in range(NL)]
        self.wa_b = [sc("wab%d" % l, [4, 128, 4, 256], BF16) for l in range(NL)]
        self.wb_b = [sc("wbb%d" % l, [4, 128, 4, 256], BF16) for l in range(NL)]
        self.wout_b = [sc("woutb%d" % l, [4, 128, 8, 256], BF16) for l in range(NL)]
        self.eint_b = sc("eintb", [NL, 128, 8, 1472], BF16)
        self.efull_b = sc("efullb", [NL, 8, 128, 1024], BF16)
        self.qaT = sc("qaT", [4, 128, LMAX], BF16)
        self.kaT = sc("kaT", [4, 128, LMAX + 512], BF16)
        self.qbT = sc("qbT", [4, 128, LMAX], BF16)
        self.kbT = sc("kbT", [4, 128, LMAX], BF16)
        self.va = sc("va", [128, NCH + 4, 512], BF16)
        self.vb = sc("vb", [4, 128, NCH, 128], BF16)
        self.gA = sc("gA", [8, 128, LMAX], F32)
        self.gB = sc("gB", [8, 128, LMAX], F32)
        self.xmid = sc("xmid", [8, 128, LMAX], F32)
        self.x1 = sc("x1", [8, 128, LMAX], F32)
        if self.has_seg:
            self.segtab_b = sc("segtabb", [2, 8, 128, 2048], BF16)
            self.sel_qaT = sc("sel_qaT", [4, 128, 2048], BF16)
            self.sel_qbT = sc("sel_qbT", [4, 128, 2048], BF16)
            self.sel_kwin = sc("sel_kwin", [4, 4, 128, 1024], BF16)
            self.sel_vwin = sc("sel_vwin", [4, 128, 8, 512], BF16)
            self.sel_gA = sc("sel_gA", [8, 128, 2048], F32)
            self.sel_gB = sc("sel_gB", [8, 128, 2048], F32)
            self.sel_xmid = sc("sel_xmid", [8, 128, 2048], F32)

        self.ps = [nc.alloc_psum_tensor("ps%d" % i, [128, 512], F32) for i in range(8)]

        sb = self.sb
        self.ident = sb("ident", [128, 128], F32)
        self.ones_b = sb("ones_b", [128, 128], BF16)
        self.blk_b = sb("blk_b", [128, 128], BF16)
        self.half_b = [sb("half0_b", [128, 128], BF16), sb("half1_b", [128, 128], BF16)]
        self.rot_b = sb("rot_b", [128, 128], BF16)
        self.zero_b = sb("zero_b", [128, 1024], BF16)
        self.gcols = sb("gcols_sb", [128, NL * 29], F32)
        self.g32 = sb("g32", [128, NL * 24], F32)
        self.gd = sb("gd", [128, NL * 8], F32)
        self.eint = sb("eint", [128, 8, 1472], BF16)
        self.xT = sb("xT", [128, 8, 512], F32)
        self.gA_sb = sb("gA_sb", [128, 8, 512], F32)
        self.gB_sb = sb("gB_sb", [128, 8, 512], F32)
        self.wslot = [sb("wslot%d" % i, [128, 2816], BF16) for i in range(4)]
        self.ws_rr = 0
        self.cos_sb = sb("cos_sb", [128, 512], F32)
        self.sin_sb = sb("sin_sb", [128, 512], F32)
        self.AT = sb("AT", [128, 4, 512], BF16)
        self.BT = sb("BT", [128, 4, 512], BF16)
        base = self.sb_ptr
        self.uT = sb("uT", [128, 8, 512], BF16)
        self.hT = sb("hT", [128, NJ, 512], BF16)
        self.mT = sb("mT", [128, 8, 512], BF16, off=self.hT.off)
        self.xtok = sb("xtok", [128, 4, 1024], F32)
        self.sq = [sb("sq%d" % i, [128, 512], BF16) for i in range(2)]
        self.rstd = [sb("rstd%d" % i, [128, 512], F32) for i in range(2)]
        self.tmpf = [sb("tmpf%d" % i, [128, 512], F32) for i in range(4)]
        self.qf = [sb("qf%d" % i, [128, 512], F32) for i in range(2)]
        self.qnb = [sb("qnb%d" % i, [128, 512], BF16) for i in range(2)]
        self.stage = sb("stage", [128, 4, 512], BF16)
        self.vstA = sb("vstA", [128, 4, 512], BF16)
        self.vstB = sb("vstB", [128, 4, 4, 128], BF16)
        self.gst = [sb("gst%d" % i, [128, 2, 512], F32) for i in range(2)]
        endA = self.sb_ptr
        o = self.hT.off
        self.qfD = [Buf(nc, "qfD%d" % i, [128, 512], F32, o + i * 2048) for i in range(4)]
        o += 4 * 2048
        self.qnD = [Buf(nc, "qnD%d" % i, [128, 512], F32, o + i * 2048) for i in range(3)]
        o += 3 * 2048
        self.rstdD = [Buf(nc, "rstdD%d" % i, [128, 512], F32, o + i * 2048) for i in range(3)]
        o += 3 * 2048
        self.t1D = [Buf(nc, "t1D%d" % i, [128, 512], F32, o + i * 2048) for i in range(2)]
        o += 2 * 2048
        self.sqD = [Buf(nc, "sqD%d" % i, [128, 512], BF16, o + i * 1024) for i in range(3)]
        o += 3 * 1024
        self.qnbD = [Buf(nc, "qnbD%d" % i, [128, 512], BF16, o + i * 1024) for i in range(3)]
        o += 3 * 1024
        self.stageD = [Buf(nc, "stageD%d" % i, [128, 4, 512], BF16, o + i * 4096) for i in range(2)]
        o += 2 * 4096
        assert o <= self.xtok.off + self.xtok.nbytes, (o, self.xtok.off + self.xtok.nbytes)
        self.jq = 0
        self.sb_ptr = base
        self.qb_sb = sb("qb_sb", [128, 4, 512], BF16)
        self.qa_sb = sb("qa_sb", [128, 4, 512], BF16)
        self.kst = [sb("kst%d" % i, [128, 2048], BF16) for i in range(2)]
        self.vst = [sb("vst%d" % i, [128, 16, 128], BF16) for i in range(2)]
        self.P = [sb("P%d" % i, [128, 512], BF16) for i in range(4)]
        self.kwin = sb("kwin", [128, 4, 1024], BF16)
        self.vwin = sb("vwin", [128, 8, 512], BF16)
        self.vaug = sb("vaug", [128, 8, 8, 128], BF16)
        self.efull = [sb("efull%d" % i, [128, 2048], BF16) for i in range(2)]
        self.ef = [sb("ef%d" % i, [128, 512], F32) for i in range(4)]
        endB = self.sb_ptr
        self.sb_ptr = max(endA, endB)
        self.stg_f = [Buf(nc, "stgf%d" % i, [128, 2816], F32, base + i * 11264) for i in range(2)]
        self.stg_b = [Buf(nc, "stgb%d" % i, [128, 2816], BF16, base + 22528 + i * 5632) for i in range(2)]
        self.sb_used = self.sb_ptr
        if self.has_seg:
            self.wsel = Buf(nc, "wsel_sb", [128, self.ncand], F32, self.sb_ptr)
            self.sb_ptr += GR
            self.selF = [Buf(nc, "selF%d" % i, [128, 4096], F32, base + i * 16384) for i in range(3)]
            self.selH = [Buf(nc, "selH%d" % i, [128, 4096], BF16, base + i * 16384) for i in range(3)]

        self.prologue()
        for si, (iname, oname, L, row0, mode) in enumerate(self.seqs):
            for l in range(NL):
                self.seq_layer(si, iname, oname, L, row0, l, mode)
        self.emit()

    def prologue(self):
        NL = self.NL
        dma = self.dma
        dma("sp", self.ident.t[:], self.ident_d, [], self.ident.all())
        st0 = self.stg_f[0]
        dma("sp", st0.t[:, 0:128], self.rot_d, [], st0.k(0, 512))
        self.cp("dve", self.rot_b.t[:], st0.t[:, 0:128], st0.k(0, 512), self.rot_b.all())
        self.memset("pool", self.ones_b.t[:], 1.0, [], self.ones_b.all())
        self.memset("pool", self.zero_b.t[:], 0.0, [], self.zero_b.all())
        self.memset("pool", self.blk_b.t[:], 0.0, [], self.blk_b.all())
        self.memset("pool", self.blk_b.t[0:64, 0:64], 1.0, [], self.blk_b.all())
        self.memset("pool", self.blk_b.t[64:128, 64:128], 1.0, [], self.blk_b.all())
        for i in range(2):
            self.memset("pool", self.half_b[i].t[:], 0.0, [], self.half_b[i].all())
            self.memset("pool", self.half_b[i].t[:, i * 64:(i + 1) * 64], 1.0, [], self.half_b[i].all())
        dma("sp", self.gcols.t[:], self.gcols_d, [], self.gcols.all())
        lamb = self.stg_f[1]
        dma("sp", lamb.t[:, 0:NL * 256], self.lamv_d.partition_broadcast(128), [], lamb.k(0, NL * 1024))
        for l in range(NL):
            lam_init = 0.8 - 0.6 * math.exp(-0.3 * l)
            g = self.gcols.t
            gk = self.gcols.all()
            self.ts("dve", self.g32.t[:, l * 24:(l + 1) * 24], g[:, l * 29:l * 29 + 24], 32.0, ALU.mult, gk, self.g32.all())
            gd = self.gd.t
            gdk = self.gd.all()
            o = l * 8
            self.cp("dve", gd[:, o + 0:o + 1], g[:, l * 29 + 24:l * 29 + 25], gk, gdk)
            self.ts("dve", gd[:, o + 1:o + 2], g[:, l * 29 + 25:l * 29 + 26], 8.0, ALU.mult, gk, gdk)
            self.cp("dve", gd[:, o + 2:o + 3], g[:, l * 29 + 26:l * 29 + 27], gk, gdk)
            self.ts("dve", gd[:, o + 3:o + 4], g[:, l * 29 + 27:l * 29 + 28], 8.0, ALU.mult, gk, gdk)
            self.ts("dve", gd[:, o + 4:o + 5], g[:, l * 29 + 28:l * 29 + 29],
                    float(math.sqrt(128.0) * (1.0 - lam_init)), ALU.mult, gk, gdk)
            t = self.tmpf[0]
            lb = lamb.t
            lbk = lamb.k(0, NL * 1024)
            b0 = l * 256
            self.tt("dve", t.t[:, 0:64], lb[:, b0:b0 + 64], lb[:, b0 + 64:b0 + 128], ALU.mult, lbk, t.all())
            self.tt("dve", t.t[:, 64:128], lb[:, b0 + 128:b0 + 192], lb[:, b0 + 192:b0 + 256], ALU.mult, lbk, t.all())
            t2 = self.tmpf[1]
            self.S.add("dve", lambda e, t=t, t2=t2: e.reduce_sum(out=t2.t[:, 0:1], in_=t.t[:, 0:64], axis=AX.X), t.all(), t2.all())
            self.S.add("dve", lambda e, t=t, t2=t2: e.reduce_sum(out=t2.t[:, 1:2], in_=t.t[:, 64:128], axis=AX.X), t.all(), t2.all())
            self.act(t2.t[:, 2:4], t2.t[:, 0:2], AF.Exp, t2.all(), t2.all())
            self.tt("dve", t2.t[:, 4:5], t2.t[:, 3:4], t2.t[:, 2:3], ALU.subtract, t2.all(), t2.all())
            self.ts("dve", gd[:, o + 5:o + 6], t2.t[:, 4:5], float(-lam_init), ALU.add, t2.all(), gdk)
        self.cv_rr = 0
        if self.has_seg:
            dma("sp", self.wsel.t[:], self.wsel_d, [], self.wsel.all())
            for k in range(2):
                for h in range(8):
                    i = self.cv_rr % 2
                    self.cv_rr += 1
                    sf, sbb = self.stg_f[i], self.stg_b[i]
                    dma("sp", sf.t[:, 0:2048], self.segtab_d[k, h], [], sf.k(0, 8192))
                    self.act(sbb.t[:, 0:2048], sf.t[:, 0:2048], AF.Exp, sf.k(0, 8192), sbb.k(0, 4096))
                    dma("pool", self.segtab_b[k, h], sbb.t[:, 0:2048], sbb.k(0, 4096), [("segtabb", k, h)])
        for l in range(NL):
            for f in range(2):
                W = self.w_ffn_wi[f]
                for j in range(NJ):
                    srcs = [(W[l, :, j * 128:(j + 1) * 128].rearrange("(kc p) n -> p kc n", p=128), 0, 128),
                            (W[l, :, DFF + j * 128:DFF + (j + 1) * 128].rearrange("(kc p) n -> p kc n", p=128), 128, 128)]
                    self.convert(srcs, 8, 256, self.wi_b[l][f][j], ("wib", l, f, j))
                W = self.w_ffn_wo[f]
                for n in range(8):
                    srcs = [(W[l, :, n * 128:(n + 1) * 128].rearrange("(kc p) n -> p kc n", p=128), 0, 128)]
                    self.convert(srcs, NJ, 128, self.wo_b[l][f][n], ("wob", l, f, n))
            for g in range(20):
                srcs = [(self.w_in[l, :, g * 256:(g + 1) * 256].rearrange("(kc p) n -> p kc n", p=128), 0, 256)]
                self.convert(srcs, 8, 256, self.win_b[l][g], ("winb", l, g))
            for g in range(4):
                srcs = [(self.w_a[l, :, g * 256:(g + 1) * 256].rearrange("(kc p) n -> p kc n", p=128), 0, 256)]
                self.convert(srcs, 4, 256, self.wa_b[l][g], ("wab", l, g))
                srcs = [(self.w_b[l, :, g * 256:(g + 1) * 256].rearrange("(kc p) n -> p kc n", p=128), 0, 256)]
                self.convert(srcs, 4, 256, self.wb_b[l][g], ("wbb", l, g))
                srcs = [(self.w_o[l, :, g * 256:(g + 1) * 256].rearrange("(kc p) n -> p kc n", p=128), 0, 256)]
                self.convert(srcs, 8, 256, self.wout_b[l][g], ("woutb", l, g))
            for h in range(8):
                i = self.cv_rr % 2
                self.cv_rr += 1
                sf, sbb = self.stg_f[i], self.stg_b[i]
                dma("sp", sf.t[:, 0:1472], self.tabI_d[l, h], [], sf.k(0, 1472 * 4))
                self.act(sbb.t[:, 0:1472], sf.t[:, 0:1472], AF.Exp, sf.k(0, 1472 * 4), sbb.k(0, 1472 * 2))
                dma("pool", self.eint_b[l, :, h, :], sbb.t[:, 0:1472], sbb.k(0, 1472 * 2), [("eintb", l)])
                i = self.cv_rr % 2
                self.cv_rr += 1
                sf, sbb = self.stg_f[i], self.stg_b[i]
                dma("sp", sf.t[:, 0:1024], self.tabF_d[l, h], [], sf.k(0, 4096))
                self.act(sbb.t[:, 0:1024], sf.t[:, 0:1024], AF.Exp, sf.k(0, 4096), sbb.k(0, 2048))
                dma("pool", self.efull_b[l, h], sbb.t[:, 0:1024], sbb.k(0, 2048), [("efullb", l, h)])

    def convert(self, srcs, kc, n, dst, key):
        i = self.cv_rr % 2
        self.cv_rr += 1
        sf, sbb = self.stg_f[i], self.stg_b[i]
        tot = kc * n
        fv = sf.t[:, 0:tot].rearrange("p (k n) -> p k n", n=n)
        for (src, c0, w) in srcs:
            self.dma("sp", fv[:, :, c0:c0 + w], src, [], sf.k(0, tot * 4))
        eng = ("dve", "pool", "act")[self.cv_rr % 3]
        self.cp(eng, sbb.t[:, 0:tot], sf.t[:, 0:tot], sf.k(0, tot * 4), sbb.k(0, tot * 2))
        self.dma("pool", dst.rearrange("p k n -> p (k n)"), sbb.t[:, 0:tot], sbb.k(0, tot * 2), [key])

    def wload(self, src_ap, kc, n, key):
        ws = self.wslot[self.ws_rr % 4]
        self.ws_rr += 1
        tot = kc * n
        self.dma("sp", ws.t[:, 0:tot], src_ap.rearrange("p k n -> p (k n)"), [key], ws.k(0, tot * 2))
        return ws.t[:, 0:tot].rearrange("p (k n) -> p k n", n=n), ws.k(0, tot * 2)

    def rmsnorm(self, l, which):
        xT, uT = self.xT, self.uT
        pb, pk = self.bank()
        for c in range(8):
            sq = self.sq[c % 2]
            self.tt("dve" if c % 2 == 0 else "pool", sq.t[:], xT.t[:, c, :], xT.t[:, c, :], ALU.mult, xT.ck(c), sq.all())
            self.mm(pb[:], self.ones_b.t[:], sq.t[:], c == 0, c == 7, sq.all() + self.ones_b.all(), [pk])
        r = self.rstd[0]
        self.rsqrt_act(r.t[:], pb[:], float(1024 * NORM_EPS), [pk], r.all())
        gc = l * 24 + which * 8
        for c in range(8):
            self.stt("dve", uT.t[:, c, :], xT.t[:, c, :], self.g32.t[:, gc + c:gc + c + 1], r.t[:], ALU.mult, ALU.mult,
                     xT.ck(c) + self.g32.all() + r.all(), uT.ck(c))

    def ffn(self, l, f):
        xT, uT, hT = self.xT, self.uT, self.hT
        for j in range(NJ):
            W, wk = self.wload(self.wi_b[l][f][j], 8, 256, ("wib", l, f, j))
            pg, pgk = self.bank()
            pu, puk = self.bank()
            for kc in range(8):
                self.mm(pg[:], W[:, kc, 0:128], uT.t[:, kc, :], kc == 0, kc == 7, wk + uT.ck(kc), [pgk])
            for kc in range(8):
                self.mm(pu[:], W[:, kc, 128:256], uT.t[:, kc, :], kc == 0, kc == 7, wk + uT.ck(kc), [puk])
            t = self.tmpf[j % 2]
            t2 = self.tmpf[2 + j % 2]
            self.act(t.t[:], pg[:], AF.Silu, [pgk], t.all())
            self.cp("act", t2.t[:], pu[:], [puk], t2.all())
            self.tt("pool", hT.t[:, j, :], t.t[:], t2.t[:], ALU.mult, t.all() + t2.all(), hT.ck(j))
        for n in range(8):
            W, wk = self.wload(self.wo_b[l][f][n], NJ, 128, ("wob", l, f, n))
            pb, pk = self.bank()
            for j in range(NJ):
                self.mm(pb[:], W[:, j, :], hT.t[:, j, :], j == 0, j == NJ - 1, wk + hT.ck(j), [pk])
            self.stt("dve", xT.t[:, n, :], pb[:], 0.5, xT.t[:, n, :], ALU.mult, ALU.add, [pk] + xT.ck(n), xT.ck(n))

    def seq_layer(self, si, iname, oname, L, row0, l, mode="full"):
        NT = L // 512
        dma = self.dma
        dma("sp", self.eint.t[:], self.eint_b[l], [("eintb", l)], self.eint.all())
        NC = L // 128
        for pr in range(4):
            dma("pool", self.kaT[pr, :, 0:256], self.zero_b.t[:, 0:256], self.zero_b.all(), [("kaT", -1)])
            dma("pool", self.kaT[pr, :, 256 + L:512 + L], self.zero_b.t[:, 0:256], self.zero_b.all(), [("kaT", NT)])
        dma("pool", self.va[:, 0:2, :], self.zero_b.t[:].rearrange("p (c n) -> p c n", n=512), self.zero_b.all(), [("va", -1)])
        dma("pool", self.va[:, NC + 2:NC + 4, :], self.zero_b.t[:].rearrange("p (c n) -> p c n", n=512), self.zero_b.all(), [("va", NT)])
        for t in range(NT):
            self.phase_a(iname, L, row0, l, t)
        if mode == "seg" and l == self.NL - 1:
            self.select_segment(L)
            for slot in range(4):
                self.phase_bc(oname, L, 0, l, slot, slot=slot)
        else:
            for t in range(NT):
                self.phase_bc(oname, L, row0, l, t)

    def select_segment(self, L):
        NT = L // 512
        ncand = L // 2048
        rr = 0
        for slot in range(4):
            def win_keys(name, t):
                return [(name, tt_) for tt_ in (t - 1, t, t + 1) if -1 <= tt_ <= NT]
            kinds = [
                ("qaT", True, 4, 512, lambda t: self.qaT[:, :, t * 512:(t + 1) * 512].rearrange("c p n -> p c n"),
                 lambda t: [("qaT", t)], self.sel_qaT[:, :, slot * 512:(slot + 1) * 512].rearrange("c p n -> p c n")),
                ("qbT", True, 4, 512, lambda t: self.qbT[:, :, t * 512:(t + 1) * 512].rearrange("c p n -> p c n"),
                 lambda t: [("qbT", t)], self.sel_qbT[:, :, slot * 512:(slot + 1) * 512].rearrange("c p n -> p c n")),
                ("kwin", True, 4, 1024, lambda t: self.kaT[:, :, 512 * t:512 * t + 1024].rearrange("c p n -> p c n"),
                 lambda t: win_keys("kaT", t), self.sel_kwin[slot].rearrange("c p n -> p c n")),
                ("vwin", True, 8, 512, lambda t: self.va[:, 4 * t:4 * t + 8, :],
                 lambda t: win_keys("va", t), self.sel_vwin[slot]),
                ("gA", False, 8, 512, lambda t: self.gA[:, :, t * 512:(t + 1) * 512].rearrange("c p n -> p c n"),
                 lambda t: [("ga", t)], self.sel_gA[:, :, slot * 512:(slot + 1) * 512].rearrange("c p n -> p c n")),
                ("gB", False, 8, 512, lambda t: self.gB[:, :, t * 512:(t + 1) * 512].rearrange("c p n -> p c n"),
                 lambda t: [("gb", t)], self.sel_gB[:, :, slot * 512:(slot + 1) * 512].rearrange("c p n -> p c n")),
                ("xmid", False, 8, 512, lambda t: self.xmid[:, :, t * 512:(t + 1) * 512].rearrange("c p n -> p c n"),
                 lambda t: [("xmid", t)], self.sel_xmid[:, :, slot * 512:(slot + 1) * 512].rearrange("c p n -> p c n")),
            ]
            for (name, half, nch, nn, srcf, keyf, dst) in kinds:
                bufs = self.selH if half else self.selF
                esz = 2 if half else 4
                tot = nch * nn
                acc = bufs[2]
                for cnd in range(ncand):
                    t = cnd * 4 + slot
                    stg = bufs[rr % 2]
                    rr += 1
                    self.dma("sp", stg.t[:, 0:tot].rearrange("p (c n) -> p c n", n=nn), srcf(t), keyf(t), stg.k(0, tot * esz))
                    eng = "pool" if cnd == 0 else "dve"
                    wcol = self.wsel.t[:, cnd:cnd + 1]
                    if cnd == 0:
                        self.ts(eng, acc.t[:, 0:tot], stg.t[:, 0:tot], wcol, ALU.mult,
                                stg.k(0, tot * esz) + self.wsel.all(), acc.k(0, tot * esz))
                    else:
                        self.stt(eng, acc.t[:, 0:tot], stg.t[:, 0:tot], wcol, acc.t[:, 0:tot], ALU.mult, ALU.add,
                                 stg.k(0, tot * esz) + self.wsel.all() + acc.k(0, tot * esz), acc.k(0, tot * esz))
                self.dma("pool", dst, acc.t[:, 0:tot].rearrange("p (c n) -> p c n", n=nn), acc.k(0, tot * esz), [("sel", name, slot)])

    def phase_a(self, iname, L, row0, l, t):
        dma = self.dma
        xT, uT = self.xT, self.uT
        tok = slice(t * 512, (t + 1) * 512)
        if l == 0:
            xin = self.din[iname]
            for b in range(4):
                r0 = row0 + t * 512 + b * 128
                dma("sp", self.xtok.t[:, b, :], xin[r0:r0 + 128, :], [], self.xtok.ck(b))
            for c in range(8):
                pb, pk = self.bank()
                for b in range(4):
                    self.S.add("pe", lambda e, pb=pb, b=b, c=c: e.transpose(out=pb[:, b * 128:(b + 1) * 128],
                                                                               in_=self.xtok.t[:, b, c * 128:(c + 1) * 128],
                                                                               identity=self.ident.t[:]),
                               self.xtok.ck(b) + self.ident.all(), [pk])
                self.cp("act" if c % 2 == 0 else "dve", xT.t[:, c, :], pb[:], [pk], xT.ck(c))
        else:
            dma("sp", xT.t[:], self.x1[:, :, tok].rearrange("c p n -> p c n"), [("x1", t)], xT.all())
        dma("sp", self.cos_sb.t[:], self.cos_d[:, tok], [], self.cos_sb.all())
        dma("sp", self.sin_sb.t[:], self.sin_d[:, tok], [], self.sin_sb.all())
        self.rmsnorm(l, 0)
        self.ffn(l, 0)
        dma("pool", self.xmid[:, :, tok].rearrange("c p n -> p c n"), xT.t[:], xT.all(), [("xmid", t)])
        self.rmsnorm(l, 1)
        gdo = l * 8
        LAG = 2
        pending = []
        tick = [0]

        def run_due(force=False):
            for job in pending:
                if job[1] and (force or job[0] <= tick[0]):
                    job[1].pop(0)()
                    job[0] = tick[0] + LAG
            pending[:] = [j for j in pending if j[1]]

        def do_tick():
            tick[0] += 1
            run_due()

        kq = 0
        for g in range(20):
            W, wk = self.wload(self.win_b[l][g], 8, 256, ("winb", l, g))
            kind = ("qa", "qa", "ka", "ka", "va", "va", "qb", "qb", "kb", "kb", "vb", "vb",
                    "ga", "ga", "ga", "ga", "gb", "gb", "gb", "gb")[g]
            if kind in ("va", "vb"):
                half = g % 2
                for bp in range(2):
                    pb, pk = self.bank()
                    for bb in range(2):
                        b = bp * 2 + bb
                        for kc in range(8):
                            self.mm(pb[:, bb * 256:(bb + 1) * 256], uT.t[:, kc, b * 128:(b + 1) * 128], W[:, kc, :],
                                    kc == 0, kc == 7, wk + uT.ck(kc), [pk])
                    for bb in range(2):
                        b = bp * 2 + bb
                        if kind == "va":
                            self.cp("act", self.vstA.t[:, b, half * 256:(half + 1) * 256], pb[:, bb * 256:(bb + 1) * 256],
                                    [pk], self.vstA.all())
                        else:
                            self.cp("act", self.vstB.t[:, 2 * half:2 * half + 2, b, :],
                                    pb[:, bb * 256:(bb + 1) * 256].rearrange("p (h e) -> p h e", e=128),
                                    [pk], self.vstB.all())
                    do_tick()
                if half == 1:
                    if kind == "va":
                        dma("pool", self.va[:, 2 + 4 * t:2 + 4 * t + 4, :], self.vstA.t[:], self.vstA.all(), [("va", t)])
                    else:
                        dma("pool", self.vb[:, :, 4 * t:4 * t + 4, :].rearrange("h p c e -> p h c e"), self.vstB.t[:],
                            self.vstB.all(), [("vb", t)])
                continue
            for cc in range(2):
                ch = g * 2 + cc
                pb, pk = self.bank()
                for kc in range(8):
                    self.mm(pb[:], W[:, kc, cc * 128:(cc + 1) * 128], uT.t[:, kc, :], kc == 0, kc == 7, wk + uT.ck(kc), [pk])
                if kind in ("ga", "gb"):
                    gi = ch - 24 if kind == "ga" else ch - 32
                    gs = self.gst[(gi // 2) % 2]
                    self.act(gs.t[:, gi % 2, :], pb[:], AF.Sigmoid, [pk], gs.ck(gi % 2))
                    if gi % 2 == 1:
                        dst = self.gA if kind == "ga" else self.gB
                        c0 = gi - 1
                        dma("pool", dst[c0:c0 + 2, :, tok].rearrange("c p n -> p c n"), gs.t[:], gs.all(), [(kind, t)])
                    do_tick()
                    continue
                ci = ch % 4 if kind in ("qa", "ka") else (ch - 12) % 4
                if ci == 0:
                    kq += 1
                col = {"qa": 0, "ka": 1, "qb": 2, "kb": 3}[kind]
                gcol = self.gd.t[:, gdo + col:gdo + col + 1]
                jq = self.jq
                self.jq += 1
                qf = self.qfD[jq % 4]
                sq = self.sqD[jq % 3]
                r = self.rstdD[jq % 3]
                qn = self.qnD[jq % 3]
                qb16 = self.qnbD[jq % 3]
                t1 = self.t1D[jq % 2]
                stg = self.stageD[kq % 2]
                self.cp("act", qf.t[:], pb[:], [pk], qf.all())
                self.tt("pool", sq.t[:], qf.t[:], qf.t[:], ALU.mult, qf.all(), sq.all())

                def store(kind=kind, stg=stg):
                    if kind == "ka":
                        dma("pool", self.kaT[:, :, 256 + t * 512:256 + (t + 1) * 512].rearrange("c p n -> p c n"),
                            stg.t[:], stg.all(), [("kaT", t)])
                    else:
                        dst = {"qa": self.qaT, "qb": self.qbT, "kb": self.kbT}[kind]
                        dma("pool", dst[:, :, tok].rearrange("c p n -> p c n"), stg.t[:], stg.all(),
                            [({"qa": "qaT", "qb": "qbT", "kb": "kbT"}[kind], t)])

                def s2(kind=kind, ci=ci, gcol=gcol, qf=qf, sq=sq, r=r, qn=qn, qb16=qb16, stg=stg, store=store):
                    p2, p2k = self.bank()
                    self.mm(p2[:], self.blk_b.t[:], sq.t[:], True, True, sq.all() + self.blk_b.all(), [p2k])
                    self.rsqrt_act(r.t[:], p2[:], float(64 * NORM_EPS), [p2k], r.all())
                    if kind in ("qa", "ka"):
                        self.stt("dve", stg.t[:, ci, :], qf.t[:], gcol, r.t[:], ALU.mult, ALU.mult,
                                 qf.all() + self.gd.all() + r.all(), stg.ck(ci))
                        if ci == 3:
                            store()
                    else:
                        self.stt("dve", qn.t[:], qf.t[:], gcol, r.t[:], ALU.mult, ALU.mult,
                                 qf.all() + self.gd.all() + r.all(), qn.all())
                        self.cp("act", qb16.t[:], qn.t[:], qn.all(), qb16.all())

                def s3(ci=ci, qn=qn, qb16=qb16, t1=t1, stg=stg, store=store):
                    p3, p3k = self.bank()
                    self.mm(p3[:], self.rot_b.t[:], qb16.t[:], True, True, qb16.all() + self.rot_b.all(), [p3k])
                    self.tt("pool", t1.t[:], qn.t[:], self.cos_sb.t[:], ALU.mult, qn.all() + self.cos_sb.all(), t1.all())
                    self.tt("dve", qn.t[:], p3[:], self.sin_sb.t[:], ALU.mult, [p3k] + self.sin_sb.all() + qn.all(), qn.all())
                    self.tt("dve", stg.t[:, ci, :], t1.t[:], qn.t[:], ALU.add, t1.all() + qn.all(), stg.ck(ci))
                    if ci == 3:
                        store()

                pending.append([tick[0] + LAG, [s2] if kind in ("qa", "ka") else [s2, s3]])
                do_tick()
        while pending:
            tick[0] += LAG
            run_due(force=True)

    def phase_bc(self, oname, L, row0, l, t, slot=None):
        dma = self.dma
        NT = L // 512
        tok = slice(t * 512, (t + 1) * 512)
        gdo = l * 8
        if slot is None:
            s_qa = (self.qaT[:, :, tok].rearrange("c p n -> p c n"), [("qaT", t)])
            s_kw = (self.kaT[:, :, 512 * t:512 * t + 1024].rearrange("c p n -> p c n"),
                    [("kaT", tt_) for tt_ in (t - 1, t, t + 1) if -1 <= tt_ <= NT])
            s_vw = (self.va[:, 4 * t:4 * t + 8, :], [("va", tt_) for tt_ in (t - 1, t, t + 1) if -1 <= tt_ <= NT])
            s_qb = (self.qbT[:, :, tok].rearrange("c p n -> p c n"), [("qbT", t)])
            s_xm = (self.xmid[:, :, tok].rearrange("c p n -> p c n"), [("xmid", t)])
            s_ga = (self.gA[:, :, tok].rearrange("c p n -> p c n"), [("ga", t)])
            s_gb = (self.gB[:, :, tok].rearrange("c p n -> p c n"), [("gb", t)])
            edge = "top" if t == 0 else ("bot" if t == NT - 1 else None)
        else:
            s_qa = (self.sel_qaT[:, :, tok].rearrange("c p n -> p c n"), [("sel", "qaT", slot)])
            s_kw = (self.sel_kwin[slot].rearrange("c p n -> p c n"), [("sel", "kwin", slot)])
            s_vw = (self.sel_vwin[slot], [("sel", "vwin", slot)])
            s_qb = (self.sel_qbT[:, :, tok].rearrange("c p n -> p c n"), [("sel", "qbT", slot)])
            s_xm = (self.sel_xmid[:, :, tok].rearrange("c p n -> p c n"), [("sel", "xmid", slot)])
            s_ga = (self.sel_gA[:, :, tok].rearrange("c p n -> p c n"), [("sel", "gA", slot)])
            s_gb = (self.sel_gB[:, :, tok].rearrange("c p n -> p c n"), [("sel", "gB", slot)])
            edge = "segtop" if slot == 0 else ("segbot" if slot == 3 else None)
        dma("sp", self.qa_sb.t[:], s_qa[0], s_qa[1], self.qa_sb.all())
        dma("sp", self.kwin.t[:], s_kw[0], s_kw[1], self.kwin.all())
        dma("sp", self.vwin.t[:], s_vw[0], s_vw[1], self.vwin.all())
        if t == 0 and l == 0 and not getattr(self, "_vaug_init", False):
            self._vaug_init = True
        self.memset("pool", self.vaug.t[:], 0.0, [], self.vaug.all())
        for par in range(2):
            self.S.add("pool", lambda e, par=par: e.tensor_copy(
                out=self.vaug.t[:, :, :, par * 64:(par + 1) * 64].rearrange("p c (hp two) d -> p c hp two d", two=2)[:, :, :, par, :],
                in_=self.vwin.t[:].rearrange("p c (hp two d) -> p c hp two d", two=2, d=64)[:, :, :, par, :]),
                self.vwin.all(), self.vaug.all())
        dma("sp", self.qb_sb.t[:], s_qb[0], s_qb[1], self.qb_sb.all())
        NB = (L + 2047) // 2048
        KB = min(L, 2048)
        CPB = KB // 128
        si = 0
        for h in range(4):
            pO = [self.bank(), self.bank()]
            pL = [self.bank(), self.bank()]
            items = [(b, c) for b in range(NB) for c in range(CPB)]
            pend = None
            kv = {}
            nI = len(items)

            def load_block(b):
                ks = self.kst[(si + b) % 2]
                vs = self.vst[(si + b) % 2]
                kkeys = [("kbT", tt_) for tt_ in range(b * KB // 512, (b + 1) * KB // 512)]
                vkeys = [("vb", tt_) for tt_ in range(b * KB // 512, (b + 1) * KB // 512)]
                dma("sp", ks.t[:, 0:KB], self.kbT[h, :, b * KB:(b + 1) * KB], kkeys, ks.k(0, KB * 2))
                dma("sp", vs.t[:, 0:CPB, :], self.vb[h, :, b * CPB:(b + 1) * CPB, :], vkeys, vs.k(0, CPB * 256))
                return ks, vs

            def qk(idx):
                b, c = items[idx]
                if c == 0:
                    kv[b] = load_block(b)
                ks, vs = kv[b]
                out = []
                for m in range(2):
                    psb, psk = self.bank3()
                    self.mm(psb[:], ks.t[m * 64:(m + 1) * 64, c * 128:(c + 1) * 128],
                            self.qb_sb.t[m * 64:(m + 1) * 64, h, :], True, True,
                            ks.k(c * 256, (c + 1) * 256) + self.qb_sb.ck(h), [psk])
                    out.append((psb, psk))
                return out

            self._b3 = 0
            self._b3banks = [self.bank() for _ in range(4)]
            cur = qk(0)
            for idx in range(nI):
                nxt = qk(idx + 1) if idx + 1 < nI else None
                b, c = items[idx]
                ks, vs = kv[b]
                for m in range(2):
                    psb, psk = cur[m]
                    P = self.P[(2 * idx + m) % 4]
                    self.act(P.t[:], psb[:], AF.Exp, [psk], P.all())
                    self.mm(pO[m][0][:], vs.t[:, c, :], P.t[:], idx == 0, idx == nI - 1, vs.k(c * 256, (c + 1) * 256) + P.all(), [pO[m][1]])
                    self.mm(pL[m][0][:], self.ones_b.t[:], P.t[:], idx == 0, idx == nI - 1, self.ones_b.all() + P.all(), [pL[m][1]])
                cur = nxt
            si += NB
            r0, r1, t0, t1 = self.ef
            self.recip_act(r0.t[:], pL[0][0][:], [pL[0][1]], r0.all())
            self.recip_act(r1.t[:], pL[1][0][:], [pL[1][1]], r1.all())
            self.tt("dve", t0.t[:], pO[0][0][:], r0.t[:], ALU.mult, [pO[0][1]] + r0.all(), t0.all())
            self.stt("dve", t1.t[:], pO[1][0][:], self.gd.t[:, gdo + 5:gdo + 6], r1.t[:], ALU.mult, ALU.mult,
                     [pO[1][1]] + self.gd.all() + r1.all(), t1.all())
            self.tt("dve", t0.t[:], t0.t[:], t1.t[:], ALU.add, t0.all() + t1.all(), t0.all())
            sq = self.P[0]
            self.tt("pool", sq.t[:], t0.t[:], t0.t[:], ALU.mult, t0.all(), sq.all())
            p2, p2k = self.bank()
            self.mm(p2[:], self.ones_b.t[:], sq.t[:], True, True, sq.all() + self.ones_b.all(), [p2k])
            self.rsqrt_act(r0.t[:], p2[:], float(128 * SUBLN_EPS), [p2k], r0.all())
            self.stt("dve", self.BT.t[:, h, :], t0.t[:], self.gd.t[:, gdo + 4:gdo + 5], r0.t[:], ALU.mult, ALU.mult,
                     t0.all() + self.gd.all() + r0.all(), self.BT.ck(h))
        pi = 0
        for pr in range(4):
            nb = [self.bank() for _ in range(8)]
            po, pok = nb[0]
            pl, plk = nb[1]
            efbs = {}
            for hh in (2 * pr, 2 * pr + 1):
                if edge in ("top", "bot"):
                    efb = self.efull[hh % 2]
                    dma("sp", efb.t[:, 0:1024], self.efull_b[l, hh], [("efullb", l, hh)], efb.all())
                    efbs[hh] = efb
                elif edge is not None:
                    efb = self.efull[hh % 2]
                    kk = 0 if edge == "segtop" else 1
                    dma("sp", efb.t[:], self.segtab_b[kk, hh], [("segtabb", kk, hh)], efb.all())
                    efbs[hh] = efb
            items = [(hh, c) for hh in (2 * pr, 2 * pr + 1) for c in range(8)]
            nI = len(items)
            NLAG = 3

            def na_qk(i):
                hh, c = items[i]
                par = hh % 2
                psb, psk = nb[2 + i % 6]
                self.mm(psb[:], self.kwin.t[par * 64:(par + 1) * 64, pr, c * 128:(c + 1) * 128],
                        self.qa_sb.t[par * 64:(par + 1) * 64, pr, :], True, True,
                        self.kwin.ck(pr) + self.qa_sb.ck(pr), [psk])

            for i in range(min(NLAG, nI)):
                na_qk(i)
            for i in range(nI):
                if i + NLAG < nI:
                    na_qk(i + NLAG)
                hh, c = items[i]
                par = hh % 2
                psb, psk = nb[2 + i % 6]
                efb = efbs.get(hh)
                if True:
                    P = self.P[pi % 4]
                    pi += 1
                    self.act(P.t[:], psb[:], AF.Exp, [psk], P.all())
                    ei = self.eint
                    s0 = (14 - 2 * c) * 64
                    if edge is None:
                        self.tt("dve", P.t[:], P.t[:], ei.t[:, hh, s0:s0 + 512], ALU.mult, P.all() + ei.ck(hh), P.all())
                    elif edge == "segtop":
                        self.tt("dve", P.t[:, 0:256], P.t[:, 0:256], efb.t[:, c * 256:(c + 1) * 256], ALU.mult,
                                P.all() + efb.all(), P.all())
                        self.tt("dve", P.t[:, 256:512], P.t[:, 256:512], ei.t[:, hh, s0 + 256:s0 + 512], ALU.mult,
                                P.all() + ei.ck(hh), P.all())
                    elif edge == "segbot":
                        self.tt("dve", P.t[:, 0:256], P.t[:, 0:256], ei.t[:, hh, s0:s0 + 256], ALU.mult,
                                P.all() + ei.ck(hh), P.all())
                        self.tt("dve", P.t[:, 256:512], P.t[:, 256:512], efb.t[:, c * 256:(c + 1) * 256], ALU.mult,
                                P.all() + efb.all(), P.all())
                    else:
                        valid = 2 <= c <= 5
                        f0 = (11 - 2 * c) * 64
                        if edge == "top":
                            if valid:
                                self.tt("dve", P.t[:, 0:256], P.t[:, 0:256], efb.t[:, f0:f0 + 256], ALU.mult,
                                        P.all() + efb.all(), P.all())
                            else:
                                self.memset("dve", P.t[:, 0:256], 0.0, P.all(), P.all())
                            self.tt("dve", P.t[:, 256:512], P.t[:, 256:512], ei.t[:, hh, s0 + 256:s0 + 512], ALU.mult,
                                    P.all() + ei.ck(hh), P.all())
                        else:
                            self.tt("dve", P.t[:, 0:256], P.t[:, 0:256], ei.t[:, hh, s0:s0 + 256], ALU.mult,
                                    P.all() + ei.ck(hh), P.all())
                            if valid:
                                self.tt("dve", P.t[:, 256:512], P.t[:, 256:512], efb.t[:, f0 + 256:f0 + 512], ALU.mult,
                                        P.all() + efb.all(), P.all())
                            else:
                                self.memset("dve", P.t[:, 256:512], 0.0, P.all(), P.all())
                    self.mm(po[:], self.vaug.t[:, c, hh, :], P.t[:], i == 0, i == nI - 1, self.vaug.all() + P.all(), [pok])
                    self.mm(pl[:], self.half_b[par].t[:], P.t[:], i == 0, i == nI - 1, self.half_b[par].all() + P.all(), [plk])
            rl = self.ef[pr % 2]
            self.recip_act(rl.t[:], pl[:], [plk], rl.all())
            self.tt("dve", self.AT.t[:, pr, :], po[:], rl.t[:], ALU.mult, [pok] + rl.all(), self.AT.ck(pr))
        xT = self.xT
        dma("sp", xT.t[:], s_xm[0], s_xm[1], xT.all())
        dma("sp", self.gA_sb.t[:], s_ga[0], s_ga[1], self.gA_sb.all())
        dma("sp", self.gB_sb.t[:], s_gb[0], s_gb[1], self.gB_sb.all())
        for g in range(4):
            Wa, wak = self.wload(self.wa_b[l][g], 4, 256, ("wab", l, g))
            Wb, wbk = self.wload(self.wb_b[l][g], 4, 256, ("wbb", l, g))
            for cc in range(2):
                n = g * 2 + cc
                pa, pak = self.bank()
                pb, pbk = self.bank()
                for j in range(4):
                    self.mm(pa[:], Wa[:, j, cc * 128:(cc + 1) * 128], self.AT.t[:, j, :], j == 0, j == 3, wak + self.AT.ck(j), [pak])
                for j in range(4):
                    self.mm(pb[:], Wb[:, j, cc * 128:(cc + 1) * 128], self.BT.t[:, j, :], j == 0, j == 3, wbk + self.BT.ck(j), [pbk])
                t1 = self.tmpf[0]
                t2 = self.tmpf[1]
                self.tt("dve", t1.t[:], pa[:], self.gA_sb.t[:, n, :], ALU.mult, [pak] + self.gA_sb.ck(n), t1.all())
                self.tt("dve", t2.t[:], pb[:], self.gB_sb.t[:, n, :], ALU.mult, [pbk] + self.gB_sb.ck(n), t2.all())
                self.tt("pool", self.mT.t[:, n, :], t1.t[:], t2.t[:], ALU.add, t1.all() + t2.all(), self.mT.ck(n))
        for g in range(4):
            W, wk = self.wload(self.wout_b[l][g], 8, 256, ("woutb", l, g))
            for cc in range(2):
                n = g * 2 + cc
                pb, pk = self.bank()
                for kc in range(8):
                    self.mm(pb[:], W[:, kc, cc * 128:(cc + 1) * 128], self.mT.t[:, kc, :], kc == 0, kc == 7, wk + self.mT.ck(kc), [pk])
                self.tt("dve", xT.t[:, n, :], pb[:], xT.t[:, n, :], ALU.add, [pk] + xT.ck(n), xT.ck(n))
        self.rmsnorm(l, 2)
        self.ffn(l, 1)
        if l < self.NL - 1:
            dma("pool", self.x1[:, :, tok].rearrange("c p n -> p c n"), xT.t[:], xT.all(), [("x1", t)])
        else:
            yout = self.dout[oname]
            for b in range(4):
                for hc in range(2):
                    pb, pk = self.bank()
                    for cq in range(4):
                        c = hc * 4 + cq
                        self.S.add("pe", lambda e, pb=pb, b=b, c=c, cq=cq: e.transpose(
                            out=pb[:, cq * 128:(cq + 1) * 128], in_=xT.t[:, c, b * 128:(b + 1) * 128], identity=self.ident.t[:]),
                            xT.ck(c) + self.ident.all(), [pk])
                    self.cp("act" if hc == 0 else "dve", self.xtok.t[:, b, hc * 512:(hc + 1) * 512], pb[:], [pk], self.xtok.ck(b))
                r0 = row0 + t * 512 + b * 128
                dma("pool", yout[r0:r0 + 128, :], self.xtok.t[:, b, :], self.xtok.ck(b), [("y", oname, r0)])

    def bank3(self):
        b = self._b3banks[self._b3 % 4]
        self._b3 += 1
        return b

    def emit(self):
        nc = self.nc
        S = self.S
        with contextlib.ExitStack() as st:
            S.finalize(nc, st)
            block = st.enter_context(nc.Block())

            @block.sync
            def _(e):
                S.run_engine("sp", e, final_wait=True)

            @block.scalar
            def _(e):
                S.run_engine("act", e)

            @block.vector
            def _(e):
                S.run_engine("dve", e)

            @block.gpsimd
            def _(e):
                S.run_engine("pool", e)

            @block.tensor
            def _(e):
                S.run_engine("pe", e)


def host_tables(inputs, n_layers, lmax):
    gc = np.zeros((128, n_layers * 29), np.float32)
    lamv = np.zeros((n_layers * 256,), np.float32)
    tabI = np.zeros((n_layers, 8, 128, 1472), np.float32)
    tabF = np.zeros((n_layers, 8, 128, 1024), np.float32)
    for l in range(n_layers):
        o = l * 29
        gc[:, o + 0:o + 8] = np.asarray(inputs["ffn1_norm"][l], np.float32).reshape(8, 128).T
        gc[:, o + 8:o + 16] = np.asarray(inputs["mix_norm"][l], np.float32).reshape(8, 128).T
        gc[:, o + 16:o + 24] = np.asarray(inputs["ffn2_norm"][l], np.float32).reshape(8, 128).T
        gc[:, o + 24] = np.tile(np.asarray(inputs["qa_norm"][l], np.float32), 2)
        gc[:, o + 25] = np.tile(np.asarray(inputs["ka_norm"][l], np.float32), 2)
        gc[:, o + 26] = np.tile(np.asarray(inputs["qb_norm"][l], np.float32), 2)
        gc[:, o + 27] = np.tile(np.asarray(inputs["kb_norm"][l], np.float32), 2)
        gc[:, o + 28] = np.asarray(inputs["subln"][l], np.float32)
        for i, nm in enumerate(("lam_q1", "lam_k1", "lam_q2", "lam_k2")):
            lamv[l * 256 + i * 64:l * 256 + (i + 1) * 64] = np.asarray(inputs[nm][l], np.float32)
        tabI[l], tabF[l] = _compact_tables(np.asarray(inputs["rpb"][l], np.float32))
    cosF, sinF, rotT = _rope_tables(lmax)
    return dict(gcols=gc, lamv=lamv, tabI=tabI, tabF=tabF, cosF=cosF, sinF=sinF, rotT=rotT,
                ident=np.eye(128, dtype=np.float32))


def seg_tables(tI, tF, is_first, is_last):
    out = np.empty((2, 8, 128, 2048), np.float32)
    for c in range(8):
        a = (14 - 2 * c) * 64
        f = (11 - 2 * c) * 64
        if is_first:
            out[0, :, :, c * 256:(c + 1) * 256] = tF[:, :, f:f + 256] if 2 <= c <= 5 else np.float32(NEG_INF)
        else:
            out[0, :, :, c * 256:(c + 1) * 256] = tI[:, :, a:a + 256]
        if is_last:
            out[1, :, :, c * 256:(c + 1) * 256] = tF[:, :, f + 256:f + 512] if 2 <= c <= 5 else np.float32(NEG_INF)
        else:
            out[1, :, :, c * 256:(c + 1) * 256] = tI[:, :, a + 256:a + 512]
    return out


_PROG_CACHE = {}


def kernel(x_prompt, x_sample, ffn1_norm, ffn1_wi, ffn1_wo, mix_norm, w_in, qa_norm, ka_norm, rpb,
           qb_norm, kb_norm, lam_q1, lam_k1, lam_q2, lam_k2, subln, w_a_out, w_b_out, w_o,
           ffn2_norm, ffn2_wi, ffn2_wo):
    inputs = dict(locals())
    NL = 2
    LP = x_prompt.shape[1]
    LS = x_sample.shape[1]
    NCORES = 8
    per = x_sample.shape[0] // NCORES
    seqs = [("xp", "yp", LP, 0, "seg")] + [("xs", "ys", LS, i * LS, "full") for i in range(per)]
    assert LP // 2048 == NCORES
    key = (LP, LS, per)
    if key not in _PROG_CACHE:
        _PROG_CACHE[key] = Prog(seqs, NL, max(LP, LS))
    prog = _PROG_CACHE[key]
    tabs = host_tables(inputs, NL, max(LP, LS))
    shared = {k: np.ascontiguousarray(np.asarray(inputs[k], np.float32)) for k in
              ("ffn1_wi", "ffn2_wi", "ffn1_wo", "ffn2_wo", "w_in", "w_a_out", "w_b_out", "w_o")}
    shared.update(tabs)
    xp = np.ascontiguousarray(np.asarray(x_prompt, np.float32).reshape(LP, D))
    xs = np.asarray(x_sample, np.float32)
    in_maps = []
    for c in range(NCORES):
        m = dict(shared)
        m["xp"] = xp
        m["xs"] = np.ascontiguousarray(xs[c * per:(c + 1) * per].reshape(per * LS, D))
        ws = np.zeros((128, NCORES), np.float32)
        ws[:, c] = 1.0
        m["wsel"] = ws
        m["segtab"] = seg_tables(tabs["tabI"][NL - 1], tabs["tabF"][NL - 1], c == 0, c == NCORES - 1)
        in_maps.append(m)
    res = run_bass_kernel_spmd(prog.nc, in_maps, core_ids=list(range(NCORES)))
    yp = np.concatenate([np.asarray(res.results[c]["yp"], np.float32) for c in range(NCORES)], axis=0).reshape(1, LP, D)
    ys = np.concatenate([np.asarray(res.results[c]["ys"], np.float32).reshape(per, LS, D) for c in range(NCORES)], axis=0)
    return (yp, ys)
```

```python
import contextlib
import math
import numpy as np
import concourse.bass as bass
import concourse.mybir as mybir
from concourse.bass_utils import run_bass_kernel_spmd

F32 = mybir.dt.float32
BF16 = mybir.dt.bfloat16
AF = mybir.ActivationFunctionType
ALU = mybir.AluOpType
AX = mybir.AxisListType

D = 1024
DFF = 2816
NJ = DFF // 128
GRID_W = 64
NEG_INF = -1e30
NORM_EPS = 1e-6
SUBLN_EPS = 1e-5
ROPE_THETA = 500000.0

ENGS = ("pe", "act", "dve", "pool", "sp")
N_DMA_SEMS = 24
SEM_LIMIT = 30000
GR = 512
SB_BASE = 16640
SB_END = 229300


class Op:
    __slots__ = ("eng", "fn", "deps", "sig", "is_dma", "needs_sig", "waits")

    def __init__(self, eng, fn, is_dma):
        self.eng = eng
        self.fn = fn
        self.deps = []
        self.sig = None
        self.is_dma = is_dma
        self.needs_sig = is_dma
        self.waits = None


class Sched:
    def __init__(self, same_engine_sync=("act", "dve", "pool")):
        self.ops = []
        self.last_w = {}
        self.readers = {}
        self.same_sync = set(same_engine_sync)
        self.dma_rr = 0
        self.dma_last = [None] * N_DMA_SEMS

    def add(self, eng, fn, reads=(), writes=(), is_dma=False):
        op = Op(eng, fn, is_dma)
        deps = []
        lw = self.last_w
        rdrs = self.readers
        for r in reads:
            w = lw.get(r)
            if w is not None:
                deps.append(w)
        for r in writes:
            w = lw.get(r)
            if w is not None:
                deps.append(w)
            rl = rdrs.get(r)
            if rl:
                deps.extend(rl)
        for r in reads:
            rl = rdrs.get(r)
            if rl is None:
                rdrs[r] = [op]
            else:
                rl.append(op)
        for r in writes:
            lw[r] = op
            rdrs[r] = []
        if is_dma:
            slot = self.dma_rr % N_DMA_SEMS
            self.dma_rr += 1
            prev = self.dma_last[slot]
            if prev is not None:
                deps.append(prev)
            self.dma_last[slot] = op
            op.sig = slot
        seen = set()
        for d in deps:
            if d is op or id(d) in seen:
                continue
            seen.add(id(d))
            if d.eng == op.eng and not d.is_dma and not op.is_dma and d.eng not in self.same_sync:
                continue
            d.needs_sig = True
            op.deps.append(d)
        self.ops.append(op)
        return op

    def finalize(self, nc, st):
        cnt = {e: 0 for e in ENGS}
        for op in self.ops:
            if not op.is_dma and op.needs_sig:
                cnt[op.eng] += 1
        self.esems = {}
        for e in ENGS:
            n = (cnt[e] + SEM_LIMIT - 1) // SEM_LIMIT
            self.esems[e] = [st.enter_context(nc.semaphore("s_%s%d" % (e, i))) for i in range(max(n, 1))]
        self.dsems = [st.enter_context(nc.semaphore("s_dma%d" % i)) for i in range(N_DMA_SEMS)]
        cnt = {e: 0 for e in ENGS}
        dcnt = [0] * N_DMA_SEMS
        for op in self.ops:
            if op.is_dma:
                slot = op.sig
                dcnt[slot] += 16
                op.sig = (self.dsems[slot], dcnt[slot], 16)
            elif op.needs_sig:
                c = cnt[op.eng]
                cnt[op.eng] = c + 1
                op.sig = (self.esems[op.eng][c // SEM_LIMIT], c % SEM_LIMIT + 1, 1)
        waited = {e: {} for e in ENGS}
        self.per_eng = {e: [] for e in ENGS}
        for op in self.ops:
            w = {}
            for d in op.deps:
                s, v, _ = d.sig
                k = s.num
                cur = w.get(k)
                if cur is None or v > cur[1]:
                    w[k] = (s, v)
            ws = []
            wd = waited[op.eng]
            for k, (s, v) in w.items():
                if wd.get(k, 0) >= v:
                    continue
                wd[k] = v
                ws.append((s, v))
            op.waits = ws
            self.per_eng[op.eng].append(op)
        self.final_dma = dcnt

    def run_engine(self, ename, e, final_wait=False):
        for op in self.per_eng[ename]:
            for s, v in op.waits:
                e.wait_ge(s, v)
            ins = op.fn(e)
            if op.needs_sig:
                s, v, inc = op.sig
                ins.then_inc(s, inc)
        if final_wait:
            for i, v in enumerate(self.final_dma):
                if v > 0:
                    e.wait_ge(self.dsems[i], v)


class Buf:
    def __init__(self, nc, name, shape, dtype, off):
        self.esz = 4 if dtype == F32 else 2
        n = 1
        for s in shape[1:]:
            n *= s
        self.nbytes = n * self.esz
        self.off = off
        self.shape = shape
        assert off >= SB_BASE and off + self.nbytes <= SB_END, (name, off, self.nbytes)
        self.t = nc.alloc_sbuf_tensor_at(name, list(shape), dtype, offset=off)
        self.inner = (n // shape[1]) * self.esz if len(shape) > 2 else self.nbytes
        self._all = self.k(0, self.nbytes)

    def k(self, lo=0, hi=None):
        if hi is None:
            hi = self.nbytes
        return list(range((self.off + lo) // GR, (self.off + hi - 1) // GR + 1))

    def all(self):
        return self._all

    def ck(self, c, n=1):
        return self.k(c * self.inner, (c + n) * self.inner)


def _bias_blocks(rpb_l):
    cq = np.arange(64)
    ck = np.arange(64)
    wstart = np.clip(cq - 8, 0, 48)
    colmask = (ck[:, None] >= wstart[None, :]) & (ck[:, None] < wstart[None, :] + 16)
    dc = np.clip(ck[:, None] - cq[None, :] + 15, 0, 30)
    Bm = np.where(colmask[None, None], rpb_l[:, :, dc], np.float32(NEG_INF)).astype(np.float32)
    return Bm


def _compact_tables(rpb_l):
    Bm = _bias_blocks(rpb_l)
    neg = np.full((8, 64, 64), NEG_INF, np.float32)

    def blk(d, lo, hi):
        if lo <= d <= hi:
            return Bm[:, d + 7]
        return neg

    tI = np.empty((8, 128, 23 * 64), np.float32)
    for s in range(23):
        d0 = 10 - s
        tI[:, 0:64, s * 64:(s + 1) * 64] = blk(d0, -4, 3)
        tI[:, 64:128, s * 64:(s + 1) * 64] = blk(d0 + 1, -4, 3)
    tF = np.empty((8, 128, 16 * 64), np.float32)
    for s in range(16):
        d0 = 7 - s
        tF[:, 0:64, s * 64:(s + 1) * 64] = blk(d0, -7, 7)
        tF[:, 64:128, s * 64:(s + 1) * 64] = blk(d0 + 1, -7, 7)
    return tI, tF


def _rope_tables(L):
    half = 8
    inv_freq = np.power(np.float32(ROPE_THETA), -np.arange(half, dtype=np.float32) * np.float32(2.0) / np.float32(16)).astype(np.float32)
    pos = np.arange(L, dtype=np.float32)
    ang = (pos[:, None] * inv_freq[None, :]).astype(np.float32)
    cos = np.cos(ang).astype(np.float32).T
    sin = np.sin(ang).astype(np.float32).T
    cosF = np.ones((128, L), np.float32)
    sinF = np.zeros((128, L), np.float32)
    for b in (0, 64):
        cosF[b:b + 8] = cos
        cosF[b + 8:b + 16] = cos
        sinF[b:b + 8] = -sin
        sinF[b + 8:b + 16] = sin
    rotT = np.zeros((128, 128), np.float32)
    for b in (0, 64):
        for d in range(8):
            rotT[b + d + 8, b + d] = 1.0
            rotT[b + d, b + d + 8] = 1.0
    return cosF, sinF, rotT


class Prog:
    def __init__(self, seqs, n_layers, lmax):
        self.seqs = seqs
        self.NL = n_layers
        self.LMAX = lmax
        self.nc = bass.Bass("TRN2", target_bir_lowering=False)
        self.S = Sched()
        self.sb_ptr = SB_BASE
        self.bank_rr = 0
        self.build()

    def sb(self, name, shape, dtype, off=None):
        esz = 4 if dtype == F32 else 2
        n = 1
        for s in shape[1:]:
            n *= s
        nbytes = n * esz
        if off is None:
            off = self.sb_ptr
            self.sb_ptr = (off + nbytes + GR - 1) // GR * GR
        return Buf(self.nc, name, shape, dtype, off)

    def bank(self):
        i = self.bank_rr % 8
        self.bank_rr += 1
        return self.ps[i], ("p", i)

    def dma(self, q, out, in_, reads, writes):
        self.S.add(q, lambda e: e.dma_start(out=out, in_=in_), reads, writes, is_dma=True)

    def mm(self, out, lhsT, rhs, start, stop, reads, writes):
        self.S.add("pe", lambda e: e.matmul(out, lhsT=lhsT, rhs=rhs, start=start, stop=stop), reads, writes)

    def act(self, out, in_, func, reads, writes, scale=1.0, bias=None):
        if bias is None:
            self.S.add("act", lambda e: e.activation(out=out, in_=in_, func=func, scale=scale), reads, writes)
        else:
            self.S.add("act", lambda e: e.activation(out=out, in_=in_, func=func, scale=scale, bias=bias), reads, writes)

    def tt(self, eng, out, in0, in1, op, reads, writes):
        self.S.add(eng, lambda e: e.tensor_tensor(out=out, in0=in0, in1=in1, op=op), reads, writes)

    def stt(self, eng, out, in0, scalar, in1, op0, op1, reads, writes):
        self.S.add(eng, lambda e: e.scalar_tensor_tensor(out=out, in0=in0, scalar=scalar, in1=in1, op0=op0, op1=op1), reads, writes)

    def ts(self, eng, out, in0, s1, op0, reads, writes, s2=None, op1=None):
        if op1 is None:
            self.S.add(eng, lambda e: e.tensor_scalar(out=out, in0=in0, scalar1=s1, scalar2=None, op0=op0), reads, writes)
        else:
            self.S.add(eng, lambda e: e.tensor_scalar(out=out, in0=in0, scalar1=s1, scalar2=s2, op0=op0, op1=op1), reads, writes)

    def cp(self, eng, out, in_, reads, writes):
        if eng == "act":
            self.S.add("act", lambda e: e.activation(out=out, in_=in_, func=AF.Copy), reads, writes)
        else:
            self.S.add(eng, lambda e: e.tensor_copy(out=out, in_=in_), reads, writes)

    def memset(self, eng, ap, val, reads, writes):
        self.S.add(eng, lambda e: e.memset(ap, val), reads, writes)

    def rsqrt_act(self, out, in_, bias, reads, writes):
        self.S.add("act", lambda e: e.activation(out=out, in_=in_, func=AF.Ln, bias=bias, scale=1.0), reads, writes)
        self.S.add("act", lambda e: e.activation(out=out, in_=out, func=AF.Exp, scale=-0.5), writes, writes)

    def recip_act(self, out, in_, reads, writes):
        self.S.add("act", lambda e: e.activation(out=out, in_=in_, func=AF.Ln), reads, writes)
        self.S.add("act", lambda e: e.activation(out=out, in_=out, func=AF.Exp, scale=-1.0), writes, writes)

    def recip(self, out, in_, reads, writes):
        self.S.add("dve", lambda e: e.reciprocal(out=out, in_=in_), reads, writes)

    def build(self):
        nc = self.nc
        NL = self.NL
        LMAX = self.LMAX
        NCH = LMAX // 128
        self.din = {}
        self.dout = {}
        rows = {}
        self.has_seg = any(m == "seg" for (_, _, _, _, m) in self.seqs)
        for (iname, oname, L, row0, mode) in self.seqs:
            rows[iname] = max(rows.get(iname, 0), row0 + L)
        for iname, r in rows.items():
            self.din[iname] = nc.dram_tensor(iname, [r, D], F32, kind="ExternalInput").ap()
        for (iname, oname, L, row0, mode) in self.seqs:
            if oname not in self.dout:
                orows = 2048 if mode == "seg" else rows[iname]
                self.dout[oname] = nc.dram_tensor(oname, [orows, D], F32, kind="ExternalOutput").ap()
        ein = lambda name, shape: nc.dram_tensor(name, shape, F32, kind="ExternalInput").ap()
        self.w_ffn_wi = [ein("ffn1_wi", [NL, D, 2 * DFF]), ein("ffn2_wi", [NL, D, 2 * DFF])]
        self.w_ffn_wo = [ein("ffn1_wo", [NL, DFF, D]), ein("ffn2_wo", [NL, DFF, D])]
        self.w_in = ein("w_in", [NL, D, 5120])
        self.w_a = ein("w_a_out", [NL, 512, D])
        self.w_b = ein("w_b_out", [NL, 512, D])
        self.w_o = ein("w_o", [NL, D, D])
        self.gcols_d = ein("gcols", [128, NL * 29])
        self.lamv_d = ein("lamv", [NL * 256])
        self.tabI_d = ein("tabI", [NL, 8, 128, 1472])
        self.tabF_d = ein("tabF", [NL, 8, 128, 1024])
        self.cos_d = ein("cosF", [128, LMAX])
        self.sin_d = ein("sinF", [128, LMAX])
        self.ident_d = ein("ident", [128, 128])
        self.rot_d = ein("rotT", [128, 128])
        if self.has_seg:
            self.ncand = LMAX // 2048
            self.wsel_d = ein("wsel", [128, self.ncand])
            self.segtab_d = ein("segtab", [2, 8, 128, 2048])
        sc = lambda name, shape, dt: nc.dram_tensor(name, shape, dt)
        self.wi_b = [[sc("wib%d%d" % (l, f), [NJ, 128, 8, 256], BF16) for f in range(2)] for l in range(NL)]
        self.wo_b = [[sc("wob%d%d" % (l, f), [8, 128, NJ, 128], BF16) for f in range(2)] for l in range(NL)]
        self.win_b = [sc("winb%d" % l, [20, 128, 8, 256], BF16) for l in range(NL)]
        self.wa_b = [sc("wab%d" % l, [4, 128, 4, 256], BF16) for l in range(NL)]
        self.wb_b = [sc("wbb%d" % l, [4, 128, 4, 256], BF16) for l in range(NL)]
        self.wout_b = [sc("woutb%d" % l, [4, 128, 8, 256], BF16) for l in range(NL)]
        self.eint_b = sc("eintb", [NL, 128, 8, 1472], BF16)
        self.efull_b = sc("efullb", [NL, 8, 128, 1024], BF16)
        self.qaT = sc("qaT", [4, 128, LMAX], BF16)
        self.kaT = sc("kaT", [4, 128, LMAX + 512], BF16)
        self.qbT = sc("qbT", [4, 128, LMAX], BF16)
        self.kbT = sc("kbT", [4, 128, LMAX], BF16)
        self.va = sc("va", [128, NCH + 4, 512], BF16)
        self.vb = sc("vb", [4, 128, NCH, 128], BF16)
        self.gA = sc("gA", [8, 128, LMAX], F32)
        self.gB = sc("gB", [8, 128, LMAX], F32)
        self.xmid = sc("xmid", [8, 128, LMAX], F32)
        self.x1 = sc("x1", [8, 128, LMAX], F32)
        if self.has_seg:
            self.segtab_b = sc("segtabb", [2, 8, 128, 2048], BF16)
            self.sel_qaT = sc("sel_qaT", [4, 128, 2048], BF16)
            self.sel_qbT = sc("sel_qbT", [4, 128, 2048], BF16)
            self.sel_kwin = sc("sel_kwin", [4, 4, 128, 1024], BF16)
            self.sel_vwin = sc("sel_vwin", [4, 128, 8, 512], BF16)
            self.sel_gA = sc("sel_gA", [8, 128, 2048], F32)
            self.sel_gB = sc("sel_gB", [8, 128, 2048], F32)
            self.sel_xmid = sc("sel_xmid", [8, 128, 2048], F32)

        self.ps = [nc.alloc_psum_tensor("ps%d" % i, [128, 512], F32) for i in range(8)]

        sb = self.sb
        self.ident = sb("ident", [128, 128], F32)
        self.ones_b = sb("ones_b", [128, 128], BF16)
        self.blk_b = sb("blk_b", [128, 128], BF16)
        self.half_b = [sb("half0_b", [128, 128], BF16), sb("half1_b", [128, 128], BF16)]
        self.rot_b = sb("rot_b", [128, 128], BF16)
        self.zero_b = sb("zero_b", [128, 1024], BF16)
        self.gcols = sb("gcols_sb", [128, NL * 29], F32)
        self.g32 = sb("g32", [128, NL * 24], F32)
        self.gd = sb("gd", [128, NL * 8], F32)
        self.eint = sb("eint", [128, 8, 1472], BF16)
        self.xT = sb("xT", [128, 8, 512], F32)
        self.gA_sb = sb("gA_sb", [128, 8, 512], F32)
        self.gB_sb = sb("gB_sb", [128, 8, 512], F32)
        self.wslot = [sb("wslot%d" % i, [128, 2816], BF16) for i in range(4)]
        self.ws_rr = 0
        self.cos_sb = sb("cos_sb", [128, 512], F32)
        self.sin_sb = sb("sin_sb", [128, 512], F32)
        self.AT = sb("AT", [128, 4, 512], BF16)
        self.BT = sb("BT", [128, 4, 512], BF16)
        base = self.sb_ptr
        self.uT = sb("uT", [128, 8, 512], BF16)
        self.hT = sb("hT", [128, NJ, 512], BF16)
        self.mT = sb("mT", [128, 8, 512], BF16, off=self.hT.off)
        self.xtok = sb("xtok", [128, 4, 1024], F32)
        self.sq = [sb("sq%d" % i, [128, 512], BF16) for i in range(2)]
        self.rstd = [sb("rstd%d" % i, [128, 512], F32) for i in range(2)]
        self.tmpf = [sb("tmpf%d" % i, [128, 512], F32) for i in range(4)]
        self.qf = [sb("qf%d" % i, [128, 512], F32) for i in range(2)]
        self.qnb = [sb("qnb%d" % i, [128, 512], BF16) for i in range(2)]
        self.stage = sb("stage", [128, 4, 512], BF16)
        self.vstA = sb("vstA", [128, 4, 512], BF16)
        self.vstB = sb("vstB", [128, 4, 4, 128], BF16)
        self.gst = [sb("gst%d" % i, [128, 2, 512], F32) for i in range(2)]
        endA = self.sb_ptr
        o = self.hT.off
        self.qfD = [Buf(nc, "qfD%d" % i, [128, 512], F32, o + i * 2048) for i in range(4)]
        o += 4 * 2048
        self.qnD = [Buf(nc, "qnD%d" % i, [128, 512], F32, o + i * 2048) for i in range(3)]
        o += 3 * 2048
        self.rstdD = [Buf(nc, "rstdD%d" % i, [128, 512], F32, o + i * 2048) for i in range(3)]
        o += 3 * 2048
        self.t1D = [Buf(nc, "t1D%d" % i, [128, 512], F32, o + i * 2048) for i in range(2)]
        o += 2 * 2048
        self.sqD = [Buf(nc, "sqD%d" % i, [128, 512], BF16, o + i * 1024) for i in range(3)]
        o += 3 * 1024
        self.qnbD = [Buf(nc, "qnbD%d" % i, [128, 512], BF16, o + i * 1024) for i in range(3)]
        o += 3 * 1024
        self.stageD = [Buf(nc, "stageD%d" % i, [128, 4, 512], BF16, o + i * 4096) for i in range(2)]
        o += 2 * 4096
        assert o <= self.xtok.off + self.xtok.nbytes, (o, self.xtok.off + self.xtok.nbytes)
        self.jq = 0
        self.sb_ptr = self.qf[0].off
        self.qb_sb = sb("qb_sb", [128, 4, 512], BF16)
        self.kst = [sb("kst%d" % i, [128, 2048], BF16) for i in range(2)]
        self.vst = [sb("vst%d" % i, [128, 16, 128], BF16) for i in range(2)]
        self.P = [sb("P%d" % i, [128, 512], BF16) for i in range(4)]
        self.esq = [sb("esq%d" % i, [128, 512], BF16) for i in range(2)]
        assert self.sb_ptr <= endA, (self.sb_ptr, endA)
        self.sb_ptr = base
        self.qa_sb = sb("qa_sb", [128, 4, 512], BF16)
        self.kwin = sb("kwin", [128, 4, 1024], BF16)
        self.vwin = sb("vwin", [128, 8, 512], BF16)
        self.vaug = sb("vaug", [128, 8, 8, 128], BF16)
        self.efull = [sb("efull%d" % i, [128, 2048], BF16) for i in range(2)]
        self.ef = [sb("ef%d" % i, [128, 512], F32) for i in range(8)]
        endB = self.sb_ptr
        self.sb_ptr = max(endA, endB)
        self.stg_f = [Buf(nc, "stgf%d" % i, [128, 2816], F32, base + i * 11264) for i in range(2)]
        self.stg_b = [Buf(nc, "stgb%d" % i, [128, 2816], BF16, base + 22528 + i * 5632) for i in range(2)]
        self.sb_used = self.sb_ptr
        if self.has_seg:
            self.wsel = Buf(nc, "wsel_sb", [128, self.ncand], F32, self.sb_ptr)
            self.sb_ptr += GR
            self.selF = [Buf(nc, "selF%d" % i, [128, 4096], F32, base + i * 16384) for i in range(3)]
            self.selH = [Buf(nc, "selH%d" % i, [128, 4096], BF16, base + i * 16384) for i in range(3)]

        self.prologue()
        for si, (iname, oname, L, row0, mode) in enumerate(self.seqs):
            for l in range(NL):
                self.seq_layer(si, iname, oname, L, row0, l, mode)
        self.emit()

    def prologue(self):
        NL = self.NL
        dma = self.dma
        dma("sp", self.ident.t[:], self.ident_d, [], self.ident.all())
        st0 = self.stg_f[0]
        dma("sp", st0.t[:, 0:128], self.rot_d, [], st0.k(0, 512))
        self.cp("dve", self.rot_b.t[:], st0.t[:, 0:128], st0.k(0, 512), self.rot_b.all())
        self.memset("pool", self.ones_b.t[:], 1.0, [], self.ones_b.all())
        self.memset("pool", self.zero_b.t[:], 0.0, [], self.zero_b.all())
        self.memset("pool", self.blk_b.t[:], 0.0, [], self.blk_b.all())
        self.memset("pool", self.blk_b.t[0:64, 0:64], 1.0, [], self.blk_b.all())
        self.memset("pool", self.blk_b.t[64:128, 64:128], 1.0, [], self.blk_b.all())
        for i in range(2):
            self.memset("pool", self.half_b[i].t[:], 0.0, [], self.half_b[i].all())
            self.memset("pool", self.half_b[i].t[:, i * 64:(i + 1) * 64], 1.0, [], self.half_b[i].all())
        dma("sp", self.gcols.t[:], self.gcols_d, [], self.gcols.all())
        lamb = self.stg_f[1]
        dma("sp", lamb.t[:, 0:NL * 256], self.lamv_d.partition_broadcast(128), [], lamb.k(0, NL * 1024))
        for l in range(NL):
            lam_init = 0.8 - 0.6 * math.exp(-0.3 * l)
            g = self.gcols.t
            gk = self.gcols.all()
            self.ts("dve", self.g32.t[:, l * 24:(l + 1) * 24], g[:, l * 29:l * 29 + 24], 32.0, ALU.mult, gk, self.g32.all())
            gd = self.gd.t
            gdk = self.gd.all()
            o = l * 8
            self.cp("dve", gd[:, o + 0:o + 1], g[:, l * 29 + 24:l * 29 + 25], gk, gdk)
            self.ts("dve", gd[:, o + 1:o + 2], g[:, l * 29 + 25:l * 29 + 26], 8.0, ALU.mult, gk, gdk)
            self.cp("dve", gd[:, o + 2:o + 3], g[:, l * 29 + 26:l * 29 + 27], gk, gdk)
            self.ts("dve", gd[:, o + 3:o + 4], g[:, l * 29 + 27:l * 29 + 28], 8.0, ALU.mult, gk, gdk)
            self.ts("dve", gd[:, o + 4:o + 5], g[:, l * 29 + 28:l * 29 + 29],
                    float(math.sqrt(128.0) * (1.0 - lam_init)), ALU.mult, gk, gdk)
            t = self.tmpf[0]
            lb = lamb.t
            lbk = lamb.k(0, NL * 1024)
            b0 = l * 256
            self.tt("dve", t.t[:, 0:64], lb[:, b0:b0 + 64], lb[:, b0 + 64:b0 + 128], ALU.mult, lbk, t.all())
            self.tt("dve", t.t[:, 64:128], lb[:, b0 + 128:b0 + 192], lb[:, b0 + 192:b0 + 256], ALU.mult, lbk, t.all())
            t2 = self.tmpf[1]
            self.S.add("dve", lambda e, t=t, t2=t2: e.reduce_sum(out=t2.t[:, 0:1], in_=t.t[:, 0:64], axis=AX.X), t.all(), t2.all())
            self.S.add("dve", lambda e, t=t, t2=t2: e.reduce_sum(out=t2.t[:, 1:2], in_=t.t[:, 64:128], axis=AX.X), t.all(), t2.all())
            self.act(t2.t[:, 2:4], t2.t[:, 0:2], AF.Exp, t2.all(), t2.all())
            self.tt("dve", t2.t[:, 4:5], t2.t[:, 3:4], t2.t[:, 2:3], ALU.subtract, t2.all(), t2.all())
            self.ts("dve", gd[:, o + 5:o + 6], t2.t[:, 4:5], float(-lam_init), ALU.add, t2.all(), gdk)
        self.cv_rr = 0
        if self.has_seg:
            dma("sp", self.wsel.t[:], self.wsel_d, [], self.wsel.all())
            for k in range(2):
                for h in range(8):
                    i = self.cv_rr % 2
                    self.cv_rr += 1
                    sf, sbb = self.stg_f[i], self.stg_b[i]
                    dma("sp", sf.t[:, 0:2048], self.segtab_d[k, h], [], sf.k(0, 8192))
                    self.act(sbb.t[:, 0:2048], sf.t[:, 0:2048], AF.Exp, sf.k(0, 8192), sbb.k(0, 4096))
                    dma("pool", self.segtab_b[k, h], sbb.t[:, 0:2048], sbb.k(0, 4096), [("segtabb", k, h)])
        for l in range(NL):
            for f in range(2):
                W = self.w_ffn_wi[f]
                for j in range(NJ):
                    srcs = [(W[l, :, j * 128:(j + 1) * 128].rearrange("(kc p) n -> p kc n", p=128), 0, 128),
                            (W[l, :, DFF + j * 128:DFF + (j + 1) * 128].rearrange("(kc p) n -> p kc n", p=128), 128, 128)]
                    self.convert(srcs, 8, 256, self.wi_b[l][f][j], ("wib", l, f, j))
                W = self.w_ffn_wo[f]
                for n in range(8):
                    srcs = [(W[l, :, n * 128:(n + 1) * 128].rearrange("(kc p) n -> p kc n", p=128), 0, 128)]
                    self.convert(srcs, NJ, 128, self.wo_b[l][f][n], ("wob", l, f, n))
            for g in range(20):
                srcs = [(self.w_in[l, :, g * 256:(g + 1) * 256].rearrange("(kc p) n -> p kc n", p=128), 0, 256)]
                self.convert(srcs, 8, 256, self.win_b[l][g], ("winb", l, g))
            for g in range(4):
                srcs = [(self.w_a[l, :, g * 256:(g + 1) * 256].rearrange("(kc p) n -> p kc n", p=128), 0, 256)]
                self.convert(srcs, 4, 256, self.wa_b[l][g], ("wab", l, g))
                srcs = [(self.w_b[l, :, g * 256:(g + 1) * 256].rearrange("(kc p) n -> p kc n", p=128), 0, 256)]
                self.convert(srcs, 4, 256, self.wb_b[l][g], ("wbb", l, g))
                srcs = [(self.w_o[l, :, g * 256:(g + 1) * 256].rearrange("(kc p) n -> p kc n", p=128), 0, 256)]
                self.convert(srcs, 8, 256, self.wout_b[l][g], ("woutb", l, g))
            for h in range(8):
                i = self.cv_rr % 2
                self.cv_rr += 1
                sf, sbb = self.stg_f[i], self.stg_b[i]
                dma("sp", sf.t[:, 0:1472], self.tabI_d[l, h], [], sf.k(0, 1472 * 4))
                self.act(sbb.t[:, 0:1472], sf.t[:, 0:1472], AF.Exp, sf.k(0, 1472 * 4), sbb.k(0, 1472 * 2))
                dma("pool", self.eint_b[l, :, h, :], sbb.t[:, 0:1472], sbb.k(0, 1472 * 2), [("eintb", l)])
                i = self.cv_rr % 2
                self.cv_rr += 1
                sf, sbb = self.stg_f[i], self.stg_b[i]
                dma("sp", sf.t[:, 0:1024], self.tabF_d[l, h], [], sf.k(0, 4096))
                self.act(sbb.t[:, 0:1024], sf.t[:, 0:1024], AF.Exp, sf.k(0, 4096), sbb.k(0, 2048))
                dma("pool", self.efull_b[l, h], sbb.t[:, 0:1024], sbb.k(0, 2048), [("efullb", l, h)])

    def convert(self, srcs, kc, n, dst, key):
        i = self.cv_rr % 2
        self.cv_rr += 1
        sf, sbb = self.stg_f[i], self.stg_b[i]
        tot = kc * n
        fv = sf.t[:, 0:tot].rearrange("p (k n) -> p k n", n=n)
        for (src, c0, w) in srcs:
            self.dma("sp", fv[:, :, c0:c0 + w], src, [], sf.k(0, tot * 4))
        eng = ("dve", "pool", "act")[self.cv_rr % 3]
        self.cp(eng, sbb.t[:, 0:tot], sf.t[:, 0:tot], sf.k(0, tot * 4), sbb.k(0, tot * 2))
        self.dma("pool", dst.rearrange("p k n -> p (k n)"), sbb.t[:, 0:tot], sbb.k(0, tot * 2), [key])

    def wload(self, src_ap, kc, n, key):
        ws = self.wslot[self.ws_rr % 4]
        self.ws_rr += 1
        tot = kc * n
        self.dma("sp", ws.t[:, 0:tot], src_ap.rearrange("p k n -> p (k n)"), [key], ws.k(0, tot * 2))
        return ws.t[:, 0:tot].rearrange("p (k n) -> p k n", n=n), ws.k(0, tot * 2)

    def rmsnorm(self, l, which):
        xT, uT = self.xT, self.uT
        pb, pk = self.bank()
        for c in range(8):
            sq = self.sq[c % 2]
            self.tt("dve" if c % 2 == 0 else "pool", sq.t[:], xT.t[:, c, :], xT.t[:, c, :], ALU.mult, xT.ck(c), sq.all())
            self.mm(pb[:], self.ones_b.t[:], sq.t[:], c == 0, c == 7, sq.all() + self.ones_b.all(), [pk])
        r = self.rstd[0]
        self.rsqrt_act(r.t[:], pb[:], float(1024 * NORM_EPS), [pk], r.all())
        gc = l * 24 + which * 8
        for c in range(8):
            self.stt("dve", uT.t[:, c, :], xT.t[:, c, :], self.g32.t[:, gc + c:gc + c + 1], r.t[:], ALU.mult, ALU.mult,
                     xT.ck(c) + self.g32.all() + r.all(), uT.ck(c))

    def ffn(self, l, f):
        xT, uT, hT = self.xT, self.uT, self.hT
        for j in range(NJ):
            W, wk = self.wload(self.wi_b[l][f][j], 8, 256, ("wib", l, f, j))
            pg, pgk = self.bank()
            pu, puk = self.bank()
            for kc in range(8):
                self.mm(pg[:], W[:, kc, 0:128], uT.t[:, kc, :], kc == 0, kc == 7, wk + uT.ck(kc), [pgk])
            for kc in range(8):
                self.mm(pu[:], W[:, kc, 128:256], uT.t[:, kc, :], kc == 0, kc == 7, wk + uT.ck(kc), [puk])
            t = self.tmpf[j % 2]
            t2 = self.tmpf[2 + j % 2]
            self.act(t.t[:], pg[:], AF.Silu, [pgk], t.all())
            self.cp("act", t2.t[:], pu[:], [puk], t2.all())
            self.tt("pool", hT.t[:, j, :], t.t[:], t2.t[:], ALU.mult, t.all() + t2.all(), hT.ck(j))
        for n in range(8):
            W, wk = self.wload(self.wo_b[l][f][n], NJ, 128, ("wob", l, f, n))
            pb, pk = self.bank()
            for j in range(NJ):
                self.mm(pb[:], W[:, j, :], hT.t[:, j, :], j == 0, j == NJ - 1, wk + hT.ck(j), [pk])
            self.stt("dve", xT.t[:, n, :], pb[:], 0.5, xT.t[:, n, :], ALU.mult, ALU.add, [pk] + xT.ck(n), xT.ck(n))

    def seq_layer(self, si, iname, oname, L, row0, l, mode="full"):
        NT = L // 512
        dma = self.dma
        dma("sp", self.eint.t[:], self.eint_b[l], [("eintb", l)], self.eint.all())
        NC = L // 128
        for pr in range(4):
            dma("pool", self.kaT[pr, :, 0:256], self.zero_b.t[:, 0:256], self.zero_b.all(), [("kaT", -1)])
            dma("pool", self.kaT[pr, :, 256 + L:512 + L], self.zero_b.t[:, 0:256], self.zero_b.all(), [("kaT", NT)])
        dma("pool", self.va[:, 0:2, :], self.zero_b.t[:].rearrange("p (c n) -> p c n", n=512), self.zero_b.all(), [("va", -1)])
        dma("pool", self.va[:, NC + 2:NC + 4, :], self.zero_b.t[:].rearrange("p (c n) -> p c n", n=512), self.zero_b.all(), [("va", NT)])
        for t in range(NT):
            self.phase_a(iname, L, row0, l, t)
        if mode == "seg" and l == self.NL - 1:
            self.select_segment(L)
            for slot in range(4):
                self.phase_bc(oname, L, 0, l, slot, slot=slot)
        else:
            for t in range(NT):
                self.phase_bc(oname, L, row0, l, t)

    def select_segment(self, L):
        NT = L // 512
        ncand = L // 2048
        rr = 0
        for slot in range(4):
            def win_keys(name, t):
                return [(name, tt_) for tt_ in (t - 1, t, t + 1) if -1 <= tt_ <= NT]
            kinds = [
                ("qaT", True, 4, 512, lambda t: self.qaT[:, :, t * 512:(t + 1) * 512].rearrange("c p n -> p c n"),
                 lambda t: [("qaT", t)], self.sel_qaT[:, :, slot * 512:(slot + 1) * 512].rearrange("c p n -> p c n")),
                ("qbT", True, 4, 512, lambda t: self.qbT[:, :, t * 512:(t + 1) * 512].rearrange("c p n -> p c n"),
                 lambda t: [("qbT", t)], self.sel_qbT[:, :, slot * 512:(slot + 1) * 512].rearrange("c p n -> p c n")),
                ("kwin", True, 4, 1024, lambda t: self.kaT[:, :, 512 * t:512 * t + 1024].rearrange("c p n -> p c n"),
                 lambda t: win_keys("kaT", t), self.sel_kwin[slot].rearrange("c p n -> p c n")),
                ("vwin", True, 8, 512, lambda t: self.va[:, 4 * t:4 * t + 8, :],
                 lambda t: win_keys("va", t), self.sel_vwin[slot]),
                ("gA", False, 8, 512, lambda t: self.gA[:, :, t * 512:(t + 1) * 512].rearrange("c p n -> p c n"),
                 lambda t: [("ga", t)], self.sel_gA[:, :, slot * 512:(slot + 1) * 512].rearrange("c p n -> p c n")),
                ("gB", False, 8, 512, lambda t: self.gB[:, :, t * 512:(t + 1) * 512].rearrange("c p n -> p c n"),
                 lambda t: [("gb", t)], self.sel_gB[:, :, slot * 512:(slot + 1) * 512].rearrange("c p n -> p c n")),
                ("xmid", False, 8, 512, lambda t: self.xmid[:, :, t * 512:(t + 1) * 512].rearrange("c p n -> p c n"),
                 lambda t: [("xmid", t)], self.sel_xmid[:, :, slot * 512:(slot + 1) * 512].rearrange("c p n -> p c n")),
            ]
            for (name, half, nch, nn, srcf, keyf, dst) in kinds:
                bufs = self.selH if half else self.selF
                esz = 2 if half else 4
                tot = nch * nn
                acc = bufs[2]
                for cnd in range(ncand):
                    t = cnd * 4 + slot
                    stg = bufs[rr % 2]
                    rr += 1
                    self.dma("sp", stg.t[:, 0:tot].rearrange("p (c n) -> p c n", n=nn), srcf(t), keyf(t), stg.k(0, tot * esz))
                    eng = "pool" if cnd == 0 else "dve"
                    wcol = self.wsel.t[:, cnd:cnd + 1]
                    if cnd == 0:
                        self.ts(eng, acc.t[:, 0:tot], stg.t[:, 0:tot], wcol, ALU.mult,
                                stg.k(0, tot * esz) + self.wsel.all(), acc.k(0, tot * esz))
                    else:
                        self.stt(eng, acc.t[:, 0:tot], stg.t[:, 0:tot], wcol, acc.t[:, 0:tot], ALU.mult, ALU.add,
                                 stg.k(0, tot * esz) + self.wsel.all() + acc.k(0, tot * esz), acc.k(0, tot * esz))
                self.dma("pool", dst, acc.t[:, 0:tot].rearrange("p (c n) -> p c n", n=nn), acc.k(0, tot * esz), [("sel", name, slot)])

    def phase_a(self, iname, L, row0, l, t):
        dma = self.dma
        xT, uT = self.xT, self.uT
        tok = slice(t * 512, (t + 1) * 512)
        if l == 0:
            xin = self.din[iname]
            for b in range(4):
                r0 = row0 + t * 512 + b * 128
                dma("sp", self.xtok.t[:, b, :], xin[r0:r0 + 128, :], [], self.xtok.ck(b))
            for c in range(8):
                pb, pk = self.bank()
                for b in range(4):
                    self.S.add("pe", lambda e, pb=pb, b=b, c=c: e.transpose(out=pb[:, b * 128:(b + 1) * 128],
                                                                               in_=self.xtok.t[:, b, c * 128:(c + 1) * 128],
                                                                               identity=self.ident.t[:]),
                               self.xtok.ck(b) + self.ident.all(), [pk])
                self.cp("act" if c % 2 == 0 else "dve", xT.t[:, c, :], pb[:], [pk], xT.ck(c))
        else:
            dma("sp", xT.t[:], self.x1[:, :, tok].rearrange("c p n -> p c n"), [("x1", t)], xT.all())
        dma("sp", self.cos_sb.t[:], self.cos_d[:, tok], [], self.cos_sb.all())
        dma("sp", self.sin_sb.t[:], self.sin_d[:, tok], [], self.sin_sb.all())
        self.rmsnorm(l, 0)
        self.ffn(l, 0)
        dma("pool", self.xmid[:, :, tok].rearrange("c p n -> p c n"), xT.t[:], xT.all(), [("xmid", t)])
        self.rmsnorm(l, 1)
        gdo = l * 8
        LAG = 2
        pending = []
        tick = [0]

        def run_due(force=False):
            for job in pending:
                if job[1] and (force or job[0] <= tick[0]):
                    job[1].pop(0)()
                    job[0] = tick[0] + LAG
            pending[:] = [j for j in pending if j[1]]

        def do_tick():
            tick[0] += 1
            run_due()

        kq = 0
        for g in range(20):
            W, wk = self.wload(self.win_b[l][g], 8, 256, ("winb", l, g))
            kind = ("qa", "qa", "ka", "ka", "va", "va", "qb", "qb", "kb", "kb", "vb", "vb",
                    "ga", "ga", "ga", "ga", "gb", "gb", "gb", "gb")[g]
            if kind in ("va", "vb"):
                half = g % 2
                for bp in range(2):
                    pb, pk = self.bank()
                    for bb in range(2):
                        b = bp * 2 + bb
                        for kc in range(8):
                            self.mm(pb[:, bb * 256:(bb + 1) * 256], uT.t[:, kc, b * 128:(b + 1) * 128], W[:, kc, :],
                                    kc == 0, kc == 7, wk + uT.ck(kc), [pk])
                    for bb in range(2):
                        b = bp * 2 + bb
                        if kind == "va":
                            self.cp("act", self.vstA.t[:, b, half * 256:(half + 1) * 256], pb[:, bb * 256:(bb + 1) * 256],
                                    [pk], self.vstA.all())
                        else:
                            self.cp("act", self.vstB.t[:, 2 * half:2 * half + 2, b, :],
                                    pb[:, bb * 256:(bb + 1) * 256].rearrange("p (h e) -> p h e", e=128),
                                    [pk], self.vstB.all())
                    do_tick()
                if half == 1:
                    if kind == "va":
                        dma("pool", self.va[:, 2 + 4 * t:2 + 4 * t + 4, :], self.vstA.t[:], self.vstA.all(), [("va", t)])
                    else:
                        dma("pool", self.vb[:, :, 4 * t:4 * t + 4, :].rearrange("h p c e -> p h c e"), self.vstB.t[:],
                            self.vstB.all(), [("vb", t)])
                continue
            for cc in range(2):
                ch = g * 2 + cc
                pb, pk = self.bank()
                for kc in range(8):
                    self.mm(pb[:], W[:, kc, cc * 128:(cc + 1) * 128], uT.t[:, kc, :], kc == 0, kc == 7, wk + uT.ck(kc), [pk])
                if kind in ("ga", "gb"):
                    gi = ch - 24 if kind == "ga" else ch - 32
                    gs = self.gst[(gi // 2) % 2]
                    self.act(gs.t[:, gi % 2, :], pb[:], AF.Sigmoid, [pk], gs.ck(gi % 2))
                    if gi % 2 == 1:
                        dst = self.gA if kind == "ga" else self.gB
                        c0 = gi - 1
                        dma("pool", dst[c0:c0 + 2, :, tok].rearrange("c p n -> p c n"), gs.t[:], gs.all(), [(kind, t)])
                    do_tick()
                    continue
                ci = ch % 4 if kind in ("qa", "ka") else (ch - 12) % 4
                if ci == 0:
                    kq += 1
                col = {"qa": 0, "ka": 1, "qb": 2, "kb": 3}[kind]
                gcol = self.gd.t[:, gdo + col:gdo + col + 1]
                jq = self.jq
                self.jq += 1
                qf = self.qfD[jq % 4]
                sq = self.sqD[jq % 3]
                r = self.rstdD[jq % 3]
                qn = self.qnD[jq % 3]
                qb16 = self.qnbD[jq % 3]
                t1 = self.t1D[jq % 2]
                stg = self.stageD[kq % 2]
                self.cp("act", qf.t[:], pb[:], [pk], qf.all())
                self.tt("pool", sq.t[:], qf.t[:], qf.t[:], ALU.mult, qf.all(), sq.all())

                def store(kind=kind, stg=stg):
                    if kind == "ka":
                        dma("pool", self.kaT[:, :, 256 + t * 512:256 + (t + 1) * 512].rearrange("c p n -> p c n"),
                            stg.t[:], stg.all(), [("kaT", t)])
                    else:
                        dst = {"qa": self.qaT, "qb": self.qbT, "kb": self.kbT}[kind]
                        dma("pool", dst[:, :, tok].rearrange("c p n -> p c n"), stg.t[:], stg.all(),
                            [({"qa": "qaT", "qb": "qbT", "kb": "kbT"}[kind], t)])

                def s2(kind=kind, ci=ci, gcol=gcol, qf=qf, sq=sq, r=r, qn=qn, qb16=qb16, stg=stg, store=store):
                    p2, p2k = self.bank()
                    self.mm(p2[:], self.blk_b.t[:], sq.t[:], True, True, sq.all() + self.blk_b.all(), [p2k])
                    self.rsqrt_act(r.t[:], p2[:], float(64 * NORM_EPS), [p2k], r.all())
                    if kind in ("qa", "ka"):
                        self.stt("dve", stg.t[:, ci, :], qf.t[:], gcol, r.t[:], ALU.mult, ALU.mult,
                                 qf.all() + self.gd.all() + r.all(), stg.ck(ci))
                        if ci == 3:
                            store()
                    else:
                        self.stt("dve", qn.t[:], qf.t[:], gcol, r.t[:], ALU.mult, ALU.mult,
                                 qf.all() + self.gd.all() + r.all(), qn.all())
                        self.cp("act", qb16.t[:], qn.t[:], qn.all(), qb16.all())

                def s3(ci=ci, qn=qn, qb16=qb16, t1=t1, stg=stg, store=store):
                    p3, p3k = self.bank()
                    self.mm(p3[:], self.rot_b.t[:], qb16.t[:], True, True, qb16.all() + self.rot_b.all(), [p3k])
                    self.tt("pool", t1.t[:], qn.t[:], self.cos_sb.t[:], ALU.mult, qn.all() + self.cos_sb.all(), t1.all())
                    self.tt("dve", qn.t[:], p3[:], self.sin_sb.t[:], ALU.mult, [p3k] + self.sin_sb.all() + qn.all(), qn.all())
                    self.tt("dve", stg.t[:, ci, :], t1.t[:], qn.t[:], ALU.add, t1.all() + qn.all(), stg.ck(ci))
                    if ci == 3:
                        store()

                pending.append([tick[0] + LAG, [s2] if kind in ("qa", "ka") else [s2, s3]])
                do_tick()
        while pending:
            tick[0] += LAG
            run_due(force=True)

    def phase_bc(self, oname, L, row0, l, t, slot=None):
        dma = self.dma
        NT = L // 512
        tok = slice(t * 512, (t + 1) * 512)
        gdo = l * 8
        if slot is None:
            s_qa = (self.qaT[:, :, tok].rearrange("c p n -> p c n"), [("qaT", t)])
            s_kw = (self.kaT[:, :, 512 * t:512 * t + 1024].rearrange("c p n -> p c n"),
                    [("kaT", tt_) for tt_ in (t - 1, t, t + 1) if -1 <= tt_ <= NT])
            s_vw = (self.va[:, 4 * t:4 * t + 8, :], [("va", tt_) for tt_ in (t - 1, t, t + 1) if -1 <= tt_ <= NT])
            s_qb = (self.qbT[:, :, tok].rearrange("c p n -> p c n"), [("qbT", t)])
            s_xm = (self.xmid[:, :, tok].rearrange("c p n -> p c n"), [("xmid", t)])
            s_ga = (self.gA[:, :, tok].rearrange("c p n -> p c n"), [("ga", t)])
            s_gb = (self.gB[:, :, tok].rearrange("c p n -> p c n"), [("gb", t)])
            edge = "top" if t == 0 else ("bot" if t == NT - 1 else None)
        else:
            s_qa = (self.sel_qaT[:, :, tok].rearrange("c p n -> p c n"), [("sel", "qaT", slot)])
            s_kw = (self.sel_kwin[slot].rearrange("c p n -> p c n"), [("sel", "kwin", slot)])
            s_vw = (self.sel_vwin[slot], [("sel", "vwin", slot)])
            s_qb = (self.sel_qbT[:, :, tok].rearrange("c p n -> p c n"), [("sel", "qbT", slot)])
            s_xm = (self.sel_xmid[:, :, tok].rearrange("c p n -> p c n"), [("sel", "xmid", slot)])
            s_ga = (self.sel_gA[:, :, tok].rearrange("c p n -> p c n"), [("sel", "gA", slot)])
            s_gb = (self.sel_gB[:, :, tok].rearrange("c p n -> p c n"), [("sel", "gB", slot)])
            edge = "segtop" if slot == 0 else ("segbot" if slot == 3 else None)
        dma("sp", self.qa_sb.t[:], s_qa[0], s_qa[1], self.qa_sb.all())
        dma("sp", self.kwin.t[:], s_kw[0], s_kw[1], self.kwin.all())
        dma("sp", self.vwin.t[:], s_vw[0], s_vw[1], self.vwin.all())
        if t == 0 and l == 0 and not getattr(self, "_vaug_init", False):
            self._vaug_init = True
        self.memset("pool", self.vaug.t[:], 0.0, [], self.vaug.all())
        for par in range(2):
            self.S.add("pool", lambda e, par=par: e.tensor_copy(
                out=self.vaug.t[:, :, :, par * 64:(par + 1) * 64].rearrange("p c (hp two) d -> p c hp two d", two=2)[:, :, :, par, :],
                in_=self.vwin.t[:].rearrange("p c (hp two d) -> p c hp two d", two=2, d=64)[:, :, :, par, :]),
                self.vwin.all(), self.vaug.all())
        dma("sp", self.qb_sb.t[:], s_qb[0], s_qb[1], self.qb_sb.all())
        NB = (L + 2047) // 2048
        KB = min(L, 2048)
        CPB = KB // 128
        si = 0
        pend_epi = [None]
        for h in range(4):
            pO = [self.bank(), self.bank()]
            pL = [self.bank(), self.bank()]
            items = [(b, c) for b in range(NB) for c in range(CPB)]
            pend = None
            kv = {}
            nI = len(items)

            def load_block(b):
                ks = self.kst[(si + b) % 2]
                vs = self.vst[(si + b) % 2]
                kkeys = [("kbT", tt_) for tt_ in range(b * KB // 512, (b + 1) * KB // 512)]
                vkeys = [("vb", tt_) for tt_ in range(b * KB // 512, (b + 1) * KB // 512)]
                dma("sp", ks.t[:, 0:KB], self.kbT[h, :, b * KB:(b + 1) * KB], kkeys, ks.k(0, KB * 2))
                dma("sp", vs.t[:, 0:CPB, :], self.vb[h, :, b * CPB:(b + 1) * CPB, :], vkeys, vs.k(0, CPB * 256))
                return ks, vs

            def qk(idx):
                b, c = items[idx]
                if c == 0:
                    kv[b] = load_block(b)
                ks, vs = kv[b]
                out = []
                for m in range(2):
                    psb, psk = self.bank3()
                    self.mm(psb[:], ks.t[m * 64:(m + 1) * 64, c * 128:(c + 1) * 128],
                            self.qb_sb.t[m * 64:(m + 1) * 64, h, :], True, True,
                            ks.k(c * 256, (c + 1) * 256) + self.qb_sb.ck(h), [psk])
                    out.append((psb, psk))
                return out

            self._b3 = 0
            self._b3banks = [self.bank() for _ in range(4)]
            cur = qk(0)
            for idx in range(nI):

                nxt = qk(idx + 1) if idx + 1 < nI else None
                b, c = items[idx]
                ks, vs = kv[b]
                for m in range(2):
                    psb, psk = cur[m]
                    P = self.P[(2 * idx + m) % 4]
                    self.act(P.t[:], psb[:], AF.Exp, [psk], P.all())
                    self.mm(pO[m][0][:], vs.t[:, c, :], P.t[:], idx == 0, idx == nI - 1, vs.k(c * 256, (c + 1) * 256) + P.all(), [pO[m][1]])
                    self.mm(pL[m][0][:], self.ones_b.t[:], P.t[:], idx == 0, idx == nI - 1, self.ones_b.all() + P.all(), [pL[m][1]])
                if idx == 3 and pend_epi[0] is not None:
                    pend_epi[0](cur[0])
                    pend_epi[0] = None
                cur = nxt
            si += NB
            r0, r1, t0, t1 = self.ef[4 * (h % 2):4 * (h % 2) + 4]
            self.S.add("act", lambda e, r0=r0, p=pL[0][0]: e.activation(out=r0.t[:], in_=p[:], func=AF.Ln), [pL[0][1]], r0.all())
            self.S.add("act", lambda e, r1=r1, p=pL[1][0]: e.activation(out=r1.t[:], in_=p[:], func=AF.Ln), [pL[1][1]], r1.all())
            self.cp("dve", t0.t[:], pO[0][0][:], [pO[0][1]], t0.all())
            self.cp("dve", t1.t[:], pO[1][0][:], [pO[1][1]], t1.all())

            def epi_rest(p2bank, h=h, r0=r0, r1=r1, t0=t0, t1=t1):
                self.S.add("act", lambda e: e.activation(out=r0.t[:], in_=r0.t[:], func=AF.Exp, scale=-1.0), r0.all(), r0.all())
                self.S.add("act", lambda e: e.activation(out=r1.t[:], in_=r1.t[:], func=AF.Exp, scale=-1.0), r1.all(), r1.all())
                self.tt("dve", t0.t[:], t0.t[:], r0.t[:], ALU.mult, t0.all() + r0.all(), t0.all())
                self.stt("dve", t1.t[:], t1.t[:], self.gd.t[:, gdo + 5:gdo + 6], r1.t[:], ALU.mult, ALU.mult,
                         t1.all() + self.gd.all() + r1.all(), t1.all())
                self.tt("dve", t0.t[:], t0.t[:], t1.t[:], ALU.add, t0.all() + t1.all(), t0.all())
                sq = self.esq[h % 2]
                self.tt("pool", sq.t[:], t0.t[:], t0.t[:], ALU.mult, t0.all(), sq.all())
                p2, p2k = p2bank
                self.mm(p2[:], self.ones_b.t[:], sq.t[:], True, True, sq.all() + self.ones_b.all(), [p2k])
                self.rsqrt_act(r0.t[:], p2[:], float(128 * SUBLN_EPS), [p2k], r0.all())
                self.stt("dve", self.BT.t[:, h, :], t0.t[:], self.gd.t[:, gdo + 4:gdo + 5], r0.t[:], ALU.mult, ALU.mult,
                         t0.all() + self.gd.all() + r0.all(), self.BT.ck(h))

            if h < 3:
                pend_epi[0] = epi_rest
            else:
                epi_rest(self.bank())
        pi = 0
        for pr in range(4):
            nb = [self.bank() for _ in range(8)]
            po, pok = nb[0]
            pl, plk = nb[1]
            efbs = {}
            for hh in (2 * pr, 2 * pr + 1):
                if edge in ("top", "bot"):
                    efb = self.efull[hh % 2]
                    dma("sp", efb.t[:, 0:1024], self.efull_b[l, hh], [("efullb", l, hh)], efb.all())
                    efbs[hh] = efb
                elif edge is not None:
                    efb = self.efull[hh % 2]
                    kk = 0 if edge == "segtop" else 1
                    dma("sp", efb.t[:], self.segtab_b[kk, hh], [("segtabb", kk, hh)], efb.all())
                    efbs[hh] = efb
            items = [(hh, c) for hh in (2 * pr, 2 * pr + 1) for c in range(8)]
            nI = len(items)
            NLAG = 3

            def na_qk(i):
                hh, c = items[i]
                par = hh % 2
                psb, psk = nb[2 + i % 6]
                self.mm(psb[:], self.kwin.t[par * 64:(par + 1) * 64, pr, c * 128:(c + 1) * 128],
                        self.qa_sb.t[par * 64:(par + 1) * 64, pr, :], True, True,
                        self.kwin.ck(pr) + self.qa_sb.ck(pr), [psk])

            for i in range(min(NLAG, nI)):
                na_qk(i)
            for i in range(nI):
                if i + NLAG < nI:
                    na_qk(i + NLAG)
                hh, c = items[i]
                par = hh % 2
                psb, psk = nb[2 + i % 6]
                efb = efbs.get(hh)
                if True:
                    P = self.P[pi % 4]
                    pi += 1
                    self.act(P.t[:], psb[:], AF.Exp, [psk], P.all())
                    ei = self.eint
                    s0 = (14 - 2 * c) * 64
                    if edge is None:
                        self.tt("dve", P.t[:], P.t[:], ei.t[:, hh, s0:s0 + 512], ALU.mult, P.all() + ei.ck(hh), P.all())
                    elif edge == "segtop":
                        self.tt("dve", P.t[:, 0:256], P.t[:, 0:256], efb.t[:, c * 256:(c + 1) * 256], ALU.mult,
                                P.all() + efb.all(), P.all())
                        self.tt("dve", P.t[:, 256:512], P.t[:, 256:512], ei.t[:, hh, s0 + 256:s0 + 512], ALU.mult,
                                P.all() + ei.ck(hh), P.all())
                    elif edge == "segbot":
                        self.tt("dve", P.t[:, 0:256], P.t[:, 0:256], ei.t[:, hh, s0:s0 + 256], ALU.mult,
                                P.all() + ei.ck(hh), P.all())
                        self.tt("dve", P.t[:, 256:512], P.t[:, 256:512], efb.t[:, c * 256:(c + 1) * 256], ALU.mult,
                                P.all() + efb.all(), P.all())
                    else:
                        valid = 2 <= c <= 5
                        f0 = (11 - 2 * c) * 64
                        if edge == "top":
                            if valid:
                                self.tt("dve", P.t[:, 0:256], P.t[:, 0:256], efb.t[:, f0:f0 + 256], ALU.mult,
                                        P.all() + efb.all(), P.all())
                            else:
                                self.memset("dve", P.t[:, 0:256], 0.0, P.all(), P.all())
                            self.tt("dve", P.t[:, 256:512], P.t[:, 256:512], ei.t[:, hh, s0 + 256:s0 + 512], ALU.mult,
                                    P.all() + ei.ck(hh), P.all())
                        else:
                            self.tt("dve", P.t[:, 0:256], P.t[:, 0:256], ei.t[:, hh, s0:s0 + 256], ALU.mult,
                                    P.all() + ei.ck(hh), P.all())
                            if valid:
                                self.tt("dve", P.t[:, 256:512], P.t[:, 256:512], efb.t[:, f0 + 256:f0 + 512], ALU.mult,
                                        P.all() + efb.all(), P.all())
                            else:
                                self.memset("dve", P.t[:, 256:512], 0.0, P.all(), P.all())
                    self.mm(po[:], self.vaug.t[:, c, hh, :], P.t[:], i == 0, i == nI - 1, self.vaug.all() + P.all(), [pok])
                    self.mm(pl[:], self.half_b[par].t[:], P.t[:], i == 0, i == nI - 1, self.half_b[par].all() + P.all(), [plk])
            rl = self.ef[pr % 2]
            self.recip_act(rl.t[:], pl[:], [plk], rl.all())
            self.tt("dve", self.AT.t[:, pr, :], po[:], rl.t[:], ALU.mult, [pok] + rl.all(), self.AT.ck(pr))
        xT = self.xT
        dma("sp", xT.t[:], s_xm[0], s_xm[1], xT.all())
        dma("sp", self.gA_sb.t[:], s_ga[0], s_ga[1], self.gA_sb.all())
        dma("sp", self.gB_sb.t[:], s_gb[0], s_gb[1], self.gB_sb.all())
        for g in range(4):
            Wa, wak = self.wload(self.wa_b[l][g], 4, 256, ("wab", l, g))
            Wb, wbk = self.wload(self.wb_b[l][g], 4, 256, ("wbb", l, g))
            for cc in range(2):
                n = g * 2 + cc
                pa, pak = self.bank()
                pb, pbk = self.bank()
                for j in range(4):
                    self.mm(pa[:], Wa[:, j, cc * 128:(cc + 1) * 128], self.AT.t[:, j, :], j == 0, j == 3, wak + self.AT.ck(j), [pak])
                for j in range(4):
                    self.mm(pb[:], Wb[:, j, cc * 128:(cc + 1) * 128], self.BT.t[:, j, :], j == 0, j == 3, wbk + self.BT.ck(j), [pbk])
                t1 = self.tmpf[0]
                t2 = self.tmpf[1]
                self.tt("dve", t1.t[:], pa[:], self.gA_sb.t[:, n, :], ALU.mult, [pak] + self.gA_sb.ck(n), t1.all())
                self.tt("dve", t2.t[:], pb[:], self.gB_sb.t[:, n, :], ALU.mult, [pbk] + self.gB_sb.ck(n), t2.all())
                self.tt("pool", self.mT.t[:, n, :], t1.t[:], t2.t[:], ALU.add, t1.all() + t2.all(), self.mT.ck(n))
        for g in range(4):
            W, wk = self.wload(self.wout_b[l][g], 8, 256, ("woutb", l, g))
            for cc in range(2):
                n = g * 2 + cc
                pb, pk = self.bank()
                for kc in range(8):
                    self.mm(pb[:], W[:, kc, cc * 128:(cc + 1) * 128], self.mT.t[:, kc, :], kc == 0, kc == 7, wk + self.mT.ck(kc), [pk])
                self.tt("dve", xT.t[:, n, :], pb[:], xT.t[:, n, :], ALU.add, [pk] + xT.ck(n), xT.ck(n))
        self.rmsnorm(l, 2)
        self.ffn(l, 1)
        if l < self.NL - 1:
            dma("pool", self.x1[:, :, tok].rearrange("c p n -> p c n"), xT.t[:], xT.all(), [("x1", t)])
        else:
            yout = self.dout[oname]
            for b in range(4):
                for hc in range(2):
                    pb, pk = self.bank()
                    for cq in range(4):
                        c = hc * 4 + cq
                        self.S.add("pe", lambda e, pb=pb, b=b, c=c, cq=cq: e.transpose(
                            out=pb[:, cq * 128:(cq + 1) * 128], in_=xT.t[:, c, b * 128:(b + 1) * 128], identity=self.ident.t[:]),
                            xT.ck(c) + self.ident.all(), [pk])
                    self.cp("act" if hc == 0 else "dve", self.xtok.t[:, b, hc * 512:(hc + 1) * 512], pb[:], [pk], self.xtok.ck(b))
                r0 = row0 + t * 512 + b * 128
                dma("pool", yout[r0:r0 + 128, :], self.xtok.t[:, b, :], self.xtok.ck(b), [("y", oname, r0)])

    def bank3(self):
        b = self._b3banks[self._b3 % 4]
        self._b3 += 1
        return b

    def emit(self):
        nc = self.nc
        S = self.S
        with contextlib.ExitStack() as st:
            S.finalize(nc, st)
            block = st.enter_context(nc.Block())

            @block.sync
            def _(e):
                S.run_engine("sp", e, final_wait=True)

            @block.scalar
            def _(e):
                S.run_engine("act", e)

            @block.vector
            def _(e):
                S.run_engine("dve", e)

            @block.gpsimd
            def _(e):
                S.run_engine("pool", e)

            @block.tensor
            def _(e):
                S.run_engine("pe", e)


def host_tables(inputs, n_layers, lmax):
    gc = np.zeros((128, n_layers * 29), np.float32)
    lamv = np.zeros((n_layers * 256,), np.float32)
    tabI = np.zeros((n_layers, 8, 128, 1472), np.float32)
    tabF = np.zeros((n_layers, 8, 128, 1024), np.float32)
    for l in range(n_layers):
        o = l * 29
        gc[:, o + 0:o + 8] = np.asarray(inputs["ffn1_norm"][l], np.float32).reshape(8, 128).T
        gc[:, o + 8:o + 16] = np.asarray(inputs["mix_norm"][l], np.float32).reshape(8, 128).T
        gc[:, o + 16:o + 24] = np.asarray(inputs["ffn2_norm"][l], np.float32).reshape(8, 128).T
        gc[:, o + 24] = np.tile(np.asarray(inputs["qa_norm"][l], np.float32), 2)
        gc[:, o + 25] = np.tile(np.asarray(inputs["ka_norm"][l], np.float32), 2)
        gc[:, o + 26] = np.tile(np.asarray(inputs["qb_norm"][l], np.float32), 2)
        gc[:, o + 27] = np.tile(np.asarray(inputs["kb_norm"][l], np.float32), 2)
        gc[:, o + 28] = np.asarray(inputs["subln"][l], np.float32)
        for i, nm in enumerate(("lam_q1", "lam_k1", "lam_q2", "lam_k2")):
            lamv[l * 256 + i * 64:l * 256 + (i + 1) * 64] = np.asarray(inputs[nm][l], np.float32)
        tabI[l], tabF[l] = _compact_tables(np.asarray(inputs["rpb"][l], np.float32))
    cosF, sinF, rotT = _rope_tables(lmax)
    return dict(gcols=gc, lamv=lamv, tabI=tabI, tabF=tabF, cosF=cosF, sinF=sinF, rotT=rotT,
                ident=np.eye(128, dtype=np.float32))


def seg_tables(tI, tF, is_first, is_last):
    out = np.empty((2, 8, 128, 2048), np.float32)
    for c in range(8):
        a = (14 - 2 * c) * 64
        f = (11 - 2 * c) * 64
        if is_first:
            out[0, :, :, c * 256:(c + 1) * 256] = tF[:, :, f:f + 256] if 2 <= c <= 5 else np.float32(NEG_INF)
        else:
            out[0, :, :, c * 256:(c + 1) * 256] = tI[:, :, a:a + 256]
        if is_last:
            out[1, :, :, c * 256:(c + 1) * 256] = tF[:, :, f + 256:f + 512] if 2 <= c <= 5 else np.float32(NEG_INF)
        else:
            out[1, :, :, c * 256:(c + 1) * 256] = tI[:, :, a + 256:a + 512]
    return out


_PROG_CACHE = {}


def kernel(x_prompt, x_sample, ffn1_norm, ffn1_wi, ffn1_wo, mix_norm, w_in, qa_norm, ka_norm, rpb,
           qb_norm, kb_norm, lam_q1, lam_k1, lam_q2, lam_k2, subln, w_a_out, w_b_out, w_o,
           ffn2_norm, ffn2_wi, ffn2_wo):
    inputs = dict(locals())
    NL = 2
    LP = x_prompt.shape[1]
    LS = x_sample.shape[1]
    NCORES = 8
    per = x_sample.shape[0] // NCORES
    seqs = [("xp", "yp", LP, 0, "seg")] + [("xs", "ys", LS, i * LS, "full") for i in range(per)]
    assert LP // 2048 == NCORES
    key = (LP, LS, per)
    if key not in _PROG_CACHE:
        _PROG_CACHE[key] = Prog(seqs, NL, max(LP, LS))
    prog = _PROG_CACHE[key]
    tabs = host_tables(inputs, NL, max(LP, LS))
    shared = {k: np.ascontiguousarray(np.asarray(inputs[k], np.float32)) for k in
              ("ffn1_wi", "ffn2_wi", "ffn1_wo", "ffn2_wo", "w_in", "w_a_out", "w_b_out", "w_o")}
    shared.update(tabs)
    xp = np.ascontiguousarray(np.asarray(x_prompt, np.float32).reshape(LP, D))
    xs = np.asarray(x_sample, np.float32)
    in_maps = []
    for c in range(NCORES):
        m = dict(shared)
        m["xp"] = xp
        m["xs"] = np.ascontiguousarray(xs[c * per:(c + 1) * per].reshape(per * LS, D))
        ws = np.zeros((128, NCORES), np.float32)
        ws[:, c] = 1.0
        m["wsel"] = ws
        m["segtab"] = seg_tables(tabs["tabI"][NL - 1], tabs["tabF"][NL - 1], c == 0, c == NCORES - 1)
        in_maps.append(m)
    res = run_bass_kernel_spmd(prog.nc, in_maps, core_ids=list(range(NCORES)))
    yp = np.concatenate([np.asarray(res.results[c]["yp"], np.float32) for c in range(NCORES)], axis=0).reshape(1, LP, D)
    ys = np.concatenate([np.asarray(res.results[c]["ys"], np.float32).reshape(per, LS, D) for c in range(NCORES)], axis=0)
    return (yp, ys)
```
